# Optimizing a Trainium2 kernel written in Bass

```python
import math
import jax, jax.numpy as jnp
from jax import lax
import numpy as np

D_MODEL = 2048
BATCH = 4
SEQ = 4096
DEPTH = 1
DEC_BATCH = 8
DEC_SEQ = 64
PAST_LEN = 1024

CHUNK = 64
SSD_EXPAND = 2
D_INNER = SSD_EXPAND * D_MODEL
SSD_HEAD_DIM = 64
SSD_HEADS = D_INNER // SSD_HEAD_DIM
SSD_GROUPS = 8
SSD_HPG = SSD_HEADS // SSD_GROUPS
SSD_STATE = 128
CONV_WIDTH = 4
CONV_DIM = D_INNER + 2 * SSD_GROUPS * SSD_STATE
SSD_CHUNK = CHUNK
ATT_HEADS = 16
ATT_HEAD_DIM = 128
ATT_WIDTH = ATT_HEADS * ATT_HEAD_DIM
PREV_CHUNKS = 8
BAND_ROWS = PREV_CHUNKS * CHUNK
REL_CLIP = 128
D_FF = 4 * D_MODEL
ALPHA = (2 * DEPTH) ** 0.25
BETA = (8 * DEPTH) ** -0.25
LN_EPS = 1e-5
RMS_EPS = 1e-5
NEG_INF = -1e30

OFF_XBC = D_INNER
OFF_DT = OFF_XBC + CONV_DIM
OFF_Q = OFF_DT + SSD_HEADS
OFF_K = OFF_Q + ATT_WIDTH
OFF_V = OFF_K + ATT_WIDTH
OFF_GS = OFF_V + ATT_WIDTH
OFF_GA = OFF_GS + D_MODEL
D_IN_PROJ = OFF_GA + D_MODEL

kernel_name = "hybrid_ssd_chunkband_stream_step"


def layer_norm(x, g, b):
    xf = x.astype(jnp.float32)
    mu = jnp.mean(xf, axis=-1, keepdims=True)
    xc = xf - mu
    var = jnp.mean(xc * xc, axis=-1, keepdims=True)
    return (xc * lax.rsqrt(var + LN_EPS) * g + b).astype(x.dtype)


def grouped_rms_norm(y, w):
    b, L, _ = y.shape
    yg = y.astype(jnp.float32).reshape(b, L, SSD_GROUPS, D_INNER // SSD_GROUPS)
    yg = yg * lax.rsqrt(jnp.mean(yg * yg, axis=-1, keepdims=True) + RMS_EPS)
    return yg.reshape(b, L, D_INNER) * w


def causal_dwconv(u, prev, w, bias):
    L = u.shape[1]
    up = jnp.concatenate([prev.astype(u.dtype), u], axis=1)
    y = bias + sum(up[:, k:k + L] * w[k] for k in range(CONV_WIDTH))
    return jax.nn.silu(y), up[:, -(CONV_WIDTH - 1):]


def ssd_block_step(state, blk, A):
    x, dt, Bm, Cm = blk
    Q = x.shape[1]
    cum = jnp.cumsum(dt * A, axis=1)
    tri = jnp.tril(jnp.ones((Q, Q), dtype=bool))[None, :, :, None, None]
    seg = cum[:, :, None] - cum[:, None, :]
    decay = jnp.exp(jnp.where(tri, seg, -jnp.inf))
    cb = jnp.einsum("bign,bjgn->bijg", Cm, Bm)
    y_diag = jnp.einsum("bijg,bijgh,bjgh,bjghp->bighp", cb, decay, dt, x)
    y_off = jnp.einsum("bign,bghpn,bigh->bighp", Cm, state, jnp.exp(cum))
    to_end = jnp.exp(cum[:, -1:] - cum) * dt
    new_state = state * jnp.exp(cum[:, -1])[..., None, None] + jnp.einsum(
        "bjgn,bjgh,bjghp->bghpn", Bm, to_end, x)
    return new_state, y_diag + y_off


def ssd_scan(x, dt, A, Bm, Cm, state0, q):
    b, L = x.shape[:2]
    nc = L // q

    def blocks(t):
        return jnp.moveaxis(t.reshape((b, nc, q) + t.shape[2:]), 1, 0)

    state, ys = lax.scan(lambda s, blk: ssd_block_step(s, blk, A), state0,
                         (blocks(x), blocks(dt), blocks(Bm), blocks(Cm)))
    y = jnp.moveaxis(ys, 0, 1).reshape(x.shape)
    return y, state


def rel_bias(table, qpos, kpos):
    d = qpos[:, None] - kpos[None, :]
    idx = jnp.clip(d, -REL_CLIP, REL_CLIP) + REL_CLIP
    return table[:, idx]


def attend(q, k, v, bias, valid):
    s = jnp.einsum("bqhd,bkhd->bhqk", q, k).astype(jnp.float32) * (ATT_HEAD_DIM ** -0.5)
    s = s + bias[None].astype(jnp.float32)
    if valid is not None:
        s = jnp.where(valid, s, NEG_INF)
    p = jax.nn.softmax(s, axis=-1).astype(v.dtype)
    return jnp.einsum("bhqk,bkhd->bqhd", p, v)


def chunk_band_attention_prompt(q, k, v, table):
    b, L = q.shape[:2]
    nc = L // CHUNK
    band = BAND_ROWS + CHUNK
    pad = ((0, 0), (BAND_ROWS, 0), (0, 0), (0, 0))
    kp = jnp.pad(k, pad)
    vp = jnp.pad(v, pad)
    kpos = jnp.arange(band) - BAND_ROWS
    bias = rel_bias(table, jnp.arange(CHUNK), kpos)

    def one_chunk(c):
        start = c * CHUNK
        qc = lax.dynamic_slice_in_dim(q, start, CHUNK, axis=1)
        kc = lax.dynamic_slice_in_dim(kp, start, band, axis=1)
        vc = lax.dynamic_slice_in_dim(vp, start, band, axis=1)
        valid = (kpos + start >= 0)[None, None, None, :]
        return attend(qc, kc, vc, bias, valid)

    out = lax.map(one_chunk, jnp.arange(nc))
    return jnp.moveaxis(out, 0, 1).reshape(q.shape)


def chunk_band_attention_step(q, k, v, k_cache, v_cache, table):
    L = q.shape[1]
    R = k_cache.shape[1]
    keys = jnp.concatenate([k_cache.astype(k.dtype), k], axis=1)
    vals = jnp.concatenate([v_cache.astype(v.dtype), v], axis=1)
    kpos = jnp.concatenate([jnp.arange(R) - R, jnp.arange(L)])
    bias = rel_bias(table, jnp.arange(L), kpos)
    return attend(q, keys, vals, bias, None)


def hybrid_layer(x, k_cache, v_cache, conv_prev, ssd_prev, w_in, conv_w, conv_b, dt_bias,
                 a_log, d_skip, ssd_norm_w, rel_table, w_ssd_out, w_att_out, w_out,
                 ln1_g, ln1_b, w_up, w_down, ln2_g, ln2_b):
    b, L, _ = x.shape
    prompt = k_cache is None
    proj = x @ w_in
    z = proj[..., :OFF_XBC]
    xbc = proj[..., OFF_XBC:OFF_DT]
    dt_raw = proj[..., OFF_DT:OFF_Q]
    q = proj[..., OFF_Q:OFF_K].reshape(b, L, ATT_HEADS, ATT_HEAD_DIM)
    k = proj[..., OFF_K:OFF_V].reshape(b, L, ATT_HEADS, ATT_HEAD_DIM)
    v = proj[..., OFF_V:OFF_GS].reshape(b, L, ATT_HEADS, ATT_HEAD_DIM)
    g_ssd = jax.nn.sigmoid(proj[..., OFF_GS:OFF_GA])
    g_att = jax.nn.sigmoid(proj[..., OFF_GA:])

    if prompt:
        conv_prev = jnp.zeros((b, CONV_WIDTH - 1, CONV_DIM), x.dtype)
        ssd_prev = jnp.zeros((b, SSD_HEADS, SSD_HEAD_DIM, SSD_STATE), jnp.float32)
    xbc, conv_new = causal_dwconv(xbc, conv_prev, conv_w, conv_b)
    gn = SSD_GROUPS * SSD_STATE
    xs = xbc[..., :D_INNER].reshape(b, L, SSD_GROUPS, SSD_HPG, SSD_HEAD_DIM)
    Bm = xbc[..., D_INNER:D_INNER + gn].reshape(b, L, SSD_GROUPS, SSD_STATE)
    Cm = xbc[..., D_INNER + gn:].reshape(b, L, SSD_GROUPS, SSD_STATE)
    dt = jax.nn.softplus(dt_raw.astype(jnp.float32) + dt_bias).reshape(b, L, SSD_GROUPS, SSD_HPG)
    A = -jnp.exp(a_log.astype(jnp.float32)).reshape(SSD_GROUPS, SSD_HPG)
    state0 = ssd_prev.astype(jnp.float32).reshape(b, SSD_GROUPS, SSD_HPG, SSD_HEAD_DIM, SSD_STATE)
    y, ssd_new = ssd_scan(xs, dt, A, Bm, Cm, state0, SSD_CHUNK if prompt else L)
    y = y + d_skip.reshape(SSD_GROUPS, SSD_HPG)[..., None] * xs
    y = y.reshape(b, L, D_INNER) * jax.nn.silu(z)
    y_ssd = grouped_rms_norm(y, ssd_norm_w).astype(x.dtype)
    ssd_new = ssd_new.reshape(b, SSD_HEADS, SSD_HEAD_DIM, SSD_STATE)

    if prompt:
        y_att = chunk_band_attention_prompt(q, k, v, rel_table)
        rows = min(BAND_ROWS, L)
        k_new, v_new = k[:, L - rows:], v[:, L - rows:]
    else:
        y_att = chunk_band_attention_step(q, k, v, k_cache, v_cache, rel_table)
        k_new, v_new = k, v

    merged = g_ssd * (y_ssd @ w_ssd_out) + g_att * (y_att.reshape(b, L, ATT_WIDTH) @ w_att_out)
    h = layer_norm(ALPHA * x + merged @ w_out, ln1_g, ln1_b)
    f = jnp.square(jax.nn.relu(h @ w_up)) @ w_down
    out = layer_norm(ALPHA * h + f, ln2_g, ln2_b)
    return out, k_new, v_new, conv_new, ssd_new


def setup_inputs(seed: int = 0) -> dict:
    key = jax.random.key(seed)
    ks = jax.random.split(key, 24)
    f32 = jnp.float32
    R = min(BAND_ROWS, PAST_LEN)

    def nrm(k, shape, scale):
        return jax.random.normal(k, shape, f32) * scale

    dt0 = jnp.exp(jax.random.uniform(ks[10], (DEPTH, SSD_HEADS), f32, math.log(1e-3), math.log(1e-1)))
    return {
        "x_prompt": nrm(ks[0], (BATCH, SEQ, D_MODEL), 1.0),
        "x_sample": nrm(ks[1], (DEC_BATCH, DEC_SEQ, D_MODEL), 1.0),
        "cache_k": nrm(ks[2], (DEPTH, DEC_BATCH, R, ATT_HEADS, ATT_HEAD_DIM), 1.0),
        "cache_v": nrm(ks[3], (DEPTH, DEC_BATCH, R, ATT_HEADS, ATT_HEAD_DIM), 1.0),
        "state_conv": nrm(ks[4], (DEPTH, DEC_BATCH, CONV_WIDTH - 1, CONV_DIM), 1.0),
        "state_ssd": nrm(ks[5], (DEPTH, DEC_BATCH, SSD_HEADS, SSD_HEAD_DIM, SSD_STATE), 0.1),
        "w_in": nrm(ks[6], (DEPTH, D_MODEL, D_IN_PROJ), D_MODEL ** -0.5),
        "conv_w": nrm(ks[7], (DEPTH, CONV_WIDTH, CONV_DIM), CONV_WIDTH ** -0.5),
        "conv_b": nrm(ks[8], (DEPTH, CONV_DIM), 0.02),
        "dt_bias": dt0 + jnp.log(-jnp.expm1(-dt0)),
        "a_log": jnp.log(jax.random.uniform(ks[11], (DEPTH, SSD_HEADS), f32, 1.0, 16.0)),
        "d_skip": 1.0 + nrm(ks[12], (DEPTH, SSD_HEADS), 0.1),
        "ssd_norm_w": 1.0 + nrm(ks[13], (DEPTH, D_INNER), 0.02),
        "rel_table": nrm(ks[14], (DEPTH, ATT_HEADS, 2 * REL_CLIP + 1), 0.1),
        "w_ssd_out": nrm(ks[15], (DEPTH, D_INNER, D_MODEL), D_INNER ** -0.5),
        "w_att_out": nrm(ks[16], (DEPTH, ATT_WIDTH, D_MODEL), ATT_WIDTH ** -0.5),
        "w_out": nrm(ks[17], (DEPTH, D_MODEL, D_MODEL), BETA * D_MODEL ** -0.5),
        "ln1_g": 1.0 + nrm(ks[18], (DEPTH, D_MODEL), 0.02),
        "ln1_b": nrm(ks[19], (DEPTH, D_MODEL), 0.02),
        "w_up": nrm(ks[20], (DEPTH, D_MODEL, D_FF), D_MODEL ** -0.5),
        "w_down": nrm(ks[21], (DEPTH, D_FF, D_MODEL), BETA * D_FF ** -0.5),
        "ln2_g": 1.0 + nrm(ks[22], (DEPTH, D_MODEL), 0.02),
        "ln2_b": nrm(ks[23], (DEPTH, D_MODEL), 0.02),
    }


def reference(x_prompt, x_sample, cache_k, cache_v, state_conv, state_ssd, w_in, conv_w, conv_b,
              dt_bias, a_log, d_skip, ssd_norm_w, rel_table, w_ssd_out, w_att_out, w_out,
              ln1_g, ln1_b, w_up, w_down, ln2_g, ln2_b):
    hp, hs = x_prompt, x_sample
    kp_l, vp_l, cp_l, sp_l = [], [], [], []
    ks_l, vs_l, cs_l, ss_l = [], [], [], []
    for l in range(DEPTH):
        params = (w_in[l], conv_w[l], conv_b[l], dt_bias[l], a_log[l], d_skip[l], ssd_norm_w[l],
                  rel_table[l], w_ssd_out[l], w_att_out[l], w_out[l], ln1_g[l], ln1_b[l],
                  w_up[l], w_down[l], ln2_g[l], ln2_b[l])
        hp, kp_, vp_, cp_, sp_ = hybrid_layer(hp, None, None, None, None, *params)
        hs, ks_, vs_, cs_, ss_ = hybrid_layer(hs, cache_k[l], cache_v[l], state_conv[l],
                                              state_ssd[l], *params)
        kp_l.append(kp_); vp_l.append(vp_); cp_l.append(cp_); sp_l.append(sp_)
        ks_l.append(ks_); vs_l.append(vs_); cs_l.append(cs_); ss_l.append(ss_)
    return (hp, hs,
            jnp.stack(kp_l), jnp.stack(vp_l), jnp.stack(cp_l), jnp.stack(sp_l),
            jnp.stack(ks_l), jnp.stack(vs_l), jnp.stack(cs_l), jnp.stack(ss_l))
```

```python
import numpy as np
import concourse.bass as bass
import concourse.mybir as mybir
from concourse.bass_utils import run_bass_kernel_spmd
from concourse.ap import AP

F32 = mybir.dt.float32
BF16 = mybir.dt.bfloat16
U8 = mybir.dt.uint8
AF = mybir.ActivationFunctionType
ALU = mybir.AluOpType

D = 2048
SEQ = 4096
BATCH = 4
DEC_BATCH = 8
DEC_SEQ = 64
D_INNER = 4096
NH = 64
NG = 8
NST = 128
CONV_DIM = 6144
AH = 16
D_FF = 8192
OFF_XBC = 4096
OFF_B = OFF_XBC + 4096
OFF_C = OFF_B + 1024
OFF_DT = OFF_XBC + CONV_DIM
OFF_Q = OFF_DT + NH
OFF_K = OFF_Q + 2048
OFF_V = OFF_K + 2048
OFF_GS = OFF_V + 2048
OFF_GA = OFF_GS + 2048
D_IN_PROJ = OFF_GA + 2048
ALPHA = (2 * 1) ** 0.25
LN_EPS = 1e-5
RMS_EPS = 1e-5
QSCALE = 128 ** -0.5


class Buf:
    __slots__ = ("t", "name", "lw", "rd", "sem", "cnt", "psum")

    def __init__(self, t, name):
        self.t = t
        self.name = name
        self.psum = False
        self.lw = None
        self.rd = {}
        self.sem = None
        self.cnt = 0

    def __getitem__(self, k):
        return self.t[k]


class Em:
    ENG = ("pe", "act", "dve", "pool", "sp")

    def __init__(self, nc):
        self.nc = nc
        self.q = {e: [] for e in self.ENG}
        self.sem = {e: nc.alloc_semaphore("s_" + e) for e in self.ENG}
        self.cnt = {e: 0 for e in self.ENG}
        self.pend = {e: False for e in self.ENG}
        self.seen = {e: {} for e in self.ENG}
        self.bufs = []
        self.nins = 0

    def sb(self, name, shape, dt):
        b = Buf(self.nc.alloc_sbuf_tensor(name, list(shape), dt), name)
        self.bufs.append(b)
        return b

    def ps(self, name, shape, dt=F32):
        b = Buf(self.nc.alloc_psum_tensor(name, list(shape), dt), name)
        b.psum = True
        self.bufs.append(b)
        return b

    def dram(self, name, shape, dt, kind="Internal"):
        b = Buf(self.nc.dram_tensor(name, list(shape), dt, kind=kind), name)
        self.bufs.append(b)
        return b

    def wrap(self, t, name):
        b = Buf(t, name)
        self.bufs.append(b)
        return b

    def _need(self, eng, tok, waits):
        if tok is None:
            return
        sem, val, _ = tok
        if self.seen[eng].get(sem.num, 0) >= val:
            return
        if sem.num in waits:
            val = max(val, waits[sem.num][1])
        waits[sem.num] = (sem, val)

    def _flush(self, eng, waits):
        out = []
        for num, (sem, val) in waits.items():
            self.seen[eng][num] = val
            out.append((sem, val))
        return out

    def _deps(self, eng, reads, writes):
        waits = {}
        for b in reads:
            self._need(eng, b.lw, waits)
            if b.psum:
                for e2, tok in b.rd.items():
                    if e2 != eng:
                        self._need(eng, tok, waits)
        for b in writes:
            if b.lw is not None and not (eng == "pe" and b.lw[2] == "pe"):
                self._need(eng, b.lw, waits)
            for e2, tok in b.rd.items():
                if not (eng == "pe" and e2 == "pe"):
                    self._need(eng, tok, waits)
        return self._flush(eng, waits)

    def op(self, eng, fn, reads=(), writes=(), sig=True):
        waits = self._deps(eng, reads, writes)
        if sig:
            self.cnt[eng] += 1
            tick = self.cnt[eng]
            self.pend[eng] = False
        else:
            tick = self.cnt[eng] + 1
            self.pend[eng] = True
        tok = (self.sem[eng], tick, eng)
        self.q[eng].append((waits, fn, (self.sem[eng], 1) if sig else None))
        self.nins += 1
        for b in reads:
            b.rd[eng] = tok
        for b in writes:
            b.lw = tok
            b.rd = {}
        return tok

    def dma(self, qeng, out_ap, in_ap, reads=(), writes=(), **kw):
        waits = self._deps(qeng, reads, writes)
        owner = writes[0] if writes else reads[0]
        if owner.sem is None:
            owner.sem = self.nc.alloc_semaphore("d%d_%s" % (len(self.bufs), owner.name) + "_%d" % id(owner))
        owner.cnt += 16
        tok = (owner.sem, owner.cnt, "dma")

        def fn(e, out_ap=out_ap, in_ap=in_ap, kw=kw):
            return e.dma_start(out=out_ap, in_=in_ap, **kw)
        self.q[qeng].append((waits, fn, (owner.sem, 16)))
        self.nins += 1
        for b in reads:
            b.rd[("dma", tok[0].num)] = tok
        for b in writes:
            b.lw = tok
            b.rd = {}
        return tok

    def wait_all(self, eng, toks):
        waits = {}
        for t in toks:
            self._need(eng, t, waits)
        self.q[eng].append((self._flush(eng, waits), None, None))

    def barrier(self, engines=("pe", "act", "dve", "sp")):
        toks = []
        for e in self.ENG:
            assert not self.pend[e], "barrier with pending unsignalled op on " + e
            if self.cnt[e] > 0:
                toks.append((self.sem[e], self.cnt[e], e))
        for b in self.bufs:
            if b.sem is not None and b.cnt > 0:
                toks.append((b.sem, b.cnt, "dma"))
        for e in engines:
            self.wait_all(e, toks)

    def emit(self):
        nc = self.nc
        q = self.q

        def run(e, lst):
            for waits, fn, inc in lst:
                for sem, val in waits:
                    e.wait_ge(sem, val)
                if fn is not None:
                    ins = fn(e)
                    if inc is not None:
                        ins.then_inc(inc[0], inc[1])

        with nc.Block() as block:
            @block.tensor
            def _(e):
                run(e, q["pe"])

            @block.scalar
            def _(e):
                run(e, q["act"])

            @block.vector
            def _(e):
                run(e, q["dve"])

            @block.gpsimd
            def _(e):
                run(e, q["pool"])

            @block.sync
            def _(e):
                run(e, q["sp"])


def build_program(NPRE, NMAIN, TB, RING=None, DEBUG=False, STOP=None):
    if RING is None:
        RING = 2 if TB >= 4 else 3
    TBM = TB
    T = 128 * TB
    TM = T
    assert NPRE % T == 0 and NMAIN % T == 0 and NMAIN >= 512 and NPRE >= 512
    nc = bass.Bass("TRN2", target_bir_lowering=False)
    em = Em(nc)

    def din(name, shape):
        return em.wrap(nc.dram_tensor(name, list(shape), F32, kind="ExternalInput"), name)

    def dout(name, shape):
        return em.wrap(nc.dram_tensor(name, list(shape), F32, kind="ExternalOutput"), name)

    xpre = din("xpre", [NPRE, D])
    xmain = din("xmain", [NMAIN, D])
    xsamp = din("xsamp", [128, D])
    flag_d = din("flag", [128, 1])
    cache_k = din("cache_k", [512, D])
    cache_v = din("cache_v", [512, D])
    state_conv = din("state_conv", [3, CONV_DIM])
    state_ssd = din("state_ssd", [D_INNER, NST])
    w_in = din("w_in", [D, D_IN_PROJ])
    conv_w = din("conv_w", [4, CONV_DIM])
    conv_b = din("conv_b", [1, CONV_DIM])
    dt_bias = din("dt_bias", [NH])
    a_log = din("a_log", [NH])
    d_skip = din("d_skip", [NH])
    ssd_norm_w = din("ssd_norm_w", [32, 128])
    rel_table = din("rel_table", [AH, 257])
    w_ssd_out = din("w_ssd_out", [D_INNER, D])
    w_att_out = din("w_att_out", [D, D])
    w_out = din("w_out", [D, D])
    ln1_g = din("ln1_g", [D])
    ln1_b = din("ln1_b", [D])
    w_up = din("w_up", [D, D_FF])
    w_down = din("w_down", [D_FF, D])
    ln2_g = din("ln2_g", [D])
    ln2_b = din("ln2_b", [D])

    y_main = dout("y_main", [NMAIN, D])
    y_samp = dout("y_samp", [128, D])
    k_main = dout("k_main", [512, D])
    v_main = dout("v_main", [512, D])
    conv_main = dout("conv_main", [3, CONV_DIM])
    ssd_main = dout("ssd_main", [D_INNER, NST])
    k_samp = dout("k_samp", [128, D])
    v_samp = dout("v_samp", [128, D])
    conv_samp = dout("conv_samp", [3, CONV_DIM])
    ssd_samp = dout("ssd_samp", [D_INNER, NST])
    ext_d = em.dram("ext_d", [AH, 512], F32)
    USE_SCRATCH = False
    WB = {}
    if USE_SCRATCH:
        WB = {
            "w_in": em.dram("w_in_b", [D, D_IN_PROJ], BF16),
            "w_ssd_out": em.dram("w_ssd_out_b", [D_INNER, D], BF16),
            "w_att_out": em.dram("w_att_out_b", [D, D], BF16),
            "w_out": em.dram("w_out_b", [D, D], BF16),
            "w_up": em.dram("w_up_b", [D, D_FF], BF16),
            "w_down": em.dram("w_down_b", [D_FF, D], BF16),
        }
    WSRC = {"w_in": w_in, "w_ssd_out": w_ssd_out, "w_att_out": w_att_out, "w_out": w_out, "w_up": w_up, "w_down": w_down}

    def convert_weight(name, nchunks):
        src = WSRC[name]
        dst = WB[name]
        rows = src.t.shape[0]
        step = rows // nchunks
        for i in range(nchunks):
            em.dma("pool", dst[i * step:(i + 1) * step, :], src[i * step:(i + 1) * step, :], reads=[src], writes=[dst])
    if DEBUG:
        dbg = {
            "yatt": em.wrap(nc.dram_tensor("dbg_yatt", [128, 16, 128 * TB], BF16, kind="ExternalOutput"), "dbg_yatt"),
            "yssd": em.wrap(nc.dram_tensor("dbg_yssd", [128, 32, 128 * TB], BF16, kind="ExternalOutput"), "dbg_yssd"),
            "merged": em.wrap(nc.dram_tensor("dbg_merged", [128, 16, 128 * TB], BF16, kind="ExternalOutput"), "dbg_merged"),
            "h": em.wrap(nc.dram_tensor("dbg_h", [128, TB, D], F32, kind="ExternalOutput"), "dbg_h"),
            "xT": em.wrap(nc.dram_tensor("dbg_xT", [128, 16, 128 * TB], BF16, kind="ExternalOutput"), "dbg_xT"),
        }

    def dump(key, buf, ti):
        if DEBUG and ti == 0:
            if len(buf.t.shape) == 3:
                em.dma("sp", dbg[key][:, :, :], buf[:, :, :], reads=[buf], writes=[dbg[key]])

    ident_bf = em.sb("ident_bf", [128, 128], BF16)
    ident_f = em.sb("ident_f", [128, 128], F32)
    ones_bf = em.sb("ones_bf", [128, 128], BF16)
    triU = em.sb("triU", [128, 128], F32)
    triLs = em.sb("triLs", [128, 128], F32)
    convp = em.sb("convp", [128, 5, 48], F32)
    normw = em.sb("normw", [128, 32], F32)
    dtb = em.sb("dtb", [128, NH], F32)
    Aneg = em.sb("Aneg", [128, NH], F32)
    dsk = em.sb("dsk", [128, NH], F32)
    EBc = em.sb("EBc", [128, AH], F32)
    wdt = em.sb("wdt", [128, 16, NH], BF16)
    flag = em.sb("flag_sb", [128, 1], F32)
    negbig = em.sb("negbig", [128, 1], F32)
    zero1 = em.sb("zero1", [128, 1], F32)
    epsc = em.sb("epsc", [128, 2], F32)
    ctail = em.sb("ctail", [128, 3, 48], F32)
    S = em.sb("S", [128, D_INNER], F32)
    KTh = em.sb("KTh", [128, AH, 512], BF16)
    Vh = em.sb("Vh", [128, 4, D], BF16)
    xtok = em.sb("xtok", [128, D], BF16)
    ring = [em.sb("ring%d" % i, [128, 16, 512], BF16) for i in range(RING)]
    ring_i = [0]

    PF = [em.ps("pf%d" % i, [128, 512], F32) for i in range(6)]
    PB = [em.ps("pb%d" % i, [128, 1024], BF16) for i in range(2)]
    pb_i = [0]

    A0 = 0
    A1 = 32 * T
    A2 = 64 * T
    A3 = 128 * T
    ARENA = A3 + (45 * 1024 + 3680 if TB >= 4 else 64 * 1024)
    arena = nc.alloc_sbuf_tensor("arena", [128, ARENA], U8)

    VIEWS = {}
    em.views = VIEWS

    def view(name, off, shape, dt):
        esz = 4 if dt == F32 else 2
        n = 1
        for s in shape[1:]:
            n *= s
        nb = n * esz
        assert off % 4 == 0 and off + nb <= ARENA, (name, off, nb, ARENA)
        ap = arena[:, off:off + nb].bitcast(dt)
        if len(shape) == 3:
            ap = ap.rearrange("p (a b) -> p a b", a=shape[1])
        elif len(shape) == 4:
            ap = ap.rearrange("p (a b c) -> p a b c", a=shape[1], b=shape[2])
        VIEWS[name] = (off, tuple(shape), "f32" if dt == F32 else "bf16")
        return em.wrap(ap, name), off + ((nb + 31) // 32) * 32

    xT, _ = view("xT", A0, [128, 16, T], BF16)
    y_attT, _ = view("y_attT", A1, [128, 16, T], BF16)
    y_ssdT, _ = view("y_ssdT", A2, [128, 32, T], BF16)

    o = A2
    EB0, o = view("EB0", o, [128, AH, 128], F32)
    EB3, o = view("EB3", o, [128, AH, 128], F32)
    EB4, o = view("EB4", o, [128, AH, 128], F32)
    QT, o = view("QT", o, [128, 4, T], BF16)
    KTc, o = view("KTc", o, [128, 4, T], BF16)
    Vc, o = view("Vc", o, [128, TB, 512], BF16)
    Pexp = []
    PTb = []
    for i in range(2):
        b_, o = view("Pexp%d" % i, o, [128, 4, 128], F32)
        Pexp.append(b_)
    for i in range(2):
        b_, o = view("PT%d" % i, o, [128, 5, 4, 128], BF16)
        PTb.append(b_)
    rden, o = view("rden", o, [128, 512], F32)
    kvst = [em.sb("kvst%d" % i, [128, 512], F32) for i in range(2)]
    ATT_END = o
    o = A3
    cst, o = view("cst", o, [128, T + 4], F32)
    cacc, o = view("cacc", o, [128, T], F32)
    if TB >= 4:
        ARENA_EXTRA = 0
    cst2, o = view("cst2", o, [128, T + 4], F32)
    cacc2, o = view("cacc2", o, [128, T], F32)
    xbcT = []
    for i in range(6):
        b_, o = view("xbcT%d" % i, o, [128, T], BF16)
        xbcT.append(b_)
    xs_tok, o = view("xs_tok", o, [128, TB, 512], BF16)
    B_tok, o = view("B_tok", o, [128, TB, 128], BF16)
    dt_t, o = view("dt_t", o, [128, TB, NH], F32)
    dt_a, o = view("dt_a", o, [128, TB, NH], F32)
    dt_v, o = view("dt_v", o, [128, TB, NH], F32)
    dtA, o = view("dtA", o, [128, TB, NH], F32)
    exb, o = view("exb", o, [128, TB, 192], F32)
    wd2, o = view("wd2", o, [128, TB, 2, NH], F32)
    Xg, o = view("Xg", o, [128, 8, 128], F32)
    Dm, o = view("Dm", o, [128, 8, 128], F32)
    cbm, o = view("cbm", o, [128, 128], F32)
    MT, o = view("MT", o, [128, 8, 128], BF16)
    xwd, o = view("xwd", o, [128, 2, 8, 64], BF16)
    Sbf, o = view("Sbf", o, [128, 512], BF16)
    y1, o = view("y1", o, [128, 512], F32)
    y2, o = view("y2", o, [128, 512], F32)
    y3, o = view("y3", o, [128, 512], F32)
    ynb, o = view("ynb", o, [128, 512], BF16)
    st8, o = view("st8", o, [128, 8], F32)
    SSD_END = o
    o = A3
    mergedT, o = view("mergedT", o, [128, 16, T], BF16)
    MERGED_END = o
    sgs, o = view("sgs", o, [128, T], F32)
    sga, o = view("sga", o, [128, T], F32)
    m1, o = view("m1", o, [128, 4, T], F32)
    m2, o = view("m2", o, [128, 4, T], F32)
    GATE_END = o
    hT, _ = view("hT", A0, [128, 16, T], BF16)
    hbuf, _ = view("hbuf", A2, [128, TB, D], F32)
    o = MERGED_END
    xres, o = view("xres", o, [128, D], F32)
    hbf, o = view("hbf", o, [128, D], BF16)
    bst, o = view("bst", o, [128, 4, 6], F32)
    mv, o = view("mv", o, [128, 8], F32)
    if TB >= 4:
        lng, _ = view("lng", A1, [128, D], F32)
        lnb, _ = view("lnb", A1 + 8192, [128, D], F32)
        LN_END = o
    else:
        lng, o = view("lng", o, [128, D], F32)
        lnb, o = view("lnb", o, [128, D], F32)
        LN_END = o
    NFH = 32
    o = LN_END if TB < 4 else A3
    uT, o = view("uT", o, [128, NFH, T], BF16)
    rtmp, o = view("rtmp", o, [128, T], F32)
    if TB >= 4:
        bst2, o = view("bst2", o, [128, 4, 6], F32)
        mv2, o = view("mv2", o, [128, 8], F32)
    else:
        bst2, mv2 = bst, mv
    MLP_END = o
    assert max(ATT_END, SSD_END, GATE_END, LN_END, MLP_END) <= ARENA, (ATT_END, SSD_END, GATE_END, LN_END, MLP_END, ARENA)

    cfg = {"TB": TBM, "T": TM, "LT": TM}
    em.sbuf_left = nc.sbuf_bytes_remaining

    cst_a, cacc_a = cst, cacc

    def next_ring():
        r = ring[ring_i[0] % RING]
        ring_i[0] += 1
        return r

    def next_pb():
        p = PB[pb_i[0] % 2]
        pb_i[0] += 1
        return p

    def wslab(wd, r0, c0, ncols, dst=None, dcol=0, nk=16):
        if dst is None:
            dst = next_ring()
        if USE_SCRATCH:
            wd = WB[wd.name]
        em.dma("pool", dst[:, 0:nk, dcol:dcol + ncols],
               wd[r0:r0 + nk * 128, c0:c0 + ncols].rearrange("(k p) c -> p k c", p=128),
               reads=[wd], writes=[dst])
        return dst

    def mm_group(out_ap, out_buf, pairs, reads):
        n = len(pairs)
        for i, (l, r) in enumerate(pairs):
            em.op("pe", lambda e, l=l, r=r, i=i: e.matmul(out_ap, lhsT=l, rhs=r, start=(i == 0), stop=(i == n - 1)),
                  reads=reads if i == 0 else (), writes=[out_buf], sig=(i == n - 1))

    def tr(out_ap, out_buf, in_ap, in_buf, idn):
        em.op("pe", lambda e: e.transpose(out_ap, in_ap, idn), reads=[in_buf, ident_bf, ident_f], writes=[out_buf])

    def act(out_ap, in_ap, func, reads, writes, **kw):
        em.op("act", lambda e: e.activation(out_ap, in_ap, func, **kw), reads=reads, writes=writes)

    def tt(eng, out_ap, a, b, op, reads, writes):
        em.op(eng, lambda e: e.tensor_tensor(out=out_ap, in0=a, in1=b, op=op), reads=reads, writes=writes)

    def ts(eng, out_ap, a, s1, s2, op0, op1, reads, writes):
        if op1 is None:
            em.op(eng, lambda e: e.tensor_scalar(out=out_ap, in0=a, scalar1=s1, scalar2=None, op0=op0), reads=reads, writes=writes)
        else:
            em.op(eng, lambda e: e.tensor_scalar(out=out_ap, in0=a, scalar1=s1, scalar2=s2, op0=op0, op1=op1), reads=reads, writes=writes)

    def stt(eng, out_ap, a, s, b, op0, op1, reads, writes):
        em.op(eng, lambda e: e.scalar_tensor_tensor(out=out_ap, in0=a, scalar=s, in1=b, op0=op0, op1=op1), reads=reads, writes=writes)

    def cp(eng, out_ap, in_ap, reads, writes):
        if eng == "act":
            act(out_ap, in_ap, AF.Copy, reads, writes)
        else:
            em.op(eng, lambda e: e.tensor_copy(out_ap, in_ap), reads=reads, writes=writes)

    def mset(eng, ap, val, buf):
        em.op(eng, lambda e: e.memset(ap, val), writes=[buf])

    def bc3(ap2, n):
        return ap2.unsqueeze(2).to_broadcast([ap2.shape[0], ap2.shape[1], n])

    def bcm(ap2, n):
        return ap2.unsqueeze(1).to_broadcast([ap2.shape[0], n, ap2.shape[1]])

    def setup():
        for t_, v in ((ident_bf, 1.0), (ident_f, 1.0), (triU, 1.0), (triLs, 1.0)):
            mset("pool", t_[:, :], v, t_)
        mset("pool", ones_bf[:, :], 1.0, ones_bf)
        mset("pool", zero1[:, :], 0.0, zero1)
        mset("pool", epsc[:, 0:1], RMS_EPS, epsc)
        mset("pool", epsc[:, 1:2], LN_EPS, epsc)
        sel = lambda t_, pat, op, base, cm: em.op(
            "pool", lambda e: e.affine_select(t_[:, :], t_[:, :], pattern=pat, compare_op=op, fill=0.0, base=base, channel_multiplier=cm),
            reads=[t_], writes=[t_])
        sel(ident_bf, [[-1, 128]], ALU.is_equal, 0, 1)
        sel(ident_f, [[-1, 128]], ALU.is_equal, 0, 1)
        sel(triU, [[1, 128]], ALU.is_ge, 0, -1)
        sel(triLs, [[-1, 128]], ALU.is_gt, 0, 1)
        mset("pool", Jm[:, :], 1.0, Jm)
        sel(Jm, [[1, 128]], ALU.is_equal, -127, 1)
        em.dma("sp", flag[:, :], flag_d[:, :], reads=[flag_d], writes=[flag])
        ts("dve", negbig[:, :], flag[:, :], -1.0, 1e30, ALU.add, ALU.mult, [flag], [negbig])
        em.dma("sp", dtb[:, :], dt_bias.t.ap().partition_broadcast(128), reads=[dt_bias], writes=[dtb])
        em.dma("sp", Aneg[:, :], a_log.t.ap().partition_broadcast(128), reads=[a_log], writes=[Aneg])
        em.dma("sp", dsk[:, :], d_skip.t.ap().partition_broadcast(128), reads=[d_skip], writes=[dsk])
        act(Aneg[:, :], Aneg[:, :], AF.Exp, [Aneg], [Aneg])
        ts("dve", Aneg[:, :], Aneg[:, :], -1.0, None, ALU.mult, None, [Aneg], [Aneg])
        wslab(w_in, 0, OFF_DT, NH, dst=wdt)
        stage, _ = view("stage", A3, [128, 8, 128], F32)
        for k in range(4):
            em.dma("sp", stage[0:48, k, :], conv_w[k, :].rearrange("(b p) -> b p", p=128), reads=[conv_w], writes=[stage])
        em.dma("sp", stage[0:48, 4, :], conv_b[0, :].rearrange("(b p) -> b p", p=128), reads=[conv_b], writes=[stage])
        em.dma("sp", stage[0:32, 5, :], ssd_norm_w[:, :], reads=[ssd_norm_w], writes=[stage])
        for k in range(5):
            tr(PF[0][:, k * 48:(k + 1) * 48], PF[0], stage[0:48, k, :], stage, ident_f[0:48, 0:48])
        cp("dve", convp[:, :, :], PF[0][:, 0:240].rearrange("p (a b) -> p a b", a=5), [PF[0]], [convp])
        tr(PF[1][:, 0:32], PF[1], stage[0:32, 5, :], stage, ident_f[0:32, 0:32])
        cp("dve", normw[:, :], PF[1][:, 0:32], [PF[1]], [normw])
        tabsb, _ = view("tabsb", A3 + 8192, [AH, 512], F32)
        em.dma("sp", tabsb[0:AH, 0:257], rel_table[:, :], reads=[rel_table], writes=[tabsb])
        cp("dve", tabsb[0:AH, 257:512], tabsb[0:AH, 256:257].to_broadcast([AH, 255]), [tabsb], [tabsb])
        em.dma("sp", ext_d[:, :], tabsb[0:AH, :], reads=[tabsb], writes=[ext_d])
        em.dma("sp", EBc[:, :], rel_table[:, 256].partition_broadcast(128), reads=[rel_table], writes=[EBc], allow_slow_non_contiguous=True)
        act(EBc[:, :], EBc[:, :], AF.Exp, [EBc], [EBc])

    Jm = em.sb("Jm", [128, 128], F32)
    toe, _o2 = view("toe", max(ATT_END, A3), [128, AH, 128], F32)

    def build_EB():
        for EB, c in ((EB4, 1), (EB3, 129)):
            src = AP(ext_d.t, c, [[1, 128], [512, AH], [1, 128]])
            em.dma("sp", toe[:, :, :], src, reads=[ext_d], writes=[toe])
            for i in range(4):
                pf = PF[i]
                mm_group(pf[:, :], pf, [(Jm[:, :], toe[:, 4 * i:4 * i + 4, :].rearrange("p h q -> p (h q)"))], [Jm, toe])
                act(EB[:, 4 * i:4 * i + 4, :].rearrange("p h q -> p (h q)"), pf[:, :], AF.Exp, [pf], [EB])
        cp("dve", EB0[:, :, :], bc3(EBc[:, :], 128), [EBc], [EB0])
        mset("dve", EB4[64:128, :, 0:64], 0.0, EB4)
        mset("dve", EB0[0:64, :, 64:128], 0.0, EB0)

    def load_xT(src, t0):
        TB = cfg["TB"]; T = cfg["T"]
        for b in range(TB):
            em.dma("pool", xtok[:, :], src[t0 + b * 128:t0 + (b + 1) * 128, :], reads=[src], writes=[xtok])
            for hlf in range(2):
                pb = next_pb()
                for k in range(8):
                    kc = hlf * 8 + k
                    tr(pb[:, k * 128:(k + 1) * 128], pb, xtok[:, kc * 128:(kc + 1) * 128], xtok, ident_bf[:, :])
                cp("act" if hlf else "dve", xT[:, hlf * 8:(hlf + 1) * 8, b * 128:(b + 1) * 128],
                   pb[:, :].rearrange("p (k t) -> p k t", k=8), [pb], [xT])

    pf_rr = [0]

    def proj_fm(slab, col0, pf_set=(0, 1)):
        TB = cfg["TB"]; T = cfg["T"]
        pf = PF[pf_set[pf_rr[0] % len(pf_set)]]
        pf_rr[0] += 1
        mm_group(pf[:, 0:T], pf, [(slab[:, kc, col0:col0 + 128], xT[:, kc, 0:T]) for kc in range(16)], [slab, xT])
        return pf

    def proj_tm(slab, b, ncols, pf, col0=0):
        TB = cfg["TB"]; T = cfg["T"]
        mm_group(pf[:, 0:ncols], pf, [(xT[:, kc, b * 128:(b + 1) * 128], slab[:, kc, col0:col0 + ncols]) for kc in range(16)], [slab, xT])
        return pf

    conv_i = [0]

    def conv_head(pf, blk):
        TB = cfg["TB"]; T = cfg["T"]
        st = ((cst_a, cacc_a), (cst2, cacc2))[conv_i[0] % 2]
        conv_i[0] += 1
        cst, cacc = st
        cp("act", cst[:, 0:3], ctail[:, :, blk], [ctail], [cst])
        cp("act", cst[:, 3:3 + T], pf[:, 0:T], [pf], [cst])
        cp("act", ctail[:, :, blk], cst[:, cfg["LT"]:cfg["LT"] + 3], [cst], [ctail])
        act(cacc[:, 0:T], pf[:, 0:T], AF.Identity, [pf, convp], [cacc], scale=convp[:, 3, blk:blk + 1], bias=convp[:, 4, blk:blk + 1])
        return st

    def conv_tail(st, blk, dst):
        TB = cfg["TB"]; T = cfg["T"]
        cst, cacc = st
        for k in (2, 1, 0):
            stt("dve", cacc[:, 0:T], cst[:, k:k + T], convp[:, k, blk:blk + 1], cacc[:, 0:T], ALU.mult, ALU.add, [cst, convp, cacc], [cacc])
        act(dst[:, 0:T], cacc[:, 0:T], AF.Silu, [cacc], [dst])

    def conv_pipeline(items):
        prev = None
        for slab, col0, blk, dst in items:
            pf = proj_fm(slab, col0)
            st = conv_head(pf, blk)
            if prev is not None:
                conv_tail(*prev)
            prev = (st, blk, dst)
        conv_tail(*prev)

    def dt_stage(L):
        TB = cfg["TB"]; T = cfg["T"]
        pf = PF[2]
        for b in range(TB):
            mm_group(pf[:, b * NH:(b + 1) * NH], pf, [(xT[:, kc, b * 128:(b + 1) * 128], wdt[:, kc, :]) for kc in range(16)], [wdt, xT])
        fl = lambda t_: t_[:, 0:TB, :].rearrange("p b h -> p (b h)")
        tt("dve", dt_t[:, 0:TB, :], pf[:, 0:TB * NH].rearrange("p (b h) -> p b h", b=TB), bcm(dtb[:, :], TB), ALU.add, [pf, dtb], [dt_t])
        act(fl(dt_a), fl(dt_t), AF.Abs, [dt_t], [dt_a])
        act(fl(dt_a), fl(dt_a), AF.Exp, [dt_a], [dt_a], scale=-1.0)
        act(fl(dt_a), fl(dt_a), AF.Ln, [dt_a], [dt_a], bias=1.0)
        stt("dve", fl(dt_v), fl(dt_t), 0.0, fl(dt_a), ALU.max, ALU.add, [dt_t, dt_a], [dt_v])
        tt("dve", dtA[:, 0:TB, :], dt_v[:, 0:TB, :], bcm(Aneg[:, :], TB), ALU.mult, [dt_v, Aneg], [dtA])
        for b in range(TB):
            pc = PF[3]
            mm_group(pc[0:L, 0:NH], pc, [(triU[0:L, 0:L], dtA[0:L, b, :])], [triU, dtA])
            mm_group(pc[0:L, NH:2 * NH], pc, [(triLs[0:L, 0:L], dtA[0:L, b, :])], [triLs, dtA])
            mm_group(pc[:, 2 * NH:3 * NH], pc, [(triU[0:L, :], dtA[0:L, b, :]), (triLs[0:L, :], dtA[0:L, b, :])], [triU, triLs, dtA])
            if L < 128:
                mset("dve", exb[:, b, :], 0.0, exb)
                act(exb[0:L, b, 0:128], pc[0:L, 0:128], AF.Exp, [pc], [exb])
                act(exb[:, b, 128:192], pc[:, 128:192], AF.Exp, [pc], [exb])
            else:
                act(exb[:, b, :], pc[:, 0:192], AF.Exp, [pc], [exb])
        tt("dve", wd2[:, 0:TB, 0, :], exb[:, 0:TB, NH:2 * NH], dt_v[:, 0:TB, :], ALU.mult, [exb, dt_v], [wd2])
        cp("act", wd2[:, 0:TB, 1, :], dt_v[:, 0:TB, :], [dt_v], [wd2])

    def ssd_projconv(g, mode, need_c=False):
        main = mode == "main"
        sx = wslab(w_in, 0, OFF_XBC + g * 512, 512)
        sbc = next_ring()
        wslab(w_in, 0, OFF_B + g * 128, 128, dst=sbc, dcol=0)
        if main or need_c:
            wslab(w_in, 0, OFF_C + g * 128, 128, dst=sbc, dcol=128)
        items = [(sx, i * 128, g * 4 + i, xbcT[i]) for i in range(4)] + [(sbc, 0, 32 + g, xbcT[4])]
        if main or need_c:
            items.append((sbc, 128, 40 + g, xbcT[5]))
        conv_pipeline(items)

    def ssd_transposes(g):
        TB = cfg["TB"]
        for b in range(TB):
            pb = next_pb()
            for i in range(4):
                tr(pb[:, i * 128:(i + 1) * 128], pb, xbcT[i][:, b * 128:(b + 1) * 128], xbcT[i], ident_bf[:, :])
            cp("act", xs_tok[:, b, :], pb[:, 0:512], [pb], [xs_tok])
        pb = next_pb()
        for b in range(TB):
            tr(pb[:, b * 128:(b + 1) * 128], pb, xbcT[4][:, b * 128:(b + 1) * 128], xbcT[4], ident_bf[:, :])
        cp("act", B_tok[:, 0:TB, :], pb[:, 0:TB * 128].rearrange("p (b n) -> p b n", b=TB), [pb], [B_tok])

    def ssd_bloop(g, mode, L):
        TB = cfg["TB"]
        main = mode == "main"
        if main:
            sz_slab = wslab(w_in, 0, g * 512, 512)
        hs = slice(g * 8, g * 8 + 8)
        Sg = S[:, g * 512:(g + 1) * 512]
        sil = (cst, cacc)
        PYD = (PF[4], PF[0])
        PYO = (PF[5], PF[1])
        if main and L < 128:
            mset("dve", ynb[L:128, :], 0.0, ynb)

        def xs3f(b):
            return xs_tok[0:L, b, :].rearrange("p (h q) -> p h q", h=8)

        def front(b):
            if main:
                cp("act", Sbf[:, :], Sg, [S], [Sbf])
            if main:
                tt("dve", xwd[0:L, :, :, :], xs3f(b).unsqueeze(1).to_broadcast([L, 2, 8, 64]),
                   wd2[0:L, b, :, hs].unsqueeze(3).to_broadcast([L, 2, 8, 64]), ALU.mult, [xs_tok, wd2], [xwd])
            else:
                tt("dve", xwd[0:L, 0, :, :], xs3f(b), bc3(wd2[0:L, b, 0, hs], 64), ALU.mult, [xs_tok, wd2], [xwd])
            if main:
                pyo = PYO[b % 2]
                mm_group(pyo[0:L, :], pyo, [(xbcT[5][:, b * 128:b * 128 + L], Sbf[:, :])], [xbcT[5], Sbf])
            psu = PF[3]
            mm_group(psu[:, :], psu, [(B_tok[0:L, b, :], xwd[0:L, 0, :, :].rearrange("p h q -> p (h q)"))], [B_tok, xwd])
            tt("dve", Sg.rearrange("p (h q) -> p h q", h=8), Sg.rearrange("p (h q) -> p h q", h=8),
               bc3(exb[:, b, 2 * NH + g * 8:2 * NH + g * 8 + 8], 64), ALU.mult, [S, exb], [S])
            tt("dve", Sg, Sg, psu[:, :], ALU.add, [S, psu], [S])

        def taila1(b):
            sb_ = sil[b % 2]
            tt("dve", Xg[0:L, :, 0:L], bc3(dtA[0:L, b, hs], L), bcm(triU[0:L, 0:L], 8), ALU.mult, [dtA, triU], [Xg])
            hu = 512 // L
            nu = 8 // hu
            pcb = PF[3]
            mm_group(pcb[0:L, 0:L], pcb, [(xbcT[4][:, b * 128:b * 128 + L], xbcT[5][:, b * 128:b * 128 + L])], [xbcT[4], xbcT[5]])
            for u in range(nu):
                pu = PF[2]
                mm_group(pu[0:L, 0:512], pu, [(triLs[0:L, 0:L], Xg[0:L, u * hu:(u + 1) * hu, 0:L])], [triLs, Xg])
                act(Dm[0:L, u * hu:(u + 1) * hu, 0:L], pu[0:L, 0:512].rearrange("p (h i) -> p h i", h=hu), AF.Exp, [pu], [Dm])

        def zs(b):
            sb_ = sil[b % 2]
            pz = proj_tm(sz_slab, b, 512, PF[2])
            act(sb_[0:L, 0:512], pz[0:L, :], AF.Silu, [pz], [sb_])

        def taila2(b):
            pcb = PF[3]
            tt("dve", cbm[0:L, 0:L], pcb[0:L, 0:L], triU[0:L, 0:L], ALU.mult, [pcb, triU], [cbm])
            tt("dve", MT[0:L, :, 0:L], Dm[0:L, :, 0:L], bcm(cbm[0:L, 0:L], 8), ALU.mult, [Dm, cbm], [MT])
            pyd = PYD[b % 2]
            for h in range(8):
                mm_group(pyd[0:L, h * 64:(h + 1) * 64], pyd, [(MT[0:L, h, 0:L], xwd[0:L, 1, h, :])], [MT, xwd])

        def tailb(b):
            pyo = PYO[b % 2]
            pyd = PYD[b % 2]
            sb_ = sil[b % 2]
            tt("dve", y1[0:L, :].rearrange("p (h q) -> p h q", h=8), pyo[0:L, :].rearrange("p (h q) -> p h q", h=8),
               bc3(exb[0:L, b, hs], 64), ALU.mult, [pyo, exb], [y1])
            tt("dve", y3[0:L, :].rearrange("p (h q) -> p h q", h=8), xs3f(b), bc3(dsk[0:L, hs], 64), ALU.mult, [xs_tok, dsk], [y3])
            tt("dve", y1[0:L, :], y1[0:L, :], y3[0:L, :], ALU.add, [y1, y3], [y1])
            tt("dve", y2[0:L, :], pyd[0:L, :], y1[0:L, :], ALU.add, [pyd, y1], [y2])
            tt("dve", y2[0:L, :], y2[0:L, :], sb_[0:L, 0:512], ALU.mult, [y2, sb_], [y2])
            act(y3[0:L, :], y2[0:L, :], AF.Square, [y2], [y3, st8], accum_out=st8[0:L, 0:1])
            act(st8[0:L, 2:3], st8[0:L, 0:1], AF.Ln, [st8, epsc], [st8], scale=1.0 / 512, bias=epsc[0:L, 0:1])
            act(st8[0:L, 3:4], st8[0:L, 2:3], AF.Exp, [st8], [st8], scale=-0.5)
            act(ynb[0:L, :], y2[0:L, :], AF.Copy, [y2, st8], [ynb], scale=st8[0:L, 3:4])
            pb2 = next_pb()
            for i in range(4):
                tr(pb2[:, i * 128:(i + 1) * 128], pb2, ynb[:, i * 128:(i + 1) * 128], ynb, ident_bf[:, :])
            return pb2

        def tailb2(b, pb2):
            tt("dve", y_ssdT[:, g * 4:g * 4 + 4, b * 128:(b + 1) * 128], pb2[:, 0:512].rearrange("p (a t) -> p a t", a=4),
               bc3(normw[:, g * 4:g * 4 + 4], 128), ALU.mult, [pb2, normw], [y_ssdT])

        if not main:
            for b in range(TB):
                front(b)
            return
        front(0)
        taila1(0)
        taila2(0)
        zs(0)
        for b in range(TB):
            if b + 1 < TB:
                front(b + 1)
                taila1(b + 1)
            pb2 = tailb(b)
            if b + 1 < TB:
                taila2(b + 1)
            tailb2(b, pb2)
            if b + 1 < TB:
                zs(b + 1)

    def ssd_group(g, mode, L, need_c=False):
        ssd_projconv(g, mode, need_c)
        ssd_transposes(g)
        ssd_bloop(g, mode, L)

    def attn_kv_proj(hg, want_q, kout=None, vout=None, orow=0):
        TB = cfg["TB"]; T = cfg["T"]
        sk = wslab(w_in, 0, OFF_K + hg * 512, 512)
        for i in range(4):
            pf = proj_fm(sk, i * 128, pf_set=(0, 1))
            cp("dve", KTc[:, i, 0:T], pf[:, 0:T], [pf], [KTc])
        if kout is not None:
            for b in range(TB):
                pf = proj_tm(sk, b, 512, PF[2 + (b % 2)])
                st_ = kvst[b % 2]
                cp("act", st_[:, :], pf[:, :], [pf], [st_])
                em.dma("sp", kout[orow + b * 128:orow + (b + 1) * 128, hg * 512:(hg + 1) * 512], st_[:, :], reads=[st_], writes=[kout])
        sv = wslab(w_in, 0, OFF_V + hg * 512, 512)
        for b in range(TB):
            pf = proj_tm(sv, b, 512, PF[2 + (b % 2)])
            if vout is not None:
                st_ = kvst[b % 2]
                cp("act", st_[:, :], pf[:, :], [pf], [st_])
                cp("dve", Vc[:, b, :], st_[:, :], [st_], [Vc])
                em.dma("sp", vout[orow + b * 128:orow + (b + 1) * 128, hg * 512:(hg + 1) * 512], st_[:, :], reads=[st_], writes=[vout])
            else:
                cp("dve", Vc[:, b, :], pf[:, :], [pf], [Vc])
        if want_q:
            sq = wslab(w_in, 0, OFF_Q + hg * 512, 512)
            for i in range(4):
                pf = proj_fm(sq, i * 128, pf_set=(0, 1))
                act(QT[:, i, 0:T], pf[:, 0:T], AF.Copy, [pf], [QT], scale=QSCALE)

    def hist_update(hg):
        TB = cfg["TB"]; T = cfg["T"]
        hsl = slice(hg * 4, hg * 4 + 4)
        vsl = slice(hg * 512, (hg + 1) * 512)
        if TB < 4:
            keep = 4 - TB
            for j in range(keep):
                cp("act", KTh[:, hsl, j * 128:(j + 1) * 128], KTh[:, hsl, (j + TB) * 128:(j + TB + 1) * 128], [KTh], [KTh])
                cp("act", Vh[:, j, vsl], Vh[:, j + TB, vsl], [Vh], [Vh])
            cp("act", KTh[:, hsl, keep * 128:512], KTc[:, :, 0:T], [KTc], [KTh])
            cp("act", Vh[:, keep:4, vsl], Vc[:, 0:TB, :], [Vc], [Vh])
        else:
            cp("act", KTh[:, hsl, :], KTc[:, :, T - 512:T], [KTc], [KTh])
            cp("act", Vh[:, :, vsl], Vc[:, TB - 4:TB, :], [Vc], [Vh])

    def attention(hist_flagged, kout=None, vout=None, orow=0, slide=True, stop=None):
        TB = cfg["TB"]; T = cfg["T"]
        pi = [0]
        for hg in range(4):
            if stop == 631:
                attn_kv_proj(hg, True, None, None, orow)
                return True
            if stop == 632:
                attn_kv_proj(hg, False, kout, vout, orow)
                return True
            attn_kv_proj(hg, True, kout, vout, orow)
            if stop == 63:
                return True
            for b in range(TB):
                PT = PTb[pi[0] % 2]
                for kb in range(5):
                    ob = b + kb - 4
                    psc = PF[2 + (kb % 2)]
                    for hl in range(4):
                        if ob < 0:
                            kt = KTh[:, hg * 4 + hl, (ob + 4) * 128:(ob + 5) * 128]
                            kbuf = KTh
                        else:
                            kt = KTc[:, hl, ob * 128:(ob + 1) * 128]
                            kbuf = KTc
                        mm_group(psc[:, hl * 128:(hl + 1) * 128], psc, [(kt, QT[:, hl, b * 128:(b + 1) * 128])], [kbuf, QT])
                    pe_ = Pexp[kb % 2]
                    bias = negbig[:, 0:1] if (hist_flagged is not None and hist_flagged + ob * 128 < 0) else zero1[:, 0:1]
                    act(pe_[:, :, :].rearrange("p h q -> p (h q)"), psc[:, :], AF.Exp, [psc, negbig, zero1], [pe_], bias=bias)
                    EBk = (EB0, None, None, EB3, EB4)[kb]
                    if EBk is None:
                        ebap = bc3(EBc[:, hg * 4:hg * 4 + 4], 128)
                        ebuf = EBc
                    else:
                        ebap = EBk[:, hg * 4:hg * 4 + 4, :]
                        ebuf = EBk
                    tt("dve", PT[:, kb, :, :], pe_[:, :, :], ebap, ALU.mult, [pe_, ebuf], [PT])
                pden = PF[4]
                mm_group(pden[:, :], pden, [(ones_bf[:, :], PT[:, kb, :, :].rearrange("p h q -> p (h q)")) for kb in range(5)], [ones_bf, PT])
                po = PF[5]
                for hl in range(4):
                    prs = []
                    for kb in range(5):
                        ob = b + kb - 4
                        if ob < 0:
                            vv = Vh[:, ob + 4, (hg * 4 + hl) * 128:(hg * 4 + hl + 1) * 128]
                        else:
                            vv = Vc[:, ob, hl * 128:(hl + 1) * 128]
                        prs.append((vv, PT[:, kb, hl, :]))
                    mm_group(po[:, hl * 128:(hl + 1) * 128], po, prs, [Vh, Vc, PT])
                em.op("dve", lambda e: e.reciprocal(rden[:, :], pden[:, :]), reads=[pden], writes=[rden])
                tt("dve", y_attT[:, hg * 4:hg * 4 + 4, b * 128:(b + 1) * 128], po[:, :].rearrange("p (h q) -> p h q", h=4),
                   rden[:, :].rearrange("p (h q) -> p h q", h=4), ALU.mult, [po, rden], [y_attT])
                pi[0] += 1
                if stop == 64:
                    return True
            if slide:
                hist_update(hg)
            if stop == 65:
                return True

    def gates_outproj():
        TB = cfg["TB"]; T = cfg["T"]
        for cg in range(4):
            c0 = cg * 512
            s_so0 = wslab(w_ssd_out, 0, c0, 512)
            for obl in range(4):
                cs = slice(obl * 128, (obl + 1) * 128)
                pa = PF[obl % 2]
                mm_group(pa[:, 0:T], pa, [(s_so0[:, kc, cs], y_ssdT[:, kc, 0:T]) for kc in range(16)], [s_so0, y_ssdT])
                cp("act", m1[:, obl, 0:T], pa[:, 0:T], [pa], [m1])
            s_so1 = wslab(w_ssd_out, 2048, c0, 512)
            for obl in range(4):
                cs = slice(obl * 128, (obl + 1) * 128)
                pa = PF[4 + obl % 2]
                mm_group(pa[:, 0:T], pa, [(s_so1[:, kc, cs], y_ssdT[:, 16 + kc, 0:T]) for kc in range(16)], [s_so1, y_ssdT])
                tt("dve", m1[:, obl, 0:T], m1[:, obl, 0:T], pa[:, 0:T], ALU.add, [m1, pa], [m1])
            s_ao = wslab(w_att_out, 0, c0, 512)
            for obl in range(4):
                cs = slice(obl * 128, (obl + 1) * 128)
                pbk = PF[2 + obl % 2]
                mm_group(pbk[:, 0:T], pbk, [(s_ao[:, kc, cs], y_attT[:, kc, 0:T]) for kc in range(16)], [s_ao, y_attT])
                cp("dve", m2[:, obl, 0:T], pbk[:, 0:T], [pbk], [m2])
            s_gs = wslab(w_in, 0, OFF_GS + c0, 512)
            for obl in range(4):
                cs = slice(obl * 128, (obl + 1) * 128)
                pg = PF[4 + obl % 2]
                mm_group(pg[:, 0:T], pg, [(s_gs[:, kc, cs], xT[:, kc, 0:T]) for kc in range(16)], [s_gs, xT])
                act(sgs[:, 0:T], pg[:, 0:T], AF.Sigmoid, [pg], [sgs])
                tt("dve", m1[:, obl, 0:T], m1[:, obl, 0:T], sgs[:, 0:T], ALU.mult, [m1, sgs], [m1])
            s_ga = wslab(w_in, 0, OFF_GA + c0, 512)
            for obl in range(4):
                ob = cg * 4 + obl
                cs = slice(obl * 128, (obl + 1) * 128)
                pg2 = PF[obl % 2]
                mm_group(pg2[:, 0:T], pg2, [(s_ga[:, kc, cs], xT[:, kc, 0:T]) for kc in range(16)], [s_ga, xT])
                act(sga[:, 0:T], pg2[:, 0:T], AF.Sigmoid, [pg2], [sga])
                tt("dve", m2[:, obl, 0:T], m2[:, obl, 0:T], sga[:, 0:T], ALU.mult, [m2, sga], [m2])
                tt("dve", mergedT[:, ob, 0:T], m1[:, obl, 0:T], m2[:, obl, 0:T], ALU.add, [m1, m2], [mergedT])

    def layer_norm_rows(hrow, bst_, mv_):
        for c in range(4):
            em.op("dve", lambda e, c=c: e.bn_stats(bst_[:, c, :], hrow[:, c * 512:(c + 1) * 512]), reads=[hbuf], writes=[bst_])
        em.op("dve", lambda e: e.bn_aggr(mv_[:, 0:2], bst_[:, :, :]), reads=[bst_], writes=[mv_])
        act(mv_[:, 3:4], mv_[:, 1:2], AF.Ln, [mv_, epsc], [mv_], bias=epsc[:, 1:2])
        act(mv_[:, 4:5], mv_[:, 3:4], AF.Exp, [mv_], [mv_], scale=-0.5)
        ts("dve", hrow, hrow, mv_[:, 0:1], mv_[:, 4:5], ALU.subtract, ALU.mult, [hbuf, mv_], [hbuf])
        tt("dve", hrow, hrow, lng[:, :], ALU.mult, [hbuf, lng], [hbuf])
        tt("dve", hrow, hrow, lnb[:, :], ALU.add, [hbuf, lnb], [hbuf])

    def wout_ln1(src, t0):
        TB = cfg["TB"]; T = cfg["T"]
        em.dma("sp", lng[:, :], ln1_g.t.ap().partition_broadcast(128), reads=[ln1_g], writes=[lng])
        em.dma("sp", lnb[:, :], ln1_b.t.ap().partition_broadcast(128), reads=[ln1_b], writes=[lnb])
        for cg in range(4):
            s_wo = wslab(w_out, 0, cg * 512, 512)
            for b in range(TB):
                pf = PF[(cg * TB + b) % 4]
                mm_group(pf[:, :], pf, [(mergedT[:, kc, b * 128:(b + 1) * 128], s_wo[:, kc, :]) for kc in range(16)], [mergedT, s_wo])
                cp("act", hbuf[:, b, cg * 512:(cg + 1) * 512], pf[:, :], [pf], [hbuf])
        for b in range(TB):
            em.dma("sp", xres[:, :], src[t0 + b * 128:t0 + (b + 1) * 128, :], reads=[src], writes=[xres])
            hrow = hbuf[:, b, :]
            stt("dve", hrow, xres[:, :], ALPHA, hrow, ALU.mult, ALU.add, [xres, hbuf], [hbuf])
            layer_norm_rows(hrow, bst, mv)
            cp("act", hbf[:, :], hrow, [hbuf], [hbf])
            for hlf in range(2):
                pb = next_pb()
                for k in range(8):
                    kc = hlf * 8 + k
                    tr(pb[:, k * 128:(k + 1) * 128], pb, hbf[:, kc * 128:(kc + 1) * 128], hbf, ident_bf[:, :])
                cp("act" if hlf else "dve", hT[:, hlf * 8:(hlf + 1) * 8, b * 128:(b + 1) * 128],
                   pb[:, :].rearrange("p (k t) -> p k t", k=8), [pb], [hT])

    def mlp_ln2(ydst, t0, nrows):
        TB = cfg["TB"]; T = cfg["T"]
        for hf in range(2):
            for s4 in range(8):
                s_up = wslab(w_up, 0, (hf * 8 + s4) * 512, 512)
                for fl_ in range(4):
                    pf = proj_fm_h(s_up, fl_ * 128)
                    act(rtmp[:, 0:T], pf[:, 0:T], AF.Relu, [pf], [rtmp])
                    tt("dve", uT[:, s4 * 4 + fl_, 0:T], rtmp[:, 0:T], rtmp[:, 0:T], ALU.mult, [rtmp], [uT])
            if hf == 1:
                em.dma("sp", lng[:, :], ln2_g.t.ap().partition_broadcast(128), reads=[ln2_g], writes=[lng])
                em.dma("sp", lnb[:, :], ln2_b.t.ap().partition_broadcast(128), reads=[ln2_b], writes=[lnb])
            for cg in range(4):
                for q2 in range(2):
                    s_d = wslab(w_down, (hf * 32 + q2 * 16) * 128, cg * 512, 512)
                    for b in range(TB):
                        pf = PF[2 + ((q2 * TB + b) % 4)]
                        mm_group(pf[:, :], pf, [(uT[:, q2 * 16 + kc, b * 128:(b + 1) * 128], s_d[:, kc, :]) for kc in range(16)], [uT, s_d])
                        hseg = hbuf[:, b, cg * 512:(cg + 1) * 512]
                        if hf == 0 and q2 == 0:
                            stt("dve", hseg, hseg, ALPHA, pf[:, :], ALU.mult, ALU.add, [hbuf, pf], [hbuf])
                        else:
                            tt("dve", hseg, hseg, pf[:, :], ALU.add, [hbuf, pf], [hbuf])
        for b in range(TB):
            hrow = hbuf[:, b, :]
            layer_norm_rows(hrow, bst2, mv2)
            r = min(128, nrows - b * 128)
            if r > 0:
                em.dma("sp", ydst[t0 + b * 128:t0 + b * 128 + r, :], hbuf[0:r, b, :], reads=[hbuf], writes=[ydst])

    def proj_fm_h(slab, col0):
        TB = cfg["TB"]; T = cfg["T"]
        pf = PF[pf_rr[0] % 2]
        pf_rr[0] += 1
        mm_group(pf[:, 0:T], pf, [(slab[:, kc, col0:col0 + 128], hT[:, kc, 0:T]) for kc in range(16)], [slab, hT])
        return pf

    def out_state(dst):
        for c in range(8):
            pf = PF[c % 4]
            for i in range(4):
                blk = c * 4 + i
                tr(pf[:, i * 128:(i + 1) * 128], pf, S[:, blk * 128:(blk + 1) * 128], S, ident_f[:, :])
            st_ = kvst[c % 2]
            cp("act" if c % 2 else "dve", st_[:, :], pf[:, :], [pf], [st_])
            em.dma("sp", dst[c * 512:(c + 1) * 512, :].rearrange("(b p) n -> p b n", p=128), st_[:, :].rearrange("p (b n) -> p b n", b=4),
                   reads=[st_], writes=[dst])

    def out_conv(dst):
        for t in range(3):
            pf = PF[t]
            tr(pf[0:48, 0:128], pf, ctail[:, t, :], ctail, ident_f[:, :])
            st_ = kvst[t % 2]
            cp("dve", st_[0:48, 0:128], pf[0:48, 0:128], [pf], [st_])
            em.dma("sp", dst[t, :].rearrange("(b p) -> b p", p=128), st_[0:48, 0:128], reads=[st_], writes=[dst])

    def finish():
        em.barrier(engines=("pe", "act", "dve", "pool", "sp"))
        em.emit()
        return nc, em

    if USE_SCRATCH:
        convert_weight("w_in", 8)
    setup()
    if STOP == 1:
        return finish()
    mset("dve", ctail[:, :, :], 0.0, ctail)
    mset("dve", S[:, :], 0.0, S)
    mset("pool", KTh[:, :, :], 0.0, KTh)
    mset("pool", Vh[:, :, :], 0.0, Vh)
    em.barrier()

    npre = NPRE // T
    for ti in range(npre):
        load_xT(xpre, ti * T)
        if STOP == 2:
            return finish()
        dt_stage(128)
        if STOP == 3:
            return finish()
        nc_ = (ti == npre - 1)
        ssd_projconv(0, "pre", nc_)
        ssd_transposes(0)
        for g in range(NG):
            if g + 1 < NG:
                ssd_projconv(g + 1, "pre", nc_)
            ssd_bloop(g, "pre", 128)
            if g + 1 < NG:
                ssd_transposes(g + 1)
        if STOP == 5:
            return finish()
        if (npre - ti) * T <= 512:
            em.barrier()
            for hg in range(4):
                attn_kv_proj(hg, False)
                hist_update(hg)
        if ti == 0 and USE_SCRATCH:
            convert_weight("w_up", 4)
            convert_weight("w_down", 4)
            convert_weight("w_ssd_out", 2)
            convert_weight("w_att_out", 1)
            convert_weight("w_out", 1)
        em.barrier()
    if STOP == 6:
        return finish()
    ts("dve", S[:, :], S[:, :], flag[:, 0:1], None, ALU.mult, None, [S, flag], [S])

    nmain = NMAIN // T
    for ti in range(nmain):
        t0 = ti * T
        load_xT(xmain, t0)
        if STOP == 61:
            return finish()
        build_EB()
        if STOP == 62:
            return finish()
        want = (NMAIN - t0) <= 512
        if attention(hist_flagged=t0, kout=k_main if want else None, vout=v_main if want else None,
                     orow=(t0 - (NMAIN - 512)) if want else 0, stop=STOP):
            return finish()
        if STOP == 7:
            return finish()
        dump("yatt", y_attT, ti)
        dump("xT", xT, ti)
        em.barrier()
        dt_stage(128)
        for g in range(NG):
            ssd_group(g, "main", 128)
        if STOP == 8:
            return finish()
        dump("yssd", y_ssdT, ti)
        em.barrier()
        gates_outproj()
        if STOP == 9:
            return finish()
        dump("merged", mergedT, ti)
        em.barrier()
        wout_ln1(xmain, t0)
        if STOP == 10:
            return finish()
        dump("h", hbuf, ti)
        em.barrier()
        mlp_ln2(y_main, t0, T)
        em.barrier()
        if STOP == 11:
            return finish()
    out_state(ssd_main)
    out_conv(conv_main)
    em.barrier()
    if STOP == 12:
        return finish()

    cfg["TB"] = 1
    cfg["T"] = 128
    cfg["LT"] = 64
    stage2, _ = view("stage2", A3, [128, 8, 128], F32)
    for t in range(3):
        em.dma("sp", stage2[0:48, t, :], state_conv[t, :].rearrange("(b p) -> b p", p=128), reads=[state_conv], writes=[stage2])
    for t in range(3):
        tr(PF[0][:, t * 48:(t + 1) * 48], PF[0], stage2[0:48, t, :], stage2, ident_f[0:48, 0:48])
    cp("dve", ctail[:, :, :], PF[0][:, 0:144].rearrange("p (a b) -> p a b", a=3), [PF[0]], [ctail])
    for c in range(8):
        st_ = kvst[c % 2]
        em.dma("sp", st_[:, :].rearrange("p (b n) -> p b n", b=4), state_ssd[c * 512:(c + 1) * 512, :].rearrange("(b p) n -> p b n", p=128),
               reads=[state_ssd], writes=[st_])
        pf = PF[c % 4]
        for i in range(4):
            tr(pf[:, i * 128:(i + 1) * 128], pf, st_[:, i * 128:(i + 1) * 128], st_, ident_f[:, :])
        cp("act" if c % 2 else "dve", S[:, c * 512:(c + 1) * 512], pf[:, :], [pf], [S])
    for j in range(4):
        em.dma("pool", Vh[:, j, :], cache_v[j * 128:(j + 1) * 128, :], reads=[cache_v], writes=[Vh])
    for j in range(4):
        em.dma("pool", xtok[:, :], cache_k[j * 128:(j + 1) * 128, :], reads=[cache_k], writes=[xtok])
        for hlf in range(2):
            pb = next_pb()
            for k in range(8):
                hh = hlf * 8 + k
                tr(pb[:, k * 128:(k + 1) * 128], pb, xtok[:, hh * 128:(hh + 1) * 128], xtok, ident_bf[:, :])
            cp("act" if hlf else "dve", KTh[:, hlf * 8:(hlf + 1) * 8, j * 128:(j + 1) * 128],
               pb[:, :].rearrange("p (k t) -> p k t", k=8), [pb], [KTh])
    em.barrier()
    load_xT(xsamp, 0)
    em.barrier()
    build_EB()
    em.barrier()
    attention(hist_flagged=None, kout=k_samp, vout=v_samp, orow=0, slide=False)
    em.barrier()
    dt_stage(64)
    for g in range(NG):
        ssd_group(g, "main", 64)
    em.barrier()
    gates_outproj()
    em.barrier()
    wout_ln1(xsamp, 0)
    em.barrier()
    mlp_ln2(y_samp, 0, 128)
    em.barrier()
    out_state(ssd_samp)
    out_conv(conv_samp)
    outs = (y_main, y_samp, k_main, v_main, conv_main, ssd_main, k_samp, v_samp, conv_samp, ssd_samp)
    em.barrier(engines=("pe", "act", "dve", "pool", "sp"))
    em.emit()
    return nc, em


TB_DEFAULT = 4
_PROG = {}


def _get_prog(npre, nmain, tb):
    key = (npre, nmain, tb)
    if key not in _PROG:
        _PROG[key] = build_program(npre, nmain, tb)
    return _PROG[key]


def _core_inputs(c, half_len, x_prompt, x_sample, cache_k, cache_v, state_conv, state_ssd, weights):
    seq, half = c // 2, c % 2
    f32 = np.float32
    xm = np.ascontiguousarray(x_prompt[seq, half * half_len:(half + 1) * half_len], dtype=f32)
    xp = np.ascontiguousarray(x_prompt[seq, 0:half_len], dtype=f32) if half else np.zeros((half_len, D), f32)
    xs = np.zeros((128, D), f32)
    xs[:DEC_SEQ] = x_sample[c]
    m = {
        "xpre": xp, "xmain": xm, "xsamp": xs,
        "flag": np.full((128, 1), float(half), f32),
        "cache_k": np.ascontiguousarray(cache_k[0, c].reshape(512, D), dtype=f32),
        "cache_v": np.ascontiguousarray(cache_v[0, c].reshape(512, D), dtype=f32),
        "state_conv": np.ascontiguousarray(state_conv[0, c], dtype=f32),
        "state_ssd": np.ascontiguousarray(state_ssd[0, c].reshape(D_INNER, NST), dtype=f32),
    }
    m.update(weights)
    return m


def _weights(w_in, conv_w, conv_b, dt_bias, a_log, d_skip, ssd_norm_w, rel_table, w_ssd_out, w_att_out,
             w_out, ln1_g, ln1_b, w_up, w_down, ln2_g, ln2_b):
    f = lambda a: np.ascontiguousarray(np.asarray(a), dtype=np.float32)
    return {
        "w_in": f(w_in[0]), "conv_w": f(conv_w[0]), "conv_b": f(conv_b[0]).reshape(1, CONV_DIM),
        "dt_bias": f(dt_bias[0]), "a_log": f(a_log[0]), "d_skip": f(d_skip[0]),
        "ssd_norm_w": f(ssd_norm_w[0]).reshape(32, 128), "rel_table": f(rel_table[0]),
        "w_ssd_out": f(w_ssd_out[0]), "w_att_out": f(w_att_out[0]), "w_out": f(w_out[0]),
        "ln1_g": f(ln1_g[0]), "ln1_b": f(ln1_b[0]), "w_up": f(w_up[0]), "w_down": f(w_down[0]),
        "ln2_g": f(ln2_g[0]), "ln2_b": f(ln2_b[0]),
    }


def kernel(x_prompt, x_sample, cache_k, cache_v, state_conv, state_ssd, w_in, conv_w, conv_b,
           dt_bias, a_log, d_skip, ssd_norm_w, rel_table, w_ssd_out, w_att_out, w_out,
           ln1_g, ln1_b, w_up, w_down, ln2_g, ln2_b):
    x_prompt = np.asarray(x_prompt)
    x_sample = np.asarray(x_sample)
    cache_k = np.asarray(cache_k)
    cache_v = np.asarray(cache_v)
    state_conv = np.asarray(state_conv)
    state_ssd = np.asarray(state_ssd)
    half_len = SEQ // 2
    nc, _ = _get_prog(half_len, half_len, TB_DEFAULT)
    wts = _weights(w_in, conv_w, conv_b, dt_bias, a_log, d_skip, ssd_norm_w, rel_table, w_ssd_out, w_att_out,
                   w_out, ln1_g, ln1_b, w_up, w_down, ln2_g, ln2_b)
    in_maps = [_core_inputs(c, half_len, x_prompt, x_sample, cache_k, cache_v, state_conv, state_ssd, wts)
               for c in range(8)]
    res = run_bass_kernel_spmd(nc, in_maps, core_ids=list(range(8))).results
    f32 = np.float32
    y_prompt = np.empty((BATCH, SEQ, D), f32)
    y_sample = np.empty((DEC_BATCH, DEC_SEQ, D), f32)
    nkp = np.empty((1, BATCH, 512, AH, 128), f32)
    nvp = np.empty((1, BATCH, 512, AH, 128), f32)
    ncp = np.empty((1, BATCH, 3, CONV_DIM), f32)
    nsp = np.empty((1, BATCH, NH, 64, NST), f32)
    nks = np.empty((1, DEC_BATCH, DEC_SEQ, AH, 128), f32)
    nvs = np.empty((1, DEC_BATCH, DEC_SEQ, AH, 128), f32)
    ncs = np.empty((1, DEC_BATCH, 3, CONV_DIM), f32)
    nss = np.empty((1, DEC_BATCH, NH, 64, NST), f32)
    for c in range(8):
        seq, half = c // 2, c % 2
        r = res[c]
        y_prompt[seq, half * half_len:(half + 1) * half_len] = r["y_main"]
        y_sample[c] = r["y_samp"][:DEC_SEQ]
        if half:
            nkp[0, seq] = r["k_main"].reshape(512, AH, 128)
            nvp[0, seq] = r["v_main"].reshape(512, AH, 128)
            ncp[0, seq] = r["conv_main"]
            nsp[0, seq] = r["ssd_main"].reshape(NH, 64, NST)
        nks[0, c] = r["k_samp"][:DEC_SEQ].reshape(DEC_SEQ, AH, 128)
        nvs[0, c] = r["v_samp"][:DEC_SEQ].reshape(DEC_SEQ, AH, 128)
        ncs[0, c] = r["conv_samp"]
        nss[0, c] = r["ssd_samp"].reshape(NH, 64, NST)
    return (y_prompt, y_sample, nkp, nvp, ncp, nsp, nks, nvs, ncs, nss)
```

```python
import numpy as np
import concourse.bass as bass
import concourse.mybir as mybir
from concourse.bass_utils import run_bass_kernel_spmd
from concourse.ap import AP

F32 = mybir.dt.float32
BF16 = mybir.dt.bfloat16
U8 = mybir.dt.uint8
AF = mybir.ActivationFunctionType
ALU = mybir.AluOpType

D = 2048
SEQ = 4096
BATCH = 4
DEC_BATCH = 8
DEC_SEQ = 64
D_INNER = 4096
NH = 64
NG = 8
NST = 128
CONV_DIM = 6144
AH = 16
D_FF = 8192
OFF_XBC = 4096
OFF_B = OFF_XBC + 4096
OFF_C = OFF_B + 1024
OFF_DT = OFF_XBC + CONV_DIM
OFF_Q = OFF_DT + NH
OFF_K = OFF_Q + 2048
OFF_V = OFF_K + 2048
OFF_GS = OFF_V + 2048
OFF_GA = OFF_GS + 2048
D_IN_PROJ = OFF_GA + 2048
ALPHA = (2 * 1) ** 0.25
LN_EPS = 1e-5
RMS_EPS = 1e-5
QSCALE = 128 ** -0.5


class Buf:
    __slots__ = ("t", "name", "lw", "rd", "sem", "cnt", "psum")

    def __init__(self, t, name):
        self.t = t
        self.name = name
        self.psum = False
        self.lw = None
        self.rd = {}
        self.sem = None
        self.cnt = 0

    def __getitem__(self, k):
        return self.t[k]


class Em:
    ENG = ("pe", "act", "dve", "pool", "sp")

    def __init__(self, nc):
        self.nc = nc
        self.q = {e: [] for e in self.ENG}
        self.sem = {e: nc.alloc_semaphore("s_" + e) for e in self.ENG}
        self.cnt = {e: 0 for e in self.ENG}
        self.pend = {e: False for e in self.ENG}
        self.seen = {e: {} for e in self.ENG}
        self.bufs = []
        self.nins = 0

    def sb(self, name, shape, dt):
        b = Buf(self.nc.alloc_sbuf_tensor(name, list(shape), dt), name)
        self.bufs.append(b)
        return b

    def ps(self, name, shape, dt=F32):
        b = Buf(self.nc.alloc_psum_tensor(name, list(shape), dt), name)
        b.psum = True
        self.bufs.append(b)
        return b

    def dram(self, name, shape, dt, kind="Internal"):
        b = Buf(self.nc.dram_tensor(name, list(shape), dt, kind=kind), name)
        self.bufs.append(b)
        return b

    def wrap(self, t, name):
        b = Buf(t, name)
        self.bufs.append(b)
        return b

    def _need(self, eng, tok, waits):
        if tok is None:
            return
        sem, val, _ = tok
        if self.seen[eng].get(sem.num, 0) >= val:
            return
        if sem.num in waits:
            val = max(val, waits[sem.num][1])
        waits[sem.num] = (sem, val)

    def _flush(self, eng, waits):
        out = []
        for num, (sem, val) in waits.items():
            self.seen[eng][num] = val
            out.append((sem, val))
        return out

    def _deps(self, eng, reads, writes):
        waits = {}
        for b in reads:
            self._need(eng, b.lw, waits)
            if b.psum:
                for e2, tok in b.rd.items():
                    if e2 != eng:
                        self._need(eng, tok, waits)
        for b in writes:
            if b.lw is not None and not (eng == "pe" and b.lw[2] == "pe"):
                self._need(eng, b.lw, waits)
            for e2, tok in b.rd.items():
                if not (eng == "pe" and e2 == "pe"):
                    self._need(eng, tok, waits)
        return self._flush(eng, waits)

    def op(self, eng, fn, reads=(), writes=(), sig=True):
        waits = self._deps(eng, reads, writes)
        if sig:
            self.cnt[eng] += 1
            tick = self.cnt[eng]
            self.pend[eng] = False
        else:
            tick = self.cnt[eng] + 1
            self.pend[eng] = True
        tok = (self.sem[eng], tick, eng)
        self.q[eng].append((waits, fn, (self.sem[eng], 1) if sig else None))
        self.nins += 1
        for b in reads:
            b.rd[eng] = tok
        for b in writes:
            b.lw = tok
            b.rd = {}
        return tok

    def dma(self, qeng, out_ap, in_ap, reads=(), writes=(), **kw):
        waits = self._deps(qeng, reads, writes)
        owner = writes[0] if writes else reads[0]
        if owner.sem is None:
            owner.sem = self.nc.alloc_semaphore("d%d_%s" % (len(self.bufs), owner.name) + "_%d" % id(owner))
        owner.cnt += 16
        tok = (owner.sem, owner.cnt, "dma")

        def fn(e, out_ap=out_ap, in_ap=in_ap, kw=kw):
            return e.dma_start(out=out_ap, in_=in_ap, **kw)
        self.q[qeng].append((waits, fn, (owner.sem, 16)))
        self.nins += 1
        for b in reads:
            b.rd[("dma", tok[0].num)] = tok
        for b in writes:
            b.lw = tok
            b.rd = {}
        return tok

    def wait_all(self, eng, toks):
        waits = {}
        for t in toks:
            self._need(eng, t, waits)
        self.q[eng].append((self._flush(eng, waits), None, None))

    def barrier(self, engines=("pe", "act", "dve", "sp")):
        toks = []
        for e in self.ENG:
            assert not self.pend[e], "barrier with pending unsignalled op on " + e
            if self.cnt[e] > 0:
                toks.append((self.sem[e], self.cnt[e], e))
        for b in self.bufs:
            if b.sem is not None and b.cnt > 0:
                toks.append((b.sem, b.cnt, "dma"))
        for e in engines:
            self.wait_all(e, toks)

    def emit(self):
        nc = self.nc
        q = self.q

        def run(e, lst):
            for waits, fn, inc in lst:
                for sem, val in waits:
                    e.wait_ge(sem, val)
                if fn is not None:
                    ins = fn(e)
                    if inc is not None:
                        ins.then_inc(inc[0], inc[1])

        with nc.Block() as block:
            @block.tensor
            def _(e):
                run(e, q["pe"])

            @block.scalar
            def _(e):
                run(e, q["act"])

            @block.vector
            def _(e):
                run(e, q["dve"])

            @block.gpsimd
            def _(e):
                run(e, q["pool"])

            @block.sync
            def _(e):
                run(e, q["sp"])


def build_program(NPRE, NMAIN, TB, RING=None, DEBUG=False, STOP=None):
    if RING is None:
        RING = 2 if TB >= 4 else 3
    TBM = TB
    T = 128 * TB
    TM = T
    assert NPRE % T == 0 and NMAIN % T == 0 and NMAIN >= 512 and NPRE >= 512
    nc = bass.Bass("TRN2", target_bir_lowering=False)
    em = Em(nc)

    def din(name, shape):
        return em.wrap(nc.dram_tensor(name, list(shape), F32, kind="ExternalInput"), name)

    def dout(name, shape):
        return em.wrap(nc.dram_tensor(name, list(shape), F32, kind="ExternalOutput"), name)

    xpre = din("xpre", [NPRE, D])
    xmain = din("xmain", [NMAIN, D])
    xsamp = din("xsamp", [128, D])
    flag_d = din("flag", [128, 1])
    cache_k = din("cache_k", [512, D])
    cache_v = din("cache_v", [512, D])
    state_conv = din("state_conv", [3, CONV_DIM])
    state_ssd = din("state_ssd", [D_INNER, NST])
    w_in = din("w_in", [D, D_IN_PROJ])
    conv_w = din("conv_w", [4, CONV_DIM])
    conv_b = din("conv_b", [1, CONV_DIM])
    dt_bias = din("dt_bias", [NH])
    a_log = din("a_log", [NH])
    d_skip = din("d_skip", [NH])
    ssd_norm_w = din("ssd_norm_w", [32, 128])
    rel_table = din("rel_table", [AH, 257])
    w_ssd_out = din("w_ssd_out", [D_INNER, D])
    w_att_out = din("w_att_out", [D, D])
    w_out = din("w_out", [D, D])
    ln1_g = din("ln1_g", [D])
    ln1_b = din("ln1_b", [D])
    w_up = din("w_up", [D, D_FF])
    w_down = din("w_down", [D_FF, D])
    ln2_g = din("ln2_g", [D])
    ln2_b = din("ln2_b", [D])

    y_main = dout("y_main", [NMAIN, D])
    y_samp = dout("y_samp", [128, D])
    k_main = dout("k_main", [512, D])
    v_main = dout("v_main", [512, D])
    conv_main = dout("conv_main", [3, CONV_DIM])
    ssd_main = dout("ssd_main", [D_INNER, NST])
    k_samp = dout("k_samp", [128, D])
    v_samp = dout("v_samp", [128, D])
    conv_samp = dout("conv_samp", [3, CONV_DIM])
    ssd_samp = dout("ssd_samp", [D_INNER, NST])
    ext_d = em.dram("ext_d", [AH, 512], F32)
    USE_SCRATCH = False
    WB = {}
    if USE_SCRATCH:
        WB = {
            "w_in": em.dram("w_in_b", [D, D_IN_PROJ], BF16),
            "w_ssd_out": em.dram("w_ssd_out_b", [D_INNER, D], BF16),
            "w_att_out": em.dram("w_att_out_b", [D, D], BF16),
            "w_out": em.dram("w_out_b", [D, D], BF16),
            "w_up": em.dram("w_up_b", [D, D_FF], BF16),
            "w_down": em.dram("w_down_b", [D_FF, D], BF16),
        }
    WSRC = {"w_in": w_in, "w_ssd_out": w_ssd_out, "w_att_out": w_att_out, "w_out": w_out, "w_up": w_up, "w_down": w_down}

    def convert_weight(name, nchunks):
        src = WSRC[name]
        dst = WB[name]
        rows = src.t.shape[0]
        step = rows // nchunks
        for i in range(nchunks):
            em.dma("pool", dst[i * step:(i + 1) * step, :], src[i * step:(i + 1) * step, :], reads=[src], writes=[dst])
    if DEBUG:
        dbg = {
            "yatt": em.wrap(nc.dram_tensor("dbg_yatt", [128, 16, 128 * TB], BF16, kind="ExternalOutput"), "dbg_yatt"),
            "yssd": em.wrap(nc.dram_tensor("dbg_yssd", [128, 32, 128 * TB], BF16, kind="ExternalOutput"), "dbg_yssd"),
            "merged": em.wrap(nc.dram_tensor("dbg_merged", [128, 16, 128 * TB], BF16, kind="ExternalOutput"), "dbg_merged"),
            "h": em.wrap(nc.dram_tensor("dbg_h", [128, TB, D], F32, kind="ExternalOutput"), "dbg_h"),
            "xT": em.wrap(nc.dram_tensor("dbg_xT", [128, 16, 128 * TB], BF16, kind="ExternalOutput"), "dbg_xT"),
        }

    def dump(key, buf, ti):
        if DEBUG and ti == 0:
            if len(buf.t.shape) == 3:
                em.dma("sp", dbg[key][:, :, :], buf[:, :, :], reads=[buf], writes=[dbg[key]])

    ident_bf = em.sb("ident_bf", [128, 128], BF16)
    ident_f = em.sb("ident_f", [128, 128], F32)
    ones_bf = em.sb("ones_bf", [128, 128], BF16)
    triU = em.sb("triU", [128, 128], F32)
    triLs = em.sb("triLs", [128, 128], F32)
    convp = em.sb("convp", [128, 5, 48], F32)
    normw = em.sb("normw", [128, 32], F32)
    dtb = em.sb("dtb", [128, NH], F32)
    Aneg = em.sb("Aneg", [128, NH], F32)
    dsk = em.sb("dsk", [128, NH], F32)
    EBc = em.sb("EBc", [128, AH], F32)
    wdt = em.sb("wdt", [128, 16, NH], BF16)
    flag = em.sb("flag_sb", [128, 1], F32)
    negbig = em.sb("negbig", [128, 1], F32)
    zero1 = em.sb("zero1", [128, 1], F32)
    epsc = em.sb("epsc", [128, 2], F32)
    ctail = em.sb("ctail", [128, 3, 48], F32)
    S = em.sb("S", [128, D_INNER], F32)
    KTh = em.sb("KTh", [128, AH, 512], BF16)
    Vh = em.sb("Vh", [128, 4, D], BF16)
    xtok = em.sb("xtok", [128, D], BF16)
    ring = [em.sb("ring%d" % i, [128, 16, 512], BF16) for i in range(RING)]
    ring_i = [0]

    PF = [em.ps("pf%d" % i, [128, 512], F32) for i in range(6)]
    PB = [em.ps("pb%d" % i, [128, 1024], BF16) for i in range(2)]
    pb_i = [0]

    A0 = 0
    A1 = 32 * T
    A2 = 64 * T
    A3 = 128 * T
    ARENA = A3 + (45 * 1024 + 3680 if TB >= 4 else 64 * 1024)
    arena = nc.alloc_sbuf_tensor("arena", [128, ARENA], U8)

    VIEWS = {}
    em.views = VIEWS

    def view(name, off, shape, dt):
        esz = 4 if dt == F32 else 2
        n = 1
        for s in shape[1:]:
            n *= s
        nb = n * esz
        assert off % 4 == 0 and off + nb <= ARENA, (name, off, nb, ARENA)
        ap = arena[:, off:off + nb].bitcast(dt)
        if len(shape) == 3:
            ap = ap.rearrange("p (a b) -> p a b", a=shape[1])
        elif len(shape) == 4:
            ap = ap.rearrange("p (a b c) -> p a b c", a=shape[1], b=shape[2])
        VIEWS[name] = (off, tuple(shape), "f32" if dt == F32 else "bf16")
        return em.wrap(ap, name), off + ((nb + 31) // 32) * 32

    xT, _ = view("xT", A0, [128, 16, T], BF16)
    y_attT, _ = view("y_attT", A1, [128, 16, T], BF16)
    y_ssdT, _ = view("y_ssdT", A2, [128, 32, T], BF16)

    o = A2
    EB0, o = view("EB0", o, [128, AH, 128], F32)
    EB3, o = view("EB3", o, [128, AH, 128], F32)
    EB4, o = view("EB4", o, [128, AH, 128], F32)
    QT, o = view("QT", o, [128, 4, T], BF16)
    KTc, o = view("KTc", o, [128, 4, T], BF16)
    Vc, o = view("Vc", o, [128, TB, 512], BF16)
    Pexp = []
    PTb = []
    for i in range(2):
        b_, o = view("Pexp%d" % i, o, [128, 4, 128], F32)
        Pexp.append(b_)
    for i in range(2):
        b_, o = view("PT%d" % i, o, [128, 5, 4, 128], BF16)
        PTb.append(b_)
    rden, o = view("rden", o, [128, 512], F32)
    kvst = [em.sb("kvst%d" % i, [128, 512], F32) for i in range(2)]
    ATT_END = o
    o = A3
    cst, o = view("cst", o, [128, T + 4], F32)
    cacc, o = view("cacc", o, [128, T], F32)
    if TB >= 4:
        ARENA_EXTRA = 0
    cst2, o = view("cst2", o, [128, T + 4], F32)
    cacc2, o = view("cacc2", o, [128, T], F32)
    xbcT = []
    for i in range(6):
        b_, o = view("xbcT%d" % i, o, [128, T], BF16)
        xbcT.append(b_)
    xs_tok, o = view("xs_tok", o, [128, TB, 512], BF16)
    B_tok, o = view("B_tok", o, [128, TB, 128], BF16)
    dt_t, o = view("dt_t", o, [128, TB, NH], F32)
    dt_a, o = view("dt_a", o, [128, TB, NH], F32)
    dt_v, o = view("dt_v", o, [128, TB, NH], F32)
    dtA, o = view("dtA", o, [128, TB, NH], F32)
    exb, o = view("exb", o, [128, TB, 192], F32)
    wd2, o = view("wd2", o, [128, TB, 2, NH], F32)
    Xg, o = view("Xg", o, [128, 8, 128], F32)
    Dm, o = view("Dm", o, [128, 8, 128], F32)
    cbm, o = view("cbm", o, [128, 128], F32)
    MT, o = view("MT", o, [128, 8, 128], BF16)
    xwd, o = view("xwd", o, [128, 2, 8, 64], BF16)
    Sbf, o = view("Sbf", o, [128, 512], BF16)
    y1, o = view("y1", o, [128, 512], F32)
    y2, o = view("y2", o, [128, 512], F32)
    y3, o = view("y3", o, [128, 512], F32)
    ynb, o = view("ynb", o, [128, 512], BF16)
    st8, o = view("st8", o, [128, 8], F32)
    SSD_END = o
    o = A3
    mergedT, o = view("mergedT", o, [128, 16, T], BF16)
    MERGED_END = o
    sgs, o = view("sgs", o, [128, T], F32)
    sga, o = view("sga", o, [128, T], F32)
    m1, o = view("m1", o, [128, 4, T], F32)
    m2, o = view("m2", o, [128, 4, T], F32)
    GATE_END = o
    hT, _ = view("hT", A0, [128, 16, T], BF16)
    hbuf, _ = view("hbuf", A2, [128, TB, D], F32)
    o = MERGED_END
    xres, o = view("xres", o, [128, D], F32)
    hbf, o = view("hbf", o, [128, D], BF16)
    bst, o = view("bst", o, [128, 4, 6], F32)
    mv, o = view("mv", o, [128, 8], F32)
    if TB >= 4:
        lng, _ = view("lng", A1, [128, D], F32)
        lnb, _ = view("lnb", A1 + 8192, [128, D], F32)
        LN_END = o
    else:
        lng, o = view("lng", o, [128, D], F32)
        lnb, o = view("lnb", o, [128, D], F32)
        LN_END = o
    NFH = 32
    o = LN_END if TB < 4 else A3
    uT, o = view("uT", o, [128, NFH, T], BF16)
    rtmp, o = view("rtmp", o, [128, T], F32)
    if TB >= 4:
        bst2, o = view("bst2", o, [128, 4, 6], F32)
        mv2, o = view("mv2", o, [128, 8], F32)
    else:
        bst2, mv2 = bst, mv
    MLP_END = o
    assert max(ATT_END, SSD_END, GATE_END, LN_END, MLP_END) <= ARENA, (ATT_END, SSD_END, GATE_END, LN_END, MLP_END, ARENA)

    cfg = {"TB": TBM, "T": TM, "LT": TM}
    em.sbuf_left = nc.sbuf_bytes_remaining

    cst_a, cacc_a = cst, cacc

    def next_ring():
        r = ring[ring_i[0] % RING]
        ring_i[0] += 1
        return r

    def next_pb():
        p = PB[pb_i[0] % 2]
        pb_i[0] += 1
        return p

    def wslab(wd, r0, c0, ncols, dst=None, dcol=0, nk=16):
        if dst is None:
            dst = next_ring()
        if USE_SCRATCH:
            wd = WB[wd.name]
        em.dma("pool", dst[:, 0:nk, dcol:dcol + ncols],
               wd[r0:r0 + nk * 128, c0:c0 + ncols].rearrange("(k p) c -> p k c", p=128),
               reads=[wd], writes=[dst])
        return dst

    def mm_group(out_ap, out_buf, pairs, reads):
        n = len(pairs)
        for i, (l, r) in enumerate(pairs):
            em.op("pe", lambda e, l=l, r=r, i=i: e.matmul(out_ap, lhsT=l, rhs=r, start=(i == 0), stop=(i == n - 1)),
                  reads=reads if i == 0 else (), writes=[out_buf], sig=(i == n - 1))

    def tr(out_ap, out_buf, in_ap, in_buf, idn):
        em.op("pe", lambda e: e.transpose(out_ap, in_ap, idn), reads=[in_buf, ident_bf, ident_f], writes=[out_buf])

    def act(out_ap, in_ap, func, reads, writes, **kw):
        em.op("act", lambda e: e.activation(out_ap, in_ap, func, **kw), reads=reads, writes=writes)

    def tt(eng, out_ap, a, b, op, reads, writes):
        em.op(eng, lambda e: e.tensor_tensor(out=out_ap, in0=a, in1=b, op=op), reads=reads, writes=writes)

    def ts(eng, out_ap, a, s1, s2, op0, op1, reads, writes):
        if op1 is None:
            em.op(eng, lambda e: e.tensor_scalar(out=out_ap, in0=a, scalar1=s1, scalar2=None, op0=op0), reads=reads, writes=writes)
        else:
            em.op(eng, lambda e: e.tensor_scalar(out=out_ap, in0=a, scalar1=s1, scalar2=s2, op0=op0, op1=op1), reads=reads, writes=writes)

    def stt(eng, out_ap, a, s, b, op0, op1, reads, writes):
        em.op(eng, lambda e: e.scalar_tensor_tensor(out=out_ap, in0=a, scalar=s, in1=b, op0=op0, op1=op1), reads=reads, writes=writes)

    def cp(eng, out_ap, in_ap, reads, writes):
        if eng == "act":
            act(out_ap, in_ap, AF.Copy, reads, writes)
        else:
            em.op(eng, lambda e: e.tensor_copy(out_ap, in_ap), reads=reads, writes=writes)

    def mset(eng, ap, val, buf):
        em.op(eng, lambda e: e.memset(ap, val), writes=[buf])

    def bc3(ap2, n):
        return ap2.unsqueeze(2).to_broadcast([ap2.shape[0], ap2.shape[1], n])

    def bcm(ap2, n):
        return ap2.unsqueeze(1).to_broadcast([ap2.shape[0], n, ap2.shape[1]])

    def setup():
        for t_, v in ((ident_bf, 1.0), (ident_f, 1.0), (triU, 1.0), (triLs, 1.0)):
            mset("pool", t_[:, :], v, t_)
        mset("pool", ones_bf[:, :], 1.0, ones_bf)
        mset("pool", zero1[:, :], 0.0, zero1)
        mset("pool", epsc[:, 0:1], RMS_EPS, epsc)
        mset("pool", epsc[:, 1:2], LN_EPS, epsc)
        sel = lambda t_, pat, op, base, cm: em.op(
            "pool", lambda e: e.affine_select(t_[:, :], t_[:, :], pattern=pat, compare_op=op, fill=0.0, base=base, channel_multiplier=cm),
            reads=[t_], writes=[t_])
        sel(ident_bf, [[-1, 128]], ALU.is_equal, 0, 1)
        sel(ident_f, [[-1, 128]], ALU.is_equal, 0, 1)
        sel(triU, [[1, 128]], ALU.is_ge, 0, -1)
        sel(triLs, [[-1, 128]], ALU.is_gt, 0, 1)
        mset("pool", Jm[:, :], 1.0, Jm)
        sel(Jm, [[1, 128]], ALU.is_equal, -127, 1)
        em.dma("sp", flag[:, :], flag_d[:, :], reads=[flag_d], writes=[flag])
        ts("dve", negbig[:, :], flag[:, :], -1.0, 1e30, ALU.add, ALU.mult, [flag], [negbig])
        em.dma("sp", dtb[:, :], dt_bias.t.ap().partition_broadcast(128), reads=[dt_bias], writes=[dtb])
        em.dma("sp", Aneg[:, :], a_log.t.ap().partition_broadcast(128), reads=[a_log], writes=[Aneg])
        em.dma("sp", dsk[:, :], d_skip.t.ap().partition_broadcast(128), reads=[d_skip], writes=[dsk])
        act(Aneg[:, :], Aneg[:, :], AF.Exp, [Aneg], [Aneg])
        ts("dve", Aneg[:, :], Aneg[:, :], -1.0, None, ALU.mult, None, [Aneg], [Aneg])
        wslab(w_in, 0, OFF_DT, NH, dst=wdt)
        stage, _ = view("stage", A3, [128, 8, 128], F32)
        for k in range(4):
            em.dma("sp", stage[0:48, k, :], conv_w[k, :].rearrange("(b p) -> b p", p=128), reads=[conv_w], writes=[stage])
        em.dma("sp", stage[0:48, 4, :], conv_b[0, :].rearrange("(b p) -> b p", p=128), reads=[conv_b], writes=[stage])
        em.dma("sp", stage[0:32, 5, :], ssd_norm_w[:, :], reads=[ssd_norm_w], writes=[stage])
        for k in range(5):
            tr(PF[0][:, k * 48:(k + 1) * 48], PF[0], stage[0:48, k, :], stage, ident_f[0:48, 0:48])
        cp("dve", convp[:, :, :], PF[0][:, 0:240].rearrange("p (a b) -> p a b", a=5), [PF[0]], [convp])
        tr(PF[1][:, 0:32], PF[1], stage[0:32, 5, :], stage, ident_f[0:32, 0:32])
        cp("dve", normw[:, :], PF[1][:, 0:32], [PF[1]], [normw])
        tabsb, _ = view("tabsb", A3 + 8192, [AH, 512], F32)
        em.dma("sp", tabsb[0:AH, 0:257], rel_table[:, :], reads=[rel_table], writes=[tabsb])
        cp("dve", tabsb[0:AH, 257:512], tabsb[0:AH, 256:257].to_broadcast([AH, 255]), [tabsb], [tabsb])
        em.dma("sp", ext_d[:, :], tabsb[0:AH, :], reads=[tabsb], writes=[ext_d])
        em.dma("sp", EBc[:, :], rel_table[:, 256].partition_broadcast(128), reads=[rel_table], writes=[EBc], allow_slow_non_contiguous=True)
        act(EBc[:, :], EBc[:, :], AF.Exp, [EBc], [EBc])

    Jm = em.sb("Jm", [128, 128], F32)
    toe, _o2 = view("toe", max(ATT_END, A3), [128, AH, 128], F32)

    def build_EB():
        for EB, c in ((EB4, 1), (EB3, 129)):
            src = AP(ext_d.t, c, [[1, 128], [512, AH], [1, 128]])
            em.dma("sp", toe[:, :, :], src, reads=[ext_d], writes=[toe])
            for i in range(4):
                pf = PF[i]
                mm_group(pf[:, :], pf, [(Jm[:, :], toe[:, 4 * i:4 * i + 4, :].rearrange("p h q -> p (h q)"))], [Jm, toe])
                act(EB[:, 4 * i:4 * i + 4, :].rearrange("p h q -> p (h q)"), pf[:, :], AF.Exp, [pf], [EB])
        cp("dve", EB0[:, :, :], bc3(EBc[:, :], 128), [EBc], [EB0])
        mset("dve", EB4[64:128, :, 0:64], 0.0, EB4)
        mset("dve", EB0[0:64, :, 64:128], 0.0, EB0)

    def load_xT(src, t0):
        TB = cfg["TB"]; T = cfg["T"]
        for b in range(TB):
            em.dma("pool", xtok[:, :], src[t0 + b * 128:t0 + (b + 1) * 128, :], reads=[src], writes=[xtok])
            for hlf in range(2):
                pb = next_pb()
                for k in range(8):
                    kc = hlf * 8 + k
                    tr(pb[:, k * 128:(k + 1) * 128], pb, xtok[:, kc * 128:(kc + 1) * 128], xtok, ident_bf[:, :])
                cp("act" if hlf else "dve", xT[:, hlf * 8:(hlf + 1) * 8, b * 128:(b + 1) * 128],
                   pb[:, :].rearrange("p (k t) -> p k t", k=8), [pb], [xT])

    pf_rr = [0]

    def proj_fm(slab, col0, pf_set=(0, 1)):
        TB = cfg["TB"]; T = cfg["T"]
        pf = PF[pf_set[pf_rr[0] % len(pf_set)]]
        pf_rr[0] += 1
        mm_group(pf[:, 0:T], pf, [(slab[:, kc, col0:col0 + 128], xT[:, kc, 0:T]) for kc in range(16)], [slab, xT])
        return pf

    def proj_tm(slab, b, ncols, pf, col0=0):
        TB = cfg["TB"]; T = cfg["T"]
        mm_group(pf[:, 0:ncols], pf, [(xT[:, kc, b * 128:(b + 1) * 128], slab[:, kc, col0:col0 + ncols]) for kc in range(16)], [slab, xT])
        return pf

    conv_i = [0]

    def conv_head(pf, blk):
        TB = cfg["TB"]; T = cfg["T"]
        st = ((cst_a, cacc_a), (cst2, cacc2))[conv_i[0] % 2]
        conv_i[0] += 1
        cst, cacc = st
        cp("act", cst[:, 0:3], ctail[:, :, blk], [ctail], [cst])
        cp("act", cst[:, 3:3 + T], pf[:, 0:T], [pf], [cst])
        cp("act", ctail[:, :, blk], cst[:, cfg["LT"]:cfg["LT"] + 3], [cst], [ctail])
        act(cacc[:, 0:T], pf[:, 0:T], AF.Identity, [pf, convp], [cacc], scale=convp[:, 3, blk:blk + 1], bias=convp[:, 4, blk:blk + 1])
        return st

    def conv_tail(st, blk, dst):
        TB = cfg["TB"]; T = cfg["T"]
        cst, cacc = st
        for k in (2, 1, 0):
            stt("dve", cacc[:, 0:T], cst[:, k:k + T], convp[:, k, blk:blk + 1], cacc[:, 0:T], ALU.mult, ALU.add, [cst, convp, cacc], [cacc])
        act(dst[:, 0:T], cacc[:, 0:T], AF.Silu, [cacc], [dst])

    def conv_pipeline(items):
        prev = None
        for slab, col0, blk, dst in items:
            pf = proj_fm(slab, col0)
            st = conv_head(pf, blk)
            if prev is not None:
                conv_tail(*prev)
            prev = (st, blk, dst)
        conv_tail(*prev)

    def dt_stage(L):
        TB = cfg["TB"]; T = cfg["T"]
        pf = PF[2]
        for b in range(TB):
            mm_group(pf[:, b * NH:(b + 1) * NH], pf, [(xT[:, kc, b * 128:(b + 1) * 128], wdt[:, kc, :]) for kc in range(16)], [wdt, xT])
        fl = lambda t_: t_[:, 0:TB, :].rearrange("p b h -> p (b h)")
        tt("dve", dt_t[:, 0:TB, :], pf[:, 0:TB * NH].rearrange("p (b h) -> p b h", b=TB), bcm(dtb[:, :], TB), ALU.add, [pf, dtb], [dt_t])
        act(fl(dt_a), fl(dt_t), AF.Abs, [dt_t], [dt_a])
        act(fl(dt_a), fl(dt_a), AF.Exp, [dt_a], [dt_a], scale=-1.0)
        act(fl(dt_a), fl(dt_a), AF.Ln, [dt_a], [dt_a], bias=1.0)
        stt("dve", fl(dt_v), fl(dt_t), 0.0, fl(dt_a), ALU.max, ALU.add, [dt_t, dt_a], [dt_v])
        tt("dve", dtA[:, 0:TB, :], dt_v[:, 0:TB, :], bcm(Aneg[:, :], TB), ALU.mult, [dt_v, Aneg], [dtA])
        for b in range(TB):
            pc = PF[3]
            mm_group(pc[0:L, 0:NH], pc, [(triU[0:L, 0:L], dtA[0:L, b, :])], [triU, dtA])
            mm_group(pc[0:L, NH:2 * NH], pc, [(triLs[0:L, 0:L], dtA[0:L, b, :])], [triLs, dtA])
            mm_group(pc[:, 2 * NH:3 * NH], pc, [(triU[0:L, :], dtA[0:L, b, :]), (triLs[0:L, :], dtA[0:L, b, :])], [triU, triLs, dtA])
            if L < 128:
                mset("dve", exb[:, b, :], 0.0, exb)
                act(exb[0:L, b, 0:128], pc[0:L, 0:128], AF.Exp, [pc], [exb])
                act(exb[:, b, 128:192], pc[:, 128:192], AF.Exp, [pc], [exb])
            else:
                act(exb[:, b, :], pc[:, 0:192], AF.Exp, [pc], [exb])
        tt("dve", wd2[:, 0:TB, 0, :], exb[:, 0:TB, NH:2 * NH], dt_v[:, 0:TB, :], ALU.mult, [exb, dt_v], [wd2])
        cp("act", wd2[:, 0:TB, 1, :], dt_v[:, 0:TB, :], [dt_v], [wd2])

    def ssd_projconv(g, mode, need_c=False):
        main = mode == "main"
        sx = wslab(w_in, 0, OFF_XBC + g * 512, 512)
        sbc = next_ring()
        wslab(w_in, 0, OFF_B + g * 128, 128, dst=sbc, dcol=0)
        if main or need_c:
            wslab(w_in, 0, OFF_C + g * 128, 128, dst=sbc, dcol=128)
        items = [(sx, i * 128, g * 4 + i, xbcT[i]) for i in range(4)] + [(sbc, 0, 32 + g, xbcT[4])]
        if main or need_c:
            items.append((sbc, 128, 40 + g, xbcT[5]))
        conv_pipeline(items)

    def ssd_transposes(g):
        TB = cfg["TB"]
        for b in range(TB):
            pb = next_pb()
            for i in range(4):
                tr(pb[:, i * 128:(i + 1) * 128], pb, xbcT[i][:, b * 128:(b + 1) * 128], xbcT[i], ident_bf[:, :])
            cp("act", xs_tok[:, b, :], pb[:, 0:512], [pb], [xs_tok])
        pb = next_pb()
        for b in range(TB):
            tr(pb[:, b * 128:(b + 1) * 128], pb, xbcT[4][:, b * 128:(b + 1) * 128], xbcT[4], ident_bf[:, :])
        cp("act", B_tok[:, 0:TB, :], pb[:, 0:TB * 128].rearrange("p (b n) -> p b n", b=TB), [pb], [B_tok])

    def ssd_bloop(g, mode, L):
        TB = cfg["TB"]
        main = mode == "main"
        if main:
            sz_slab = wslab(w_in, 0, g * 512, 512)
        hs = slice(g * 8, g * 8 + 8)
        Sg = S[:, g * 512:(g + 1) * 512]
        sil = (cst, cacc)
        PYD = (PF[4], PF[0])
        PYO = (PF[5], PF[1])
        if main and L < 128:
            mset("dve", ynb[L:128, :], 0.0, ynb)

        def xs3f(b):
            return xs_tok[0:L, b, :].rearrange("p (h q) -> p h q", h=8)

        def front(b):
            if main:
                cp("act", Sbf[:, :], Sg, [S], [Sbf])
            if main:
                tt("dve", xwd[0:L, :, :, :], xs3f(b).unsqueeze(1).to_broadcast([L, 2, 8, 64]),
                   wd2[0:L, b, :, hs].unsqueeze(3).to_broadcast([L, 2, 8, 64]), ALU.mult, [xs_tok, wd2], [xwd])
            else:
                tt("dve", xwd[0:L, 0, :, :], xs3f(b), bc3(wd2[0:L, b, 0, hs], 64), ALU.mult, [xs_tok, wd2], [xwd])
            if main:
                pyo = PYO[b % 2]
                mm_group(pyo[0:L, :], pyo, [(xbcT[5][:, b * 128:b * 128 + L], Sbf[:, :])], [xbcT[5], Sbf])
            psu = PF[3]
            mm_group(psu[:, :], psu, [(B_tok[0:L, b, :], xwd[0:L, 0, :, :].rearrange("p h q -> p (h q)"))], [B_tok, xwd])
            tt("dve", Sg.rearrange("p (h q) -> p h q", h=8), Sg.rearrange("p (h q) -> p h q", h=8),
               bc3(exb[:, b, 2 * NH + g * 8:2 * NH + g * 8 + 8], 64), ALU.mult, [S, exb], [S])
            tt("dve", Sg, Sg, psu[:, :], ALU.add, [S, psu], [S])

        def taila1(b):
            sb_ = sil[b % 2]
            tt("dve", Xg[0:L, :, 0:L], bc3(dtA[0:L, b, hs], L), bcm(triU[0:L, 0:L], 8), ALU.mult, [dtA, triU], [Xg])
            hu = 512 // L
            nu = 8 // hu
            pcb = PF[3]
            mm_group(pcb[0:L, 0:L], pcb, [(xbcT[4][:, b * 128:b * 128 + L], xbcT[5][:, b * 128:b * 128 + L])], [xbcT[4], xbcT[5]])
            for u in range(nu):
                pu = PF[2]
                mm_group(pu[0:L, 0:512], pu, [(triLs[0:L, 0:L], Xg[0:L, u * hu:(u + 1) * hu, 0:L])], [triLs, Xg])
                act(Dm[0:L, u * hu:(u + 1) * hu, 0:L], pu[0:L, 0:512].rearrange("p (h i) -> p h i", h=hu), AF.Exp, [pu], [Dm])

        def zs(b):
            sb_ = sil[b % 2]
            pz = proj_tm(sz_slab, b, 512, PF[2])
            act(sb_[0:L, 0:512], pz[0:L, :], AF.Silu, [pz], [sb_])

        def taila2(b):
            pcb = PF[3]
            tt("dve", cbm[0:L, 0:L], pcb[0:L, 0:L], triU[0:L, 0:L], ALU.mult, [pcb, triU], [cbm])
            tt("dve", MT[0:L, :, 0:L], Dm[0:L, :, 0:L], bcm(cbm[0:L, 0:L], 8), ALU.mult, [Dm, cbm], [MT])
            pyd = PYD[b % 2]
            for h in range(8):
                mm_group(pyd[0:L, h * 64:(h + 1) * 64], pyd, [(MT[0:L, h, 0:L], xwd[0:L, 1, h, :])], [MT, xwd])

        def tailb(b):
            pyo = PYO[b % 2]
            pyd = PYD[b % 2]
            sb_ = sil[b % 2]
            tt("dve", y1[0:L, :].rearrange("p (h q) -> p h q", h=8), pyo[0:L, :].rearrange("p (h q) -> p h q", h=8),
               bc3(exb[0:L, b, hs], 64), ALU.mult, [pyo, exb], [y1])
            tt("dve", y3[0:L, :].rearrange("p (h q) -> p h q", h=8), xs3f(b), bc3(dsk[0:L, hs], 64), ALU.mult, [xs_tok, dsk], [y3])
            tt("dve", y1[0:L, :], y1[0:L, :], y3[0:L, :], ALU.add, [y1, y3], [y1])
            tt("dve", y2[0:L, :], pyd[0:L, :], y1[0:L, :], ALU.add, [pyd, y1], [y2])
            tt("dve", y2[0:L, :], y2[0:L, :], sb_[0:L, 0:512], ALU.mult, [y2, sb_], [y2])
            act(y3[0:L, :], y2[0:L, :], AF.Square, [y2], [y3, st8], accum_out=st8[0:L, 0:1])
            act(st8[0:L, 2:3], st8[0:L, 0:1], AF.Ln, [st8, epsc], [st8], scale=1.0 / 512, bias=epsc[0:L, 0:1])
            act(st8[0:L, 3:4], st8[0:L, 2:3], AF.Exp, [st8], [st8], scale=-0.5)
            act(ynb[0:L, :], y2[0:L, :], AF.Copy, [y2, st8], [ynb], scale=st8[0:L, 3:4])
            pb2 = next_pb()
            for i in range(4):
                tr(pb2[:, i * 128:(i + 1) * 128], pb2, ynb[:, i * 128:(i + 1) * 128], ynb, ident_bf[:, :])
            return pb2

        def tailb2(b, pb2):
            tt("dve", y_ssdT[:, g * 4:g * 4 + 4, b * 128:(b + 1) * 128], pb2[:, 0:512].rearrange("p (a t) -> p a t", a=4),
               bc3(normw[:, g * 4:g * 4 + 4], 128), ALU.mult, [pb2, normw], [y_ssdT])

        if not main:
            for b in range(TB):
                front(b)
            return
        front(0)
        taila1(0)
        taila2(0)
        zs(0)
        for b in range(TB):
            if b + 1 < TB:
                front(b + 1)
                taila1(b + 1)
            pb2 = tailb(b)
            if b + 1 < TB:
                taila2(b + 1)
            tailb2(b, pb2)
            if b + 1 < TB:
                zs(b + 1)

    def ssd_group(g, mode, L, need_c=False):
        ssd_projconv(g, mode, need_c)
        ssd_transposes(g)
        ssd_bloop(g, mode, L)

    def attn_kv_proj(hg, want_q, kout=None, vout=None, orow=0):
        TB = cfg["TB"]; T = cfg["T"]
        sk = wslab(w_in, 0, OFF_K + hg * 512, 512)
        for i in range(4):
            pf = proj_fm(sk, i * 128, pf_set=(0, 1))
            cp("dve", KTc[:, i, 0:T], pf[:, 0:T], [pf], [KTc])
        if kout is not None:
            for b in range(TB):
                pf = proj_tm(sk, b, 512, PF[2 + (b % 2)])
                st_ = kvst[b % 2]
                cp("act", st_[:, :], pf[:, :], [pf], [st_])
                em.dma("sp", kout[orow + b * 128:orow + (b + 1) * 128, hg * 512:(hg + 1) * 512], st_[:, :], reads=[st_], writes=[kout])
        sv = wslab(w_in, 0, OFF_V + hg * 512, 512)
        for b in range(TB):
            pf = proj_tm(sv, b, 512, PF[2 + (b % 2)])
            if vout is not None:
                st_ = kvst[b % 2]
                cp("act", st_[:, :], pf[:, :], [pf], [st_])
                cp("dve", Vc[:, b, :], st_[:, :], [st_], [Vc])
                em.dma("sp", vout[orow + b * 128:orow + (b + 1) * 128, hg * 512:(hg + 1) * 512], st_[:, :], reads=[st_], writes=[vout])
            else:
                cp("dve", Vc[:, b, :], pf[:, :], [pf], [Vc])
        if want_q:
            sq = wslab(w_in, 0, OFF_Q + hg * 512, 512)
            for i in range(4):
                pf = proj_fm(sq, i * 128, pf_set=(0, 1))
                act(QT[:, i, 0:T], pf[:, 0:T], AF.Copy, [pf], [QT], scale=QSCALE)

    def hist_update(hg):
        TB = cfg["TB"]; T = cfg["T"]
        hsl = slice(hg * 4, hg * 4 + 4)
        vsl = slice(hg * 512, (hg + 1) * 512)
        if TB < 4:
            keep = 4 - TB
            for j in range(keep):
                cp("act", KTh[:, hsl, j * 128:(j + 1) * 128], KTh[:, hsl, (j + TB) * 128:(j + TB + 1) * 128], [KTh], [KTh])
                cp("act", Vh[:, j, vsl], Vh[:, j + TB, vsl], [Vh], [Vh])
            cp("act", KTh[:, hsl, keep * 128:512], KTc[:, :, 0:T], [KTc], [KTh])
            cp("act", Vh[:, keep:4, vsl], Vc[:, 0:TB, :], [Vc], [Vh])
        else:
            cp("act", KTh[:, hsl, :], KTc[:, :, T - 512:T], [KTc], [KTh])
            cp("act", Vh[:, :, vsl], Vc[:, TB - 4:TB, :], [Vc], [Vh])

    def attention(hist_flagged, kout=None, vout=None, orow=0, slide=True, stop=None):
        TB = cfg["TB"]; T = cfg["T"]
        pi = [0]
        for hg in range(4):
            if stop == 631:
                attn_kv_proj(hg, True, None, None, orow)
                return True
            if stop == 632:
                attn_kv_proj(hg, False, kout, vout, orow)
                return True
            attn_kv_proj(hg, True, kout, vout, orow)
            if stop == 63:
                return True
            for b in range(TB):
                PT = PTb[pi[0] % 2]
                for kb in range(5):
                    ob = b + kb - 4
                    psc = PF[2 + (kb % 2)]
                    for hl in range(4):
                        if ob < 0:
                            kt = KTh[:, hg * 4 + hl, (ob + 4) * 128:(ob + 5) * 128]
                            kbuf = KTh
                        else:
                            kt = KTc[:, hl, ob * 128:(ob + 1) * 128]
                            kbuf = KTc
                        mm_group(psc[:, hl * 128:(hl + 1) * 128], psc, [(kt, QT[:, hl, b * 128:(b + 1) * 128])], [kbuf, QT])
                    pe_ = Pexp[kb % 2]
                    bias = negbig[:, 0:1] if (hist_flagged is not None and hist_flagged + ob * 128 < 0) else zero1[:, 0:1]
                    act(pe_[:, :, :].rearrange("p h q -> p (h q)"), psc[:, :], AF.Exp, [psc, negbig, zero1], [pe_], bias=bias)
                    EBk = (EB0, None, None, EB3, EB4)[kb]
                    if EBk is None:
                        ebap = bc3(EBc[:, hg * 4:hg * 4 + 4], 128)
                        ebuf = EBc
                    else:
                        ebap = EBk[:, hg * 4:hg * 4 + 4, :]
                        ebuf = EBk
                    tt("dve", PT[:, kb, :, :], pe_[:, :, :], ebap, ALU.mult, [pe_, ebuf], [PT])
                pden = PF[4]
                mm_group(pden[:, :], pden, [(ones_bf[:, :], PT[:, kb, :, :].rearrange("p h q -> p (h q)")) for kb in range(5)], [ones_bf, PT])
                po = PF[5]
                for hl in range(4):
                    prs = []
                    for kb in range(5):
                        ob = b + kb - 4
                        if ob < 0:
                            vv = Vh[:, ob + 4, (hg * 4 + hl) * 128:(hg * 4 + hl + 1) * 128]
                        else:
                            vv = Vc[:, ob, hl * 128:(hl + 1) * 128]
                        prs.append((vv, PT[:, kb, hl, :]))
                    mm_group(po[:, hl * 128:(hl + 1) * 128], po, prs, [Vh, Vc, PT])
                act(rden[:, :], pden[:, :], AF.Ln, [pden], [rden])
                act(rden[:, :], rden[:, :], AF.Exp, [rden], [rden], scale=-1.0)
                tt("dve", y_attT[:, hg * 4:hg * 4 + 4, b * 128:(b + 1) * 128], po[:, :].rearrange("p (h q) -> p h q", h=4),
                   rden[:, :].rearrange("p (h q) -> p h q", h=4), ALU.mult, [po, rden], [y_attT])
                pi[0] += 1
                if stop == 64:
                    return True
            if slide:
                hist_update(hg)
            if stop == 65:
                return True

    def gates_outproj():
        TB = cfg["TB"]; T = cfg["T"]
        for cg in range(4):
            c0 = cg * 512
            s_so0 = wslab(w_ssd_out, 0, c0, 512)
            for obl in range(4):
                cs = slice(obl * 128, (obl + 1) * 128)
                pa = PF[obl % 2]
                mm_group(pa[:, 0:T], pa, [(s_so0[:, kc, cs], y_ssdT[:, kc, 0:T]) for kc in range(16)], [s_so0, y_ssdT])
                cp("act", m1[:, obl, 0:T], pa[:, 0:T], [pa], [m1])
            s_so1 = wslab(w_ssd_out, 2048, c0, 512)
            for obl in range(4):
                cs = slice(obl * 128, (obl + 1) * 128)
                pa = PF[4 + obl % 2]
                mm_group(pa[:, 0:T], pa, [(s_so1[:, kc, cs], y_ssdT[:, 16 + kc, 0:T]) for kc in range(16)], [s_so1, y_ssdT])
                tt("dve", m1[:, obl, 0:T], m1[:, obl, 0:T], pa[:, 0:T], ALU.add, [m1, pa], [m1])
            s_ao = wslab(w_att_out, 0, c0, 512)
            for obl in range(4):
                cs = slice(obl * 128, (obl + 1) * 128)
                pbk = PF[2 + obl % 2]
                mm_group(pbk[:, 0:T], pbk, [(s_ao[:, kc, cs], y_attT[:, kc, 0:T]) for kc in range(16)], [s_ao, y_attT])
                cp("dve", m2[:, obl, 0:T], pbk[:, 0:T], [pbk], [m2])
            s_gs = wslab(w_in, 0, OFF_GS + c0, 512)
            for obl in range(4):
                cs = slice(obl * 128, (obl + 1) * 128)
                pg = PF[4 + obl % 2]
                mm_group(pg[:, 0:T], pg, [(s_gs[:, kc, cs], xT[:, kc, 0:T]) for kc in range(16)], [s_gs, xT])
                act(sgs[:, 0:T], pg[:, 0:T], AF.Sigmoid, [pg], [sgs])
                tt("dve", m1[:, obl, 0:T], m1[:, obl, 0:T], sgs[:, 0:T], ALU.mult, [m1, sgs], [m1])
            s_ga = wslab(w_in, 0, OFF_GA + c0, 512)
            for obl in range(4):
                ob = cg * 4 + obl
                cs = slice(obl * 128, (obl + 1) * 128)
                pg2 = PF[obl % 2]
                mm_group(pg2[:, 0:T], pg2, [(s_ga[:, kc, cs], xT[:, kc, 0:T]) for kc in range(16)], [s_ga, xT])
                act(sga[:, 0:T], pg2[:, 0:T], AF.Sigmoid, [pg2], [sga])
                tt("dve", m2[:, obl, 0:T], m2[:, obl, 0:T], sga[:, 0:T], ALU.mult, [m2, sga], [m2])
                tt("dve", mergedT[:, ob, 0:T], m1[:, obl, 0:T], m2[:, obl, 0:T], ALU.add, [m1, m2], [mergedT])

    def layer_norm_rows(hrow, bst_, mv_):
        for c in range(4):
            em.op("dve", lambda e, c=c: e.bn_stats(bst_[:, c, :], hrow[:, c * 512:(c + 1) * 512]), reads=[hbuf], writes=[bst_])
        em.op("dve", lambda e: e.bn_aggr(mv_[:, 0:2], bst_[:, :, :]), reads=[bst_], writes=[mv_])
        act(mv_[:, 3:4], mv_[:, 1:2], AF.Ln, [mv_, epsc], [mv_], bias=epsc[:, 1:2])
        act(mv_[:, 4:5], mv_[:, 3:4], AF.Exp, [mv_], [mv_], scale=-0.5)
        ts("dve", hrow, hrow, mv_[:, 0:1], mv_[:, 4:5], ALU.subtract, ALU.mult, [hbuf, mv_], [hbuf])
        tt("dve", hrow, hrow, lng[:, :], ALU.mult, [hbuf, lng], [hbuf])
        tt("dve", hrow, hrow, lnb[:, :], ALU.add, [hbuf, lnb], [hbuf])

    def wout_ln1(src, t0):
        TB = cfg["TB"]; T = cfg["T"]
        em.dma("sp", lng[:, :], ln1_g.t.ap().partition_broadcast(128), reads=[ln1_g], writes=[lng])
        em.dma("sp", lnb[:, :], ln1_b.t.ap().partition_broadcast(128), reads=[ln1_b], writes=[lnb])
        for cg in range(4):
            s_wo = wslab(w_out, 0, cg * 512, 512)
            for b in range(TB):
                pf = PF[(cg * TB + b) % 4]
                mm_group(pf[:, :], pf, [(mergedT[:, kc, b * 128:(b + 1) * 128], s_wo[:, kc, :]) for kc in range(16)], [mergedT, s_wo])
                cp("act", hbuf[:, b, cg * 512:(cg + 1) * 512], pf[:, :], [pf], [hbuf])
        for b in range(TB):
            em.dma("sp", xres[:, :], src[t0 + b * 128:t0 + (b + 1) * 128, :], reads=[src], writes=[xres])
            hrow = hbuf[:, b, :]
            stt("dve", hrow, xres[:, :], ALPHA, hrow, ALU.mult, ALU.add, [xres, hbuf], [hbuf])
            layer_norm_rows(hrow, bst, mv)
            cp("act", hbf[:, :], hrow, [hbuf], [hbf])
            for hlf in range(2):
                pb = next_pb()
                for k in range(8):
                    kc = hlf * 8 + k
                    tr(pb[:, k * 128:(k + 1) * 128], pb, hbf[:, kc * 128:(kc + 1) * 128], hbf, ident_bf[:, :])
                cp("act" if hlf else "dve", hT[:, hlf * 8:(hlf + 1) * 8, b * 128:(b + 1) * 128],
                   pb[:, :].rearrange("p (k t) -> p k t", k=8), [pb], [hT])

    def mlp_ln2(ydst, t0, nrows):
        TB = cfg["TB"]; T = cfg["T"]
        for hf in range(2):
            for s4 in range(8):
                s_up = wslab(w_up, 0, (hf * 8 + s4) * 512, 512)
                for fl_ in range(4):
                    pf = proj_fm_h(s_up, fl_ * 128)
                    act(rtmp[:, 0:T], pf[:, 0:T], AF.Relu, [pf], [rtmp])
                    tt("dve", uT[:, s4 * 4 + fl_, 0:T], rtmp[:, 0:T], rtmp[:, 0:T], ALU.mult, [rtmp], [uT])
            if hf == 1:
                em.dma("sp", lng[:, :], ln2_g.t.ap().partition_broadcast(128), reads=[ln2_g], writes=[lng])
                em.dma("sp", lnb[:, :], ln2_b.t.ap().partition_broadcast(128), reads=[ln2_b], writes=[lnb])
            for cg in range(4):
                for q2 in range(2):
                    s_d = wslab(w_down, (hf * 32 + q2 * 16) * 128, cg * 512, 512)
                    for b in range(TB):
                        pf = PF[2 + ((q2 * TB + b) % 4)]
                        mm_group(pf[:, :], pf, [(uT[:, q2 * 16 + kc, b * 128:(b + 1) * 128], s_d[:, kc, :]) for kc in range(16)], [uT, s_d])
                        hseg = hbuf[:, b, cg * 512:(cg + 1) * 512]
                        if hf == 0 and q2 == 0:
                            stt("dve", hseg, hseg, ALPHA, pf[:, :], ALU.mult, ALU.add, [hbuf, pf], [hbuf])
                        else:
                            tt("dve", hseg, hseg, pf[:, :], ALU.add, [hbuf, pf], [hbuf])
        for b in range(TB):
            hrow = hbuf[:, b, :]
            layer_norm_rows(hrow, bst2, mv2)
            r = min(128, nrows - b * 128)
            if r > 0:
                em.dma("sp", ydst[t0 + b * 128:t0 + b * 128 + r, :], hbuf[0:r, b, :], reads=[hbuf], writes=[ydst])

    def proj_fm_h(slab, col0):
        TB = cfg["TB"]; T = cfg["T"]
        pf = PF[pf_rr[0] % 2]
        pf_rr[0] += 1
        mm_group(pf[:, 0:T], pf, [(slab[:, kc, col0:col0 + 128], hT[:, kc, 0:T]) for kc in range(16)], [slab, hT])
        return pf

    def out_state(dst):
        for c in range(8):
            pf = PF[c % 4]
            for i in range(4):
                blk = c * 4 + i
                tr(pf[:, i * 128:(i + 1) * 128], pf, S[:, blk * 128:(blk + 1) * 128], S, ident_f[:, :])
            st_ = kvst[c % 2]
            cp("act" if c % 2 else "dve", st_[:, :], pf[:, :], [pf], [st_])
            em.dma("sp", dst[c * 512:(c + 1) * 512, :].rearrange("(b p) n -> p b n", p=128), st_[:, :].rearrange("p (b n) -> p b n", b=4),
                   reads=[st_], writes=[dst])

    def out_conv(dst):
        for t in range(3):
            pf = PF[t]
            tr(pf[0:48, 0:128], pf, ctail[:, t, :], ctail, ident_f[:, :])
            st_ = kvst[t % 2]
            cp("dve", st_[0:48, 0:128], pf[0:48, 0:128], [pf], [st_])
            em.dma("sp", dst[t, :].rearrange("(b p) -> b p", p=128), st_[0:48, 0:128], reads=[st_], writes=[dst])

    def finish():
        em.barrier(engines=("pe", "act", "dve", "pool", "sp"))
        em.emit()
        return nc, em

    if USE_SCRATCH:
        convert_weight("w_in", 8)
    setup()
    if STOP == 1:
        return finish()
    mset("dve", ctail[:, :, :], 0.0, ctail)
    mset("dve", S[:, :], 0.0, S)
    mset("pool", KTh[:, :, :], 0.0, KTh)
    mset("pool", Vh[:, :, :], 0.0, Vh)
    em.barrier()

    npre = NPRE // T
    for ti in range(npre):
        load_xT(xpre, ti * T)
        if STOP == 2:
            return finish()
        dt_stage(128)
        if STOP == 3:
            return finish()
        nc_ = (ti == npre - 1)
        ssd_projconv(0, "pre", nc_)
        ssd_transposes(0)
        for g in range(NG):
            if g + 1 < NG:
                ssd_projconv(g + 1, "pre", nc_)
            ssd_bloop(g, "pre", 128)
            if g + 1 < NG:
                ssd_transposes(g + 1)
        if STOP == 5:
            return finish()
        if (npre - ti) * T <= 512:
            em.barrier()
            for hg in range(4):
                attn_kv_proj(hg, False)
                hist_update(hg)
        if ti == 0 and USE_SCRATCH:
            convert_weight("w_up", 4)
            convert_weight("w_down", 4)
            convert_weight("w_ssd_out", 2)
            convert_weight("w_att_out", 1)
            convert_weight("w_out", 1)
        em.barrier()
    if STOP == 6:
        return finish()
    ts("dve", S[:, :], S[:, :], flag[:, 0:1], None, ALU.mult, None, [S, flag], [S])

    nmain = NMAIN // T
    for ti in range(nmain):
        t0 = ti * T
        load_xT(xmain, t0)
        if STOP == 61:
            return finish()
        build_EB()
        if STOP == 62:
            return finish()
        want = (NMAIN - t0) <= 512
        if attention(hist_flagged=t0, kout=k_main if want else None, vout=v_main if want else None,
                     orow=(t0 - (NMAIN - 512)) if want else 0, stop=STOP):
            return finish()
        if STOP == 7:
            return finish()
        dump("yatt", y_attT, ti)
        dump("xT", xT, ti)
        em.barrier()
        dt_stage(128)
        for g in range(NG):
            ssd_group(g, "main", 128)
        if STOP == 8:
            return finish()
        dump("yssd", y_ssdT, ti)
        em.barrier()
        gates_outproj()
        if STOP == 9:
            return finish()
        dump("merged", mergedT, ti)
        em.barrier()
        wout_ln1(xmain, t0)
        if STOP == 10:
            return finish()
        dump("h", hbuf, ti)
        em.barrier()
        mlp_ln2(y_main, t0, T)
        em.barrier()
        if STOP == 11:
            return finish()
    out_state(ssd_main)
    out_conv(conv_main)
    em.barrier()
    if STOP == 12:
        return finish()

    cfg["TB"] = 1
    cfg["T"] = 128
    cfg["LT"] = 64
    stage2, _ = view("stage2", A3, [128, 8, 128], F32)
    for t in range(3):
        em.dma("sp", stage2[0:48, t, :], state_conv[t, :].rearrange("(b p) -> b p", p=128), reads=[state_conv], writes=[stage2])
    for t in range(3):
        tr(PF[0][:, t * 48:(t + 1) * 48], PF[0], stage2[0:48, t, :], stage2, ident_f[0:48, 0:48])
    cp("dve", ctail[:, :, :], PF[0][:, 0:144].rearrange("p (a b) -> p a b", a=3), [PF[0]], [ctail])
    for c in range(8):
        st_ = kvst[c % 2]
        em.dma("sp", st_[:, :].rearrange("p (b n) -> p b n", b=4), state_ssd[c * 512:(c + 1) * 512, :].rearrange("(b p) n -> p b n", p=128),
               reads=[state_ssd], writes=[st_])
        pf = PF[c % 4]
        for i in range(4):
            tr(pf[:, i * 128:(i + 1) * 128], pf, st_[:, i * 128:(i + 1) * 128], st_, ident_f[:, :])
        cp("act" if c % 2 else "dve", S[:, c * 512:(c + 1) * 512], pf[:, :], [pf], [S])
    for j in range(4):
        em.dma("pool", Vh[:, j, :], cache_v[j * 128:(j + 1) * 128, :], reads=[cache_v], writes=[Vh])
    for j in range(4):
        em.dma("pool", xtok[:, :], cache_k[j * 128:(j + 1) * 128, :], reads=[cache_k], writes=[xtok])
        for hlf in range(2):
            pb = next_pb()
            for k in range(8):
                hh = hlf * 8 + k
                tr(pb[:, k * 128:(k + 1) * 128], pb, xtok[:, hh * 128:(hh + 1) * 128], xtok, ident_bf[:, :])
            cp("act" if hlf else "dve", KTh[:, hlf * 8:(hlf + 1) * 8, j * 128:(j + 1) * 128],
               pb[:, :].rearrange("p (k t) -> p k t", k=8), [pb], [KTh])
    em.barrier()
    load_xT(xsamp, 0)
    em.barrier()
    build_EB()
    em.barrier()
    attention(hist_flagged=None, kout=k_samp, vout=v_samp, orow=0, slide=False)
    em.barrier()
    dt_stage(64)
    for g in range(NG):
        ssd_group(g, "main", 64)
    em.barrier()
    gates_outproj()
    em.barrier()
    wout_ln1(xsamp, 0)
    em.barrier()
    mlp_ln2(y_samp, 0, 128)
    em.barrier()
    out_state(ssd_samp)
    out_conv(conv_samp)
    outs = (y_main, y_samp, k_main, v_main, conv_main, ssd_main, k_samp, v_samp, conv_samp, ssd_samp)
    em.barrier(engines=("pe", "act", "dve", "pool", "sp"))
    em.emit()
    return nc, em


TB_DEFAULT = 4
_PROG = {}


def _get_prog(npre, nmain, tb):
    key = (npre, nmain, tb)
    if key not in _PROG:
        _PROG[key] = build_program(npre, nmain, tb)
    return _PROG[key]


def _core_inputs(c, half_len, x_prompt, x_sample, cache_k, cache_v, state_conv, state_ssd, weights):
    seq, half = c // 2, c % 2
    f32 = np.float32
    xm = np.ascontiguousarray(x_prompt[seq, half * half_len:(half + 1) * half_len], dtype=f32)
    xp = np.ascontiguousarray(x_prompt[seq, 0:half_len], dtype=f32) if half else np.zeros((half_len, D), f32)
    xs = np.zeros((128, D), f32)
    xs[:DEC_SEQ] = x_sample[c]
    m = {
        "xpre": xp, "xmain": xm, "xsamp": xs,
        "flag": np.full((128, 1), float(half), f32),
        "cache_k": np.ascontiguousarray(cache_k[0, c].reshape(512, D), dtype=f32),
        "cache_v": np.ascontiguousarray(cache_v[0, c].reshape(512, D), dtype=f32),
        "state_conv": np.ascontiguousarray(state_conv[0, c], dtype=f32),
        "state_ssd": np.ascontiguousarray(state_ssd[0, c].reshape(D_INNER, NST), dtype=f32),
    }
    m.update(weights)
    return m


def _weights(w_in, conv_w, conv_b, dt_bias, a_log, d_skip, ssd_norm_w, rel_table, w_ssd_out, w_att_out,
             w_out, ln1_g, ln1_b, w_up, w_down, ln2_g, ln2_b):
    f = lambda a: np.ascontiguousarray(np.asarray(a), dtype=np.float32)
    return {
        "w_in": f(w_in[0]), "conv_w": f(conv_w[0]), "conv_b": f(conv_b[0]).reshape(1, CONV_DIM),
        "dt_bias": f(dt_bias[0]), "a_log": f(a_log[0]), "d_skip": f(d_skip[0]),
        "ssd_norm_w": f(ssd_norm_w[0]).reshape(32, 128), "rel_table": f(rel_table[0]),
        "w_ssd_out": f(w_ssd_out[0]), "w_att_out": f(w_att_out[0]), "w_out": f(w_out[0]),
        "ln1_g": f(ln1_g[0]), "ln1_b": f(ln1_b[0]), "w_up": f(w_up[0]), "w_down": f(w_down[0]),
        "ln2_g": f(ln2_g[0]), "ln2_b": f(ln2_b[0]),
    }


def kernel(x_prompt, x_sample, cache_k, cache_v, state_conv, state_ssd, w_in, conv_w, conv_b,
           dt_bias, a_log, d_skip, ssd_norm_w, rel_table, w_ssd_out, w_att_out, w_out,
           ln1_g, ln1_b, w_up, w_down, ln2_g, ln2_b):
    x_prompt = np.asarray(x_prompt)
    x_sample = np.asarray(x_sample)
    cache_k = np.asarray(cache_k)
    cache_v = np.asarray(cache_v)
    state_conv = np.asarray(state_conv)
    state_ssd = np.asarray(state_ssd)
    half_len = SEQ // 2
    nc, _ = _get_prog(half_len, half_len, TB_DEFAULT)
    wts = _weights(w_in, conv_w, conv_b, dt_bias, a_log, d_skip, ssd_norm_w, rel_table, w_ssd_out, w_att_out,
                   w_out, ln1_g, ln1_b, w_up, w_down, ln2_g, ln2_b)
    in_maps = [_core_inputs(c, half_len, x_prompt, x_sample, cache_k, cache_v, state_conv, state_ssd, wts)
               for c in range(8)]
    res = run_bass_kernel_spmd(nc, in_maps, core_ids=list(range(8))).results
    f32 = np.float32
    y_prompt = np.empty((BATCH, SEQ, D), f32)
    y_sample = np.empty((DEC_BATCH, DEC_SEQ, D), f32)
    nkp = np.empty((1, BATCH, 512, AH, 128), f32)
    nvp = np.empty((1, BATCH, 512, AH, 128), f32)
    ncp = np.empty((1, BATCH, 3, CONV_DIM), f32)
    nsp = np.empty((1, BATCH, NH, 64, NST), f32)
    nks = np.empty((1, DEC_BATCH, DEC_SEQ, AH, 128), f32)
    nvs = np.empty((1, DEC_BATCH, DEC_SEQ, AH, 128), f32)
    ncs = np.empty((1, DEC_BATCH, 3, CONV_DIM), f32)
    nss = np.empty((1, DEC_BATCH, NH, 64, NST), f32)
    for c in range(8):
        seq, half = c // 2, c % 2
        r = res[c]
        y_prompt[seq, half * half_len:(half + 1) * half_len] = r["y_main"]
        y_sample[c] = r["y_samp"][:DEC_SEQ]
        if half:
            nkp[0, seq] = r["k_main"].reshape(512, AH, 128)
            nvp[0, seq] = r["v_main"].reshape(512, AH, 128)
            ncp[0, seq] = r["conv_main"]
            nsp[0, seq] = r["ssd_main"].reshape(NH, 64, NST)
        nks[0, c] = r["k_samp"][:DEC_SEQ].reshape(DEC_SEQ, AH, 128)
        nvs[0, c] = r["v_samp"][:DEC_SEQ].reshape(DEC_SEQ, AH, 128)
        ncs[0, c] = r["conv_samp"]
        nss[0, c] = r["ssd_samp"].reshape(NH, 64, NST)
    return (y_prompt, y_sample, nkp, nvp, ncp, nsp, nks, nvs, ncs, nss)
```

```python
import numpy as np
import concourse.bass as bass
import concourse.mybir as mybir
from concourse.bass_utils import run_bass_kernel_spmd
from concourse.ap import AP

F32 = mybir.dt.float32
BF16 = mybir.dt.bfloat16
U8 = mybir.dt.uint8
AF = mybir.ActivationFunctionType
ALU = mybir.AluOpType

D = 2048
SEQ = 4096
BATCH = 4
DEC_BATCH = 8
DEC_SEQ = 64
D_INNER = 4096
NH = 64
NG = 8
NST = 128
CONV_DIM = 6144
AH = 16
D_FF = 8192
OFF_XBC = 4096
OFF_B = OFF_XBC + 4096
OFF_C = OFF_B + 1024
OFF_DT = OFF_XBC + CONV_DIM
OFF_Q = OFF_DT + NH
OFF_K = OFF_Q + 2048
OFF_V = OFF_K + 2048
OFF_GS = OFF_V + 2048
OFF_GA = OFF_GS + 2048
D_IN_PROJ = OFF_GA + 2048
ALPHA = (2 * 1) ** 0.25
LN_EPS = 1e-5
RMS_EPS = 1e-5
QSCALE = 128 ** -0.5


class Buf:
    __slots__ = ("t", "name", "lw", "rd", "sem", "cnt", "psum")

    def __init__(self, t, name):
        self.t = t
        self.name = name
        self.psum = False
        self.lw = None
        self.rd = {}
        self.sem = None
        self.cnt = 0

    def __getitem__(self, k):
        return self.t[k]


class Em:
    ENG = ("pe", "act", "dve", "pool", "sp")

    def __init__(self, nc):
        self.nc = nc
        self.q = {e: [] for e in self.ENG}
        self.sem = {e: nc.alloc_semaphore("s_" + e) for e in self.ENG}
        self.cnt = {e: 0 for e in self.ENG}
        self.pend = {e: False for e in self.ENG}
        self.seen = {e: {} for e in self.ENG}
        self.bufs = []
        self.nins = 0

    def sb(self, name, shape, dt):
        b = Buf(self.nc.alloc_sbuf_tensor(name, list(shape), dt), name)
        self.bufs.append(b)
        return b

    def ps(self, name, shape, dt=F32):
        b = Buf(self.nc.alloc_psum_tensor(name, list(shape), dt), name)
        b.psum = True
        self.bufs.append(b)
        return b

    def dram(self, name, shape, dt, kind="Internal"):
        b = Buf(self.nc.dram_tensor(name, list(shape), dt, kind=kind), name)
        self.bufs.append(b)
        return b

    def wrap(self, t, name):
        b = Buf(t, name)
        self.bufs.append(b)
        return b

    def _need(self, eng, tok, waits):
        if tok is None:
            return
        sem, val, _ = tok
        if self.seen[eng].get(sem.num, 0) >= val:
            return
        if sem.num in waits:
            val = max(val, waits[sem.num][1])
        waits[sem.num] = (sem, val)

    def _flush(self, eng, waits):
        out = []
        for num, (sem, val) in waits.items():
            self.seen[eng][num] = val
            out.append((sem, val))
        return out

    def _deps(self, eng, reads, writes):
        waits = {}
        for b in reads:
            self._need(eng, b.lw, waits)
            if b.psum:
                for e2, tok in b.rd.items():
                    if e2 != eng:
                        self._need(eng, tok, waits)
        for b in writes:
            if b.lw is not None and not (eng == "pe" and b.lw[2] == "pe"):
                self._need(eng, b.lw, waits)
            for e2, tok in b.rd.items():
                if not (eng == "pe" and e2 == "pe"):
                    self._need(eng, tok, waits)
        return self._flush(eng, waits)

    def op(self, eng, fn, reads=(), writes=(), sig=True):
        waits = self._deps(eng, reads, writes)
        if sig:
            self.cnt[eng] += 1
            tick = self.cnt[eng]
            self.pend[eng] = False
        else:
            tick = self.cnt[eng] + 1
            self.pend[eng] = True
        tok = (self.sem[eng], tick, eng)
        self.q[eng].append((waits, fn, (self.sem[eng], 1) if sig else None))
        self.nins += 1
        for b in reads:
            b.rd[eng] = tok
        for b in writes:
            b.lw = tok
            b.rd = {}
        return tok

    def dma(self, qeng, out_ap, in_ap, reads=(), writes=(), **kw):
        waits = self._deps(qeng, reads, writes)
        owner = writes[0] if writes else reads[0]
        if owner.sem is None:
            owner.sem = self.nc.alloc_semaphore("d%d_%s" % (len(self.bufs), owner.name) + "_%d" % id(owner))
        owner.cnt += 16
        tok = (owner.sem, owner.cnt, "dma")

        def fn(e, out_ap=out_ap, in_ap=in_ap, kw=kw):
            return e.dma_start(out=out_ap, in_=in_ap, **kw)
        self.q[qeng].append((waits, fn, (owner.sem, 16)))
        self.nins += 1
        for b in reads:
            b.rd[("dma", tok[0].num)] = tok
        for b in writes:
            b.lw = tok
            b.rd = {}
        return tok

    def wait_all(self, eng, toks):
        waits = {}
        for t in toks:
            self._need(eng, t, waits)
        self.q[eng].append((self._flush(eng, waits), None, None))

    def barrier(self, engines=("pe", "act", "dve", "sp")):
        toks = []
        for e in self.ENG:
            assert not self.pend[e], "barrier with pending unsignalled op on " + e
            if self.cnt[e] > 0:
                toks.append((self.sem[e], self.cnt[e], e))
        for b in self.bufs:
            if b.sem is not None and b.cnt > 0:
                toks.append((b.sem, b.cnt, "dma"))
        for e in engines:
            self.wait_all(e, toks)

    def emit(self):
        nc = self.nc
        q = self.q

        def run(e, lst):
            for waits, fn, inc in lst:
                for sem, val in waits:
                    e.wait_ge(sem, val)
                if fn is not None:
                    ins = fn(e)
                    if inc is not None:
                        ins.then_inc(inc[0], inc[1])

        with nc.Block() as block:
            @block.tensor
            def _(e):
                run(e, q["pe"])

            @block.scalar
            def _(e):
                run(e, q["act"])

            @block.vector
            def _(e):
                run(e, q["dve"])

            @block.gpsimd
            def _(e):
                run(e, q["pool"])

            @block.sync
            def _(e):
                run(e, q["sp"])


def build_program(NPRE, NMAIN, TB, RING=None, DEBUG=False, STOP=None):
    if RING is None:
        RING = 2 if TB >= 4 else 3
    TBM = TB
    T = 128 * TB
    TM = T
    assert NPRE % T == 0 and NMAIN % T == 0 and NMAIN >= 512 and NPRE >= 512
    nc = bass.Bass("TRN2", target_bir_lowering=False)
    em = Em(nc)

    def din(name, shape):
        return em.wrap(nc.dram_tensor(name, list(shape), F32, kind="ExternalInput"), name)

    def dout(name, shape):
        return em.wrap(nc.dram_tensor(name, list(shape), F32, kind="ExternalOutput"), name)

    xpre = din("xpre", [NPRE, D])
    xmain = din("xmain", [NMAIN, D])
    xsamp = din("xsamp", [128, D])
    flag_d = din("flag", [128, 1])
    cache_k = din("cache_k", [512, D])
    cache_v = din("cache_v", [512, D])
    state_conv = din("state_conv", [3, CONV_DIM])
    state_ssd = din("state_ssd", [D_INNER, NST])
    w_in = din("w_in", [D, D_IN_PROJ])
    conv_w = din("conv_w", [4, CONV_DIM])
    conv_b = din("conv_b", [1, CONV_DIM])
    dt_bias = din("dt_bias", [NH])
    a_log = din("a_log", [NH])
    d_skip = din("d_skip", [NH])
    ssd_norm_w = din("ssd_norm_w", [32, 128])
    rel_table = din("rel_table", [AH, 257])
    w_ssd_out = din("w_ssd_out", [D_INNER, D])
    w_att_out = din("w_att_out", [D, D])
    w_out = din("w_out", [D, D])
    ln1_g = din("ln1_g", [D])
    ln1_b = din("ln1_b", [D])
    w_up = din("w_up", [D, D_FF])
    w_down = din("w_down", [D_FF, D])
    ln2_g = din("ln2_g", [D])
    ln2_b = din("ln2_b", [D])

    y_main = dout("y_main", [NMAIN, D])
    y_samp = dout("y_samp", [128, D])
    k_main = dout("k_main", [512, D])
    v_main = dout("v_main", [512, D])
    conv_main = dout("conv_main", [3, CONV_DIM])
    ssd_main = dout("ssd_main", [D_INNER, NST])
    k_samp = dout("k_samp", [128, D])
    v_samp = dout("v_samp", [128, D])
    conv_samp = dout("conv_samp", [3, CONV_DIM])
    ssd_samp = dout("ssd_samp", [D_INNER, NST])
    ext_d = em.dram("ext_d", [AH, 512], F32)
    USE_SCRATCH = False
    WB = {}
    if USE_SCRATCH:
        WB = {
            "w_in": em.dram("w_in_b", [D, D_IN_PROJ], BF16),
            "w_ssd_out": em.dram("w_ssd_out_b", [D_INNER, D], BF16),
            "w_att_out": em.dram("w_att_out_b", [D, D], BF16),
            "w_out": em.dram("w_out_b", [D, D], BF16),
            "w_up": em.dram("w_up_b", [D, D_FF], BF16),
            "w_down": em.dram("w_down_b", [D_FF, D], BF16),
        }
    WSRC = {"w_in": w_in, "w_ssd_out": w_ssd_out, "w_att_out": w_att_out, "w_out": w_out, "w_up": w_up, "w_down": w_down}

    def convert_weight(name, nchunks):
        src = WSRC[name]
        dst = WB[name]
        rows = src.t.shape[0]
        step = rows // nchunks
        for i in range(nchunks):
            em.dma("pool", dst[i * step:(i + 1) * step, :], src[i * step:(i + 1) * step, :], reads=[src], writes=[dst])
    if DEBUG:
        dbg = {
            "yatt": em.wrap(nc.dram_tensor("dbg_yatt", [128, 16, 128 * TB], BF16, kind="ExternalOutput"), "dbg_yatt"),
            "yssd": em.wrap(nc.dram_tensor("dbg_yssd", [128, 32, 128 * TB], BF16, kind="ExternalOutput"), "dbg_yssd"),
            "merged": em.wrap(nc.dram_tensor("dbg_merged", [128, 16, 128 * TB], BF16, kind="ExternalOutput"), "dbg_merged"),
            "h": em.wrap(nc.dram_tensor("dbg_h", [128, TB, D], F32, kind="ExternalOutput"), "dbg_h"),
            "xT": em.wrap(nc.dram_tensor("dbg_xT", [128, 16, 128 * TB], BF16, kind="ExternalOutput"), "dbg_xT"),
        }

    def dump(key, buf, ti):
        if DEBUG and ti == 0:
            if len(buf.t.shape) == 3:
                em.dma("sp", dbg[key][:, :, :], buf[:, :, :], reads=[buf], writes=[dbg[key]])

    ident_bf = em.sb("ident_bf", [128, 128], BF16)
    ident_f = em.sb("ident_f", [128, 128], F32)
    ones_bf = em.sb("ones_bf", [128, 128], BF16)
    triU = em.sb("triU", [128, 128], F32)
    triLs = em.sb("triLs", [128, 128], F32)
    convp = em.sb("convp", [128, 5, 48], F32)
    normw = em.sb("normw", [128, 32], F32)
    dtb = em.sb("dtb", [128, NH], F32)
    Aneg = em.sb("Aneg", [128, NH], F32)
    dsk = em.sb("dsk", [128, NH], F32)
    EBc = em.sb("EBc", [128, AH], F32)
    wdt = em.sb("wdt", [128, 16, NH], BF16)
    flag = em.sb("flag_sb", [128, 1], F32)
    negbig = em.sb("negbig", [128, 1], F32)
    zero1 = em.sb("zero1", [128, 1], F32)
    epsc = em.sb("epsc", [128, 2], F32)
    ctail = em.sb("ctail", [128, 3, 48], F32)
    S = em.sb("S", [128, D_INNER], F32)
    KTh = em.sb("KTh", [128, AH, 512], BF16)
    Vh = em.sb("Vh", [128, 4, D], BF16)
    xtok = em.sb("xtok", [128, D], BF16)
    ring = [em.sb("ring%d" % i, [128, 16, 512], BF16) for i in range(RING)]
    ring_i = [0]

    PF = [em.ps("pf%d" % i, [128, 512], F32) for i in range(6)]
    PB = [em.ps("pb%d" % i, [128, 1024], BF16) for i in range(2)]
    pb_i = [0]

    A0 = 0
    A1 = 32 * T
    A2 = 64 * T
    A3 = 128 * T
    ARENA = A3 + (45 * 1024 + 3680 if TB >= 4 else 64 * 1024)
    arena = nc.alloc_sbuf_tensor("arena", [128, ARENA], U8)

    VIEWS = {}
    em.views = VIEWS

    def view(name, off, shape, dt):
        esz = 4 if dt == F32 else 2
        n = 1
        for s in shape[1:]:
            n *= s
        nb = n * esz
        assert off % 4 == 0 and off + nb <= ARENA, (name, off, nb, ARENA)
        ap = arena[:, off:off + nb].bitcast(dt)
        if len(shape) == 3:
            ap = ap.rearrange("p (a b) -> p a b", a=shape[1])
        elif len(shape) == 4:
            ap = ap.rearrange("p (a b c) -> p a b c", a=shape[1], b=shape[2])
        VIEWS[name] = (off, tuple(shape), "f32" if dt == F32 else "bf16")
        return em.wrap(ap, name), off + ((nb + 31) // 32) * 32

    xT, _ = view("xT", A0, [128, 16, T], BF16)
    y_attT, _ = view("y_attT", A1, [128, 16, T], BF16)
    y_ssdT, _ = view("y_ssdT", A2, [128, 32, T], BF16)

    o = A2
    EB0, o = view("EB0", o, [128, AH, 128], F32)
    EB3, o = view("EB3", o, [128, AH, 128], F32)
    EB4, o = view("EB4", o, [128, AH, 128], F32)
    QT, o = view("QT", o, [128, 4, T], BF16)
    KTc, o = view("KTc", o, [128, 4, T], BF16)
    Vc, o = view("Vc", o, [128, TB, 512], BF16)
    Pexp = []
    PTb = []
    for i in range(2):
        b_, o = view("Pexp%d" % i, o, [128, 4, 128], F32)
        Pexp.append(b_)
    for i in range(2):
        b_, o = view("PT%d" % i, o, [128, 5, 4, 128], BF16)
        PTb.append(b_)
    rden, o = view("rden", o, [128, 512], F32)
    kvst = [em.sb("kvst%d" % i, [128, 512], F32) for i in range(2)]
    ATT_END = o
    o = A3
    cst, o = view("cst", o, [128, T + 4], F32)
    cacc, o = view("cacc", o, [128, T], F32)
    if TB >= 4:
        ARENA_EXTRA = 0
    cst2, o = view("cst2", o, [128, T + 4], F32)
    cacc2, o = view("cacc2", o, [128, T], F32)
    xbcT = []
    for i in range(6):
        b_, o = view("xbcT%d" % i, o, [128, T], BF16)
        xbcT.append(b_)
    xs_tok, o = view("xs_tok", o, [128, TB, 512], BF16)
    B_tok, o = view("B_tok", o, [128, TB, 128], BF16)
    dt_t, o = view("dt_t", o, [128, TB, NH], F32)
    dt_a, o = view("dt_a", o, [128, TB, NH], F32)
    dt_v, o = view("dt_v", o, [128, TB, NH], F32)
    dtA, o = view("dtA", o, [128, TB, NH], F32)
    exb, o = view("exb", o, [128, TB, 192], F32)
    wd2, o = view("wd2", o, [128, TB, 2, NH], F32)
    Xg, o = view("Xg", o, [128, 8, 128], F32)
    Dm, o = view("Dm", o, [128, 8, 128], F32)
    cbm, o = view("cbm", o, [128, 128], F32)
    MT, o = view("MT", o, [128, 8, 128], BF16)
    xwd, o = view("xwd", o, [128, 2, 8, 64], BF16)
    Sbf, o = view("Sbf", o, [128, 512], BF16)
    y1, o = view("y1", o, [128, 512], F32)
    y2, o = view("y2", o, [128, 512], F32)
    y3, o = view("y3", o, [128, 512], F32)
    ynb, o = view("ynb", o, [128, 512], BF16)
    st8, o = view("st8", o, [128, 8], F32)
    SSD_END = o
    o = A3
    mergedT, o = view("mergedT", o, [128, 16, T], BF16)
    MERGED_END = o
    sgs, o = view("sgs", o, [128, T], F32)
    sga, o = view("sga", o, [128, T], F32)
    m1, o = view("m1", o, [128, 4, T], F32)
    m2, o = view("m2", o, [128, 4, T], F32)
    GATE_END = o
    hT, _ = view("hT", A0, [128, 16, T], BF16)
    hbuf, _ = view("hbuf", A2, [128, TB, D], F32)
    o = MERGED_END
    xres, o = view("xres", o, [128, D], F32)
    hbf, o = view("hbf", o, [128, D], BF16)
    bst, o = view("bst", o, [128, 4, 6], F32)
    mv, o = view("mv", o, [128, 8], F32)
    if TB >= 4:
        lng, _ = view("lng", A1, [128, D], F32)
        lnb, _ = view("lnb", A1 + 8192, [128, D], F32)
        LN_END = o
    else:
        lng, o = view("lng", o, [128, D], F32)
        lnb, o = view("lnb", o, [128, D], F32)
        LN_END = o
    NFH = 32
    o = LN_END if TB < 4 else A3
    uT, o = view("uT", o, [128, NFH, T], BF16)
    rtmp, o = view("rtmp", o, [128, T], F32)
    if TB >= 4:
        bst2, o = view("bst2", o, [128, 4, 6], F32)
        mv2, o = view("mv2", o, [128, 8], F32)
    else:
        bst2, mv2 = bst, mv
    MLP_END = o
    assert max(ATT_END, SSD_END, GATE_END, LN_END, MLP_END) <= ARENA, (ATT_END, SSD_END, GATE_END, LN_END, MLP_END, ARENA)

    cfg = {"TB": TBM, "T": TM, "LT": TM}
    em.sbuf_left = nc.sbuf_bytes_remaining

    cst_a, cacc_a = cst, cacc

    def next_ring():
        r = ring[ring_i[0] % RING]
        ring_i[0] += 1
        return r

    def next_pb():
        p = PB[pb_i[0] % 2]
        pb_i[0] += 1
        return p

    def wslab(wd, r0, c0, ncols, dst=None, dcol=0, nk=16):
        if dst is None:
            dst = next_ring()
        if USE_SCRATCH:
            wd = WB[wd.name]
        em.dma("pool", dst[:, 0:nk, dcol:dcol + ncols],
               wd[r0:r0 + nk * 128, c0:c0 + ncols].rearrange("(k p) c -> p k c", p=128),
               reads=[wd], writes=[dst])
        return dst

    def mm_group(out_ap, out_buf, pairs, reads):
        n = len(pairs)
        for i, (l, r) in enumerate(pairs):
            em.op("pe", lambda e, l=l, r=r, i=i: e.matmul(out_ap, lhsT=l, rhs=r, start=(i == 0), stop=(i == n - 1)),
                  reads=reads if i == 0 else (), writes=[out_buf], sig=(i == n - 1))

    def tr(out_ap, out_buf, in_ap, in_buf, idn):
        em.op("pe", lambda e: e.transpose(out_ap, in_ap, idn), reads=[in_buf, ident_bf, ident_f], writes=[out_buf])

    def act(out_ap, in_ap, func, reads, writes, **kw):
        em.op("act", lambda e: e.activation(out_ap, in_ap, func, **kw), reads=reads, writes=writes)

    def tt(eng, out_ap, a, b, op, reads, writes):
        em.op(eng, lambda e: e.tensor_tensor(out=out_ap, in0=a, in1=b, op=op), reads=reads, writes=writes)

    def ts(eng, out_ap, a, s1, s2, op0, op1, reads, writes):
        if op1 is None:
            em.op(eng, lambda e: e.tensor_scalar(out=out_ap, in0=a, scalar1=s1, scalar2=None, op0=op0), reads=reads, writes=writes)
        else:
            em.op(eng, lambda e: e.tensor_scalar(out=out_ap, in0=a, scalar1=s1, scalar2=s2, op0=op0, op1=op1), reads=reads, writes=writes)

    def stt(eng, out_ap, a, s, b, op0, op1, reads, writes):
        em.op(eng, lambda e: e.scalar_tensor_tensor(out=out_ap, in0=a, scalar=s, in1=b, op0=op0, op1=op1), reads=reads, writes=writes)

    def cp(eng, out_ap, in_ap, reads, writes):
        if eng == "act":
            act(out_ap, in_ap, AF.Copy, reads, writes)
        else:
            em.op(eng, lambda e: e.tensor_copy(out_ap, in_ap), reads=reads, writes=writes)

    def mset(eng, ap, val, buf):
        em.op(eng, lambda e: e.memset(ap, val), writes=[buf])

    def bc3(ap2, n):
        return ap2.unsqueeze(2).to_broadcast([ap2.shape[0], ap2.shape[1], n])

    def bcm(ap2, n):
        return ap2.unsqueeze(1).to_broadcast([ap2.shape[0], n, ap2.shape[1]])

    def setup():
        for t_, v in ((ident_bf, 1.0), (ident_f, 1.0), (triU, 1.0), (triLs, 1.0)):
            mset("pool", t_[:, :], v, t_)
        mset("pool", ones_bf[:, :], 1.0, ones_bf)
        mset("pool", zero1[:, :], 0.0, zero1)
        mset("pool", epsc[:, 0:1], RMS_EPS, epsc)
        mset("pool", epsc[:, 1:2], LN_EPS, epsc)
        sel = lambda t_, pat, op, base, cm: em.op(
            "pool", lambda e: e.affine_select(t_[:, :], t_[:, :], pattern=pat, compare_op=op, fill=0.0, base=base, channel_multiplier=cm),
            reads=[t_], writes=[t_])
        sel(ident_bf, [[-1, 128]], ALU.is_equal, 0, 1)
        sel(ident_f, [[-1, 128]], ALU.is_equal, 0, 1)
        sel(triU, [[1, 128]], ALU.is_ge, 0, -1)
        sel(triLs, [[-1, 128]], ALU.is_gt, 0, 1)
        mset("pool", Jm[:, :], 1.0, Jm)
        sel(Jm, [[1, 128]], ALU.is_equal, -127, 1)
        em.dma("sp", flag[:, :], flag_d[:, :], reads=[flag_d], writes=[flag])
        ts("dve", negbig[:, :], flag[:, :], -1.0, 1e30, ALU.add, ALU.mult, [flag], [negbig])
        em.dma("sp", dtb[:, :], dt_bias.t.ap().partition_broadcast(128), reads=[dt_bias], writes=[dtb])
        em.dma("sp", Aneg[:, :], a_log.t.ap().partition_broadcast(128), reads=[a_log], writes=[Aneg])
        em.dma("sp", dsk[:, :], d_skip.t.ap().partition_broadcast(128), reads=[d_skip], writes=[dsk])
        act(Aneg[:, :], Aneg[:, :], AF.Exp, [Aneg], [Aneg])
        ts("dve", Aneg[:, :], Aneg[:, :], -1.0, None, ALU.mult, None, [Aneg], [Aneg])
        wslab(w_in, 0, OFF_DT, NH, dst=wdt)
        stage, _ = view("stage", A3, [128, 8, 128], F32)
        for k in range(4):
            em.dma("sp", stage[0:48, k, :], conv_w[k, :].rearrange("(b p) -> b p", p=128), reads=[conv_w], writes=[stage])
        em.dma("sp", stage[0:48, 4, :], conv_b[0, :].rearrange("(b p) -> b p", p=128), reads=[conv_b], writes=[stage])
        em.dma("sp", stage[0:32, 5, :], ssd_norm_w[:, :], reads=[ssd_norm_w], writes=[stage])
        for k in range(5):
            tr(PF[0][:, k * 48:(k + 1) * 48], PF[0], stage[0:48, k, :], stage, ident_f[0:48, 0:48])
        cp("dve", convp[:, :, :], PF[0][:, 0:240].rearrange("p (a b) -> p a b", a=5), [PF[0]], [convp])
        tr(PF[1][:, 0:32], PF[1], stage[0:32, 5, :], stage, ident_f[0:32, 0:32])
        cp("dve", normw[:, :], PF[1][:, 0:32], [PF[1]], [normw])
        tabsb, _ = view("tabsb", A3 + 8192, [AH, 512], F32)
        em.dma("sp", tabsb[0:AH, 0:257], rel_table[:, :], reads=[rel_table], writes=[tabsb])
        cp("dve", tabsb[0:AH, 257:512], tabsb[0:AH, 256:257].to_broadcast([AH, 255]), [tabsb], [tabsb])
        em.dma("sp", ext_d[:, :], tabsb[0:AH, :], reads=[tabsb], writes=[ext_d])
        em.dma("sp", EBc[:, :], rel_table[:, 256].partition_broadcast(128), reads=[rel_table], writes=[EBc], allow_slow_non_contiguous=True)
        act(EBc[:, :], EBc[:, :], AF.Exp, [EBc], [EBc])

    Jm = em.sb("Jm", [128, 128], F32)
    toe, _o2 = view("toe", max(ATT_END, A3), [128, AH, 128], F32)

    def build_EB():
        for EB, c in ((EB4, 1), (EB3, 129)):
            src = AP(ext_d.t, c, [[1, 128], [512, AH], [1, 128]])
            em.dma("sp", toe[:, :, :], src, reads=[ext_d], writes=[toe])
            for i in range(4):
                pf = PF[i]
                mm_group(pf[:, :], pf, [(Jm[:, :], toe[:, 4 * i:4 * i + 4, :].rearrange("p h q -> p (h q)"))], [Jm, toe])
                act(EB[:, 4 * i:4 * i + 4, :].rearrange("p h q -> p (h q)"), pf[:, :], AF.Exp, [pf], [EB])
        cp("dve", EB0[:, :, :], bc3(EBc[:, :], 128), [EBc], [EB0])
        mset("dve", EB4[64:128, :, 0:64], 0.0, EB4)
        mset("dve", EB0[0:64, :, 64:128], 0.0, EB0)

    def load_xT(src, t0):
        TB = cfg["TB"]; T = cfg["T"]
        for b in range(TB):
            em.dma("pool", xtok[:, :], src[t0 + b * 128:t0 + (b + 1) * 128, :], reads=[src], writes=[xtok])
            for hlf in range(2):
                pb = next_pb()
                for k in range(8):
                    kc = hlf * 8 + k
                    tr(pb[:, k * 128:(k + 1) * 128], pb, xtok[:, kc * 128:(kc + 1) * 128], xtok, ident_bf[:, :])
                cp("act" if hlf else "dve", xT[:, hlf * 8:(hlf + 1) * 8, b * 128:(b + 1) * 128],
                   pb[:, :].rearrange("p (k t) -> p k t", k=8), [pb], [xT])

    pf_rr = [0]

    def proj_fm(slab, col0, pf_set=(0, 1)):
        TB = cfg["TB"]; T = cfg["T"]
        pf = PF[pf_set[pf_rr[0] % len(pf_set)]]
        pf_rr[0] += 1
        mm_group(pf[:, 0:T], pf, [(slab[:, kc, col0:col0 + 128], xT[:, kc, 0:T]) for kc in range(16)], [slab, xT])
        return pf

    def proj_tm(slab, b, ncols, pf, col0=0):
        TB = cfg["TB"]; T = cfg["T"]
        mm_group(pf[:, 0:ncols], pf, [(xT[:, kc, b * 128:(b + 1) * 128], slab[:, kc, col0:col0 + ncols]) for kc in range(16)], [slab, xT])
        return pf

    conv_i = [0]

    def conv_head(pf, blk):
        TB = cfg["TB"]; T = cfg["T"]
        st = ((cst_a, cacc_a), (cst2, cacc2))[conv_i[0] % 2]
        conv_i[0] += 1
        cst, cacc = st
        cp("act", cst[:, 0:3], ctail[:, :, blk], [ctail], [cst])
        cp("act", cst[:, 3:3 + T], pf[:, 0:T], [pf], [cst])
        cp("act", ctail[:, :, blk], cst[:, cfg["LT"]:cfg["LT"] + 3], [cst], [ctail])
        act(cacc[:, 0:T], pf[:, 0:T], AF.Identity, [pf, convp], [cacc], scale=convp[:, 3, blk:blk + 1], bias=convp[:, 4, blk:blk + 1])
        return st

    def conv_tail(st, blk, dst):
        TB = cfg["TB"]; T = cfg["T"]
        cst, cacc = st
        for k in (2, 1, 0):
            stt("dve", cacc[:, 0:T], cst[:, k:k + T], convp[:, k, blk:blk + 1], cacc[:, 0:T], ALU.mult, ALU.add, [cst, convp, cacc], [cacc])
        act(dst[:, 0:T], cacc[:, 0:T], AF.Silu, [cacc], [dst])

    def conv_pipeline(items):
        prev = None
        for slab, col0, blk, dst in items:
            pf = proj_fm(slab, col0)
            st = conv_head(pf, blk)
            if prev is not None:
                conv_tail(*prev)
            prev = (st, blk, dst)
        conv_tail(*prev)

    def dt_stage(L):
        TB = cfg["TB"]; T = cfg["T"]
        pf = PF[2]
        for b in range(TB):
            mm_group(pf[:, b * NH:(b + 1) * NH], pf, [(xT[:, kc, b * 128:(b + 1) * 128], wdt[:, kc, :]) for kc in range(16)], [wdt, xT])
        fl = lambda t_: t_[:, 0:TB, :].rearrange("p b h -> p (b h)")
        tt("dve", dt_t[:, 0:TB, :], pf[:, 0:TB * NH].rearrange("p (b h) -> p b h", b=TB), bcm(dtb[:, :], TB), ALU.add, [pf, dtb], [dt_t])
        act(fl(dt_a), fl(dt_t), AF.Abs, [dt_t], [dt_a])
        act(fl(dt_a), fl(dt_a), AF.Exp, [dt_a], [dt_a], scale=-1.0)
        act(fl(dt_a), fl(dt_a), AF.Ln, [dt_a], [dt_a], bias=1.0)
        stt("dve", fl(dt_v), fl(dt_t), 0.0, fl(dt_a), ALU.max, ALU.add, [dt_t, dt_a], [dt_v])
        tt("dve", dtA[:, 0:TB, :], dt_v[:, 0:TB, :], bcm(Aneg[:, :], TB), ALU.mult, [dt_v, Aneg], [dtA])
        for b in range(TB):
            pc = PF[3]
            mm_group(pc[0:L, 0:NH], pc, [(triU[0:L, 0:L], dtA[0:L, b, :])], [triU, dtA])
            mm_group(pc[0:L, NH:2 * NH], pc, [(triLs[0:L, 0:L], dtA[0:L, b, :])], [triLs, dtA])
            mm_group(pc[:, 2 * NH:3 * NH], pc, [(triU[0:L, :], dtA[0:L, b, :]), (triLs[0:L, :], dtA[0:L, b, :])], [triU, triLs, dtA])
            if L < 128:
                mset("dve", exb[:, b, :], 0.0, exb)
                act(exb[0:L, b, 0:128], pc[0:L, 0:128], AF.Exp, [pc], [exb])
                act(exb[:, b, 128:192], pc[:, 128:192], AF.Exp, [pc], [exb])
            else:
                act(exb[:, b, :], pc[:, 0:192], AF.Exp, [pc], [exb])
        tt("dve", wd2[:, 0:TB, 0, :], exb[:, 0:TB, NH:2 * NH], dt_v[:, 0:TB, :], ALU.mult, [exb, dt_v], [wd2])
        cp("act", wd2[:, 0:TB, 1, :], dt_v[:, 0:TB, :], [dt_v], [wd2])

    def ssd_projconv(g, mode, need_c=False):
        main = mode == "main"
        sx = wslab(w_in, 0, OFF_XBC + g * 512, 512)
        sbc = next_ring()
        wslab(w_in, 0, OFF_B + g * 128, 128, dst=sbc, dcol=0)
        if main or need_c:
            wslab(w_in, 0, OFF_C + g * 128, 128, dst=sbc, dcol=128)
        items = [(sx, i * 128, g * 4 + i, xbcT[i]) for i in range(4)] + [(sbc, 0, 32 + g, xbcT[4])]
        if main or need_c:
            items.append((sbc, 128, 40 + g, xbcT[5]))
        conv_pipeline(items)

    def ssd_transposes(g):
        TB = cfg["TB"]
        for b in range(TB):
            pb = next_pb()
            for i in range(4):
                tr(pb[:, i * 128:(i + 1) * 128], pb, xbcT[i][:, b * 128:(b + 1) * 128], xbcT[i], ident_bf[:, :])
            cp("act", xs_tok[:, b, :], pb[:, 0:512], [pb], [xs_tok])
        pb = next_pb()
        for b in range(TB):
            tr(pb[:, b * 128:(b + 1) * 128], pb, xbcT[4][:, b * 128:(b + 1) * 128], xbcT[4], ident_bf[:, :])
        cp("act", B_tok[:, 0:TB, :], pb[:, 0:TB * 128].rearrange("p (b n) -> p b n", b=TB), [pb], [B_tok])

    def ssd_bloop(g, mode, L):
        TB = cfg["TB"]
        main = mode == "main"
        if main:
            sz_slab = wslab(w_in, 0, g * 512, 512)
        hs = slice(g * 8, g * 8 + 8)
        Sg = S[:, g * 512:(g + 1) * 512]
        sil = (cst, cacc, cst2, cacc2)
        PYD = (PF[4], PF[0])
        PYO = (PF[5], PF[1])
        if main and L < 128:
            mset("dve", ynb[L:128, :], 0.0, ynb)

        def xs3f(b):
            return xs_tok[0:L, b, :].rearrange("p (h q) -> p h q", h=8)

        def front(b):
            if main:
                cp("act", Sbf[:, :], Sg, [S], [Sbf])
            if main:
                tt("dve", xwd[0:L, :, :, :], xs3f(b).unsqueeze(1).to_broadcast([L, 2, 8, 64]),
                   wd2[0:L, b, :, hs].unsqueeze(3).to_broadcast([L, 2, 8, 64]), ALU.mult, [xs_tok, wd2], [xwd])
            else:
                tt("dve", xwd[0:L, 0, :, :], xs3f(b), bc3(wd2[0:L, b, 0, hs], 64), ALU.mult, [xs_tok, wd2], [xwd])
            if main:
                pyo = PYO[b % 2]
                mm_group(pyo[0:L, :], pyo, [(xbcT[5][:, b * 128:b * 128 + L], Sbf[:, :])], [xbcT[5], Sbf])
            psu = PF[3]
            mm_group(psu[:, :], psu, [(B_tok[0:L, b, :], xwd[0:L, 0, :, :].rearrange("p h q -> p (h q)"))], [B_tok, xwd])
            tt("dve", Sg.rearrange("p (h q) -> p h q", h=8), Sg.rearrange("p (h q) -> p h q", h=8),
               bc3(exb[:, b, 2 * NH + g * 8:2 * NH + g * 8 + 8], 64), ALU.mult, [S, exb], [S])
            tt("dve", Sg, Sg, psu[:, :], ALU.add, [S, psu], [S])

        def taila1(b):
            tt("dve", Xg[0:L, :, 0:L], bc3(dtA[0:L, b, hs], L), bcm(triU[0:L, 0:L], 8), ALU.mult, [dtA, triU], [Xg])
            hu = 512 // L
            nu = 8 // hu
            pcb = PF[3]
            mm_group(pcb[0:L, 0:L], pcb, [(xbcT[4][:, b * 128:b * 128 + L], xbcT[5][:, b * 128:b * 128 + L])], [xbcT[4], xbcT[5]])
            for u in range(nu):
                pu = PF[2]
                mm_group(pu[0:L, 0:512], pu, [(triLs[0:L, 0:L], Xg[0:L, u * hu:(u + 1) * hu, 0:L])], [triLs, Xg])
                act(Dm[0:L, u * hu:(u + 1) * hu, 0:L], pu[0:L, 0:512].rearrange("p (h i) -> p h i", h=hu), AF.Exp, [pu], [Dm])

        def zs(b):
            sb_ = sil[b % 4]
            pz = proj_tm(sz_slab, b, 512, PF[2 + (b % 2)])
            act(sb_[0:L, 0:512], pz[0:L, :], AF.Silu, [pz], [sb_])

        def taila2(b):
            pcb = PF[3]
            tt("dve", cbm[0:L, 0:L], pcb[0:L, 0:L], triU[0:L, 0:L], ALU.mult, [pcb, triU], [cbm])
            tt("dve", MT[0:L, :, 0:L], Dm[0:L, :, 0:L], bcm(cbm[0:L, 0:L], 8), ALU.mult, [Dm, cbm], [MT])
            pyd = PYD[b % 2]
            for h in range(8):
                mm_group(pyd[0:L, h * 64:(h + 1) * 64], pyd, [(MT[0:L, h, 0:L], xwd[0:L, 1, h, :])], [MT, xwd])

        def tailb(b):
            pyo = PYO[b % 2]
            pyd = PYD[b % 2]
            sb_ = sil[b % 4]
            tt("dve", y1[0:L, :].rearrange("p (h q) -> p h q", h=8), pyo[0:L, :].rearrange("p (h q) -> p h q", h=8),
               bc3(exb[0:L, b, hs], 64), ALU.mult, [pyo, exb], [y1])
            tt("dve", y3[0:L, :].rearrange("p (h q) -> p h q", h=8), xs3f(b), bc3(dsk[0:L, hs], 64), ALU.mult, [xs_tok, dsk], [y3])
            tt("dve", y1[0:L, :], y1[0:L, :], y3[0:L, :], ALU.add, [y1, y3], [y1])
            tt("dve", y2[0:L, :], pyd[0:L, :], y1[0:L, :], ALU.add, [pyd, y1], [y2])
            tt("dve", y2[0:L, :], y2[0:L, :], sb_[0:L, 0:512], ALU.mult, [y2, sb_], [y2])
            act(y3[0:L, :], y2[0:L, :], AF.Square, [y2], [y3, st8], accum_out=st8[0:L, 0:1])
            act(st8[0:L, 2:3], st8[0:L, 0:1], AF.Ln, [st8, epsc], [st8], scale=1.0 / 512, bias=epsc[0:L, 0:1])
            act(st8[0:L, 3:4], st8[0:L, 2:3], AF.Exp, [st8], [st8], scale=-0.5)
            act(ynb[0:L, :], y2[0:L, :], AF.Copy, [y2, st8], [ynb], scale=st8[0:L, 3:4])
            pb2 = next_pb()
            for i in range(4):
                tr(pb2[:, i * 128:(i + 1) * 128], pb2, ynb[:, i * 128:(i + 1) * 128], ynb, ident_bf[:, :])
            return pb2

        def tailb2(b, pb2):
            tt("dve", y_ssdT[:, g * 4:g * 4 + 4, b * 128:(b + 1) * 128], pb2[:, 0:512].rearrange("p (a t) -> p a t", a=4),
               bc3(normw[:, g * 4:g * 4 + 4], 128), ALU.mult, [pb2, normw], [y_ssdT])

        if not main:
            for b in range(TB):
                front(b)
            return
        assert TB <= 4
        for b in range(TB):
            zs(b)
        front(0)
        taila1(0)
        taila2(0)
        for b in range(TB):
            if b + 1 < TB:
                front(b + 1)
                taila1(b + 1)
            pb2 = tailb(b)
            if b + 1 < TB:
                taila2(b + 1)
            tailb2(b, pb2)

    def ssd_group(g, mode, L, need_c=False):
        ssd_projconv(g, mode, need_c)
        ssd_transposes(g)
        ssd_bloop(g, mode, L)

    def attn_kv_proj(hg, want_q, kout=None, vout=None, orow=0):
        TB = cfg["TB"]; T = cfg["T"]
        sk = wslab(w_in, 0, OFF_K + hg * 512, 512)
        for i in range(4):
            pf = proj_fm(sk, i * 128, pf_set=(0, 1))
            cp("dve", KTc[:, i, 0:T], pf[:, 0:T], [pf], [KTc])
        if kout is not None:
            for b in range(TB):
                pf = proj_tm(sk, b, 512, PF[2 + (b % 2)])
                st_ = kvst[b % 2]
                cp("act", st_[:, :], pf[:, :], [pf], [st_])
                em.dma("sp", kout[orow + b * 128:orow + (b + 1) * 128, hg * 512:(hg + 1) * 512], st_[:, :], reads=[st_], writes=[kout])
        sv = wslab(w_in, 0, OFF_V + hg * 512, 512)
        for b in range(TB):
            pf = proj_tm(sv, b, 512, PF[2 + (b % 2)])
            if vout is not None:
                st_ = kvst[b % 2]
                cp("act", st_[:, :], pf[:, :], [pf], [st_])
                cp("dve", Vc[:, b, :], st_[:, :], [st_], [Vc])
                em.dma("sp", vout[orow + b * 128:orow + (b + 1) * 128, hg * 512:(hg + 1) * 512], st_[:, :], reads=[st_], writes=[vout])
            else:
                cp("dve", Vc[:, b, :], pf[:, :], [pf], [Vc])
        if want_q:
            sq = wslab(w_in, 0, OFF_Q + hg * 512, 512)
            for i in range(4):
                pf = proj_fm(sq, i * 128, pf_set=(0, 1))
                act(QT[:, i, 0:T], pf[:, 0:T], AF.Copy, [pf], [QT], scale=QSCALE)

    def hist_update(hg):
        TB = cfg["TB"]; T = cfg["T"]
        hsl = slice(hg * 4, hg * 4 + 4)
        vsl = slice(hg * 512, (hg + 1) * 512)
        if TB < 4:
            keep = 4 - TB
            for j in range(keep):
                cp("act", KTh[:, hsl, j * 128:(j + 1) * 128], KTh[:, hsl, (j + TB) * 128:(j + TB + 1) * 128], [KTh], [KTh])
                cp("act", Vh[:, j, vsl], Vh[:, j + TB, vsl], [Vh], [Vh])
            cp("act", KTh[:, hsl, keep * 128:512], KTc[:, :, 0:T], [KTc], [KTh])
            cp("act", Vh[:, keep:4, vsl], Vc[:, 0:TB, :], [Vc], [Vh])
        else:
            cp("act", KTh[:, hsl, :], KTc[:, :, T - 512:T], [KTc], [KTh])
            cp("act", Vh[:, :, vsl], Vc[:, TB - 4:TB, :], [Vc], [Vh])

    def attention(hist_flagged, kout=None, vout=None, orow=0, slide=True, stop=None):
        TB = cfg["TB"]; T = cfg["T"]
        pi = [0]
        for hg in range(4):
            if stop == 631:
                attn_kv_proj(hg, True, None, None, orow)
                return True
            if stop == 632:
                attn_kv_proj(hg, False, kout, vout, orow)
                return True
            attn_kv_proj(hg, True, kout, vout, orow)
            if stop == 63:
                return True
            for b in range(TB):
                PT = PTb[pi[0] % 2]
                for kb in range(5):
                    ob = b + kb - 4
                    psc = PF[2 + (kb % 2)]
                    for hl in range(4):
                        if ob < 0:
                            kt = KTh[:, hg * 4 + hl, (ob + 4) * 128:(ob + 5) * 128]
                            kbuf = KTh
                        else:
                            kt = KTc[:, hl, ob * 128:(ob + 1) * 128]
                            kbuf = KTc
                        mm_group(psc[:, hl * 128:(hl + 1) * 128], psc, [(kt, QT[:, hl, b * 128:(b + 1) * 128])], [kbuf, QT])
                    pe_ = Pexp[kb % 2]
                    bias = negbig[:, 0:1] if (hist_flagged is not None and hist_flagged + ob * 128 < 0) else zero1[:, 0:1]
                    act(pe_[:, :, :].rearrange("p h q -> p (h q)"), psc[:, :], AF.Exp, [psc, negbig, zero1], [pe_], bias=bias)
                    EBk = (EB0, None, None, EB3, EB4)[kb]
                    if EBk is None:
                        ebap = bc3(EBc[:, hg * 4:hg * 4 + 4], 128)
                        ebuf = EBc
                    else:
                        ebap = EBk[:, hg * 4:hg * 4 + 4, :]
                        ebuf = EBk
                    tt("dve", PT[:, kb, :, :], pe_[:, :, :], ebap, ALU.mult, [pe_, ebuf], [PT])
                pden = PF[4]
                mm_group(pden[:, :], pden, [(ones_bf[:, :], PT[:, kb, :, :].rearrange("p h q -> p (h q)")) for kb in range(5)], [ones_bf, PT])
                po = PF[5]
                for hl in range(4):
                    prs = []
                    for kb in range(5):
                        ob = b + kb - 4
                        if ob < 0:
                            vv = Vh[:, ob + 4, (hg * 4 + hl) * 128:(hg * 4 + hl + 1) * 128]
                        else:
                            vv = Vc[:, ob, hl * 128:(hl + 1) * 128]
                        prs.append((vv, PT[:, kb, hl, :]))
                    mm_group(po[:, hl * 128:(hl + 1) * 128], po, prs, [Vh, Vc, PT])
                act(rden[:, :], pden[:, :], AF.Ln, [pden], [rden])
                act(rden[:, :], rden[:, :], AF.Exp, [rden], [rden], scale=-1.0)
                tt("dve", y_attT[:, hg * 4:hg * 4 + 4, b * 128:(b + 1) * 128], po[:, :].rearrange("p (h q) -> p h q", h=4),
                   rden[:, :].rearrange("p (h q) -> p h q", h=4), ALU.mult, [po, rden], [y_attT])
                pi[0] += 1
                if stop == 64:
                    return True
            if slide:
                hist_update(hg)
            if stop == 65:
                return True

    def gates_outproj():
        TB = cfg["TB"]; T = cfg["T"]
        for cg in range(4):
            c0 = cg * 512
            s_so0 = wslab(w_ssd_out, 0, c0, 512)
            for obl in range(4):
                cs = slice(obl * 128, (obl + 1) * 128)
                pa = PF[obl % 2]
                mm_group(pa[:, 0:T], pa, [(s_so0[:, kc, cs], y_ssdT[:, kc, 0:T]) for kc in range(16)], [s_so0, y_ssdT])
                cp("act", m1[:, obl, 0:T], pa[:, 0:T], [pa], [m1])
            s_so1 = wslab(w_ssd_out, 2048, c0, 512)
            for obl in range(4):
                cs = slice(obl * 128, (obl + 1) * 128)
                pa = PF[4 + obl % 2]
                mm_group(pa[:, 0:T], pa, [(s_so1[:, kc, cs], y_ssdT[:, 16 + kc, 0:T]) for kc in range(16)], [s_so1, y_ssdT])
                tt("dve", m1[:, obl, 0:T], m1[:, obl, 0:T], pa[:, 0:T], ALU.add, [m1, pa], [m1])
            s_ao = wslab(w_att_out, 0, c0, 512)
            for obl in range(4):
                cs = slice(obl * 128, (obl + 1) * 128)
                pbk = PF[2 + obl % 2]
                mm_group(pbk[:, 0:T], pbk, [(s_ao[:, kc, cs], y_attT[:, kc, 0:T]) for kc in range(16)], [s_ao, y_attT])
                cp("dve", m2[:, obl, 0:T], pbk[:, 0:T], [pbk], [m2])
            s_gs = wslab(w_in, 0, OFF_GS + c0, 512)
            for obl in range(4):
                cs = slice(obl * 128, (obl + 1) * 128)
                pg = PF[4 + obl % 2]
                mm_group(pg[:, 0:T], pg, [(s_gs[:, kc, cs], xT[:, kc, 0:T]) for kc in range(16)], [s_gs, xT])
                act(sgs[:, 0:T], pg[:, 0:T], AF.Sigmoid, [pg], [sgs])
                tt("dve", m1[:, obl, 0:T], m1[:, obl, 0:T], sgs[:, 0:T], ALU.mult, [m1, sgs], [m1])
            s_ga = wslab(w_in, 0, OFF_GA + c0, 512)
            for obl in range(4):
                ob = cg * 4 + obl
                cs = slice(obl * 128, (obl + 1) * 128)
                pg2 = PF[obl % 2]
                mm_group(pg2[:, 0:T], pg2, [(s_ga[:, kc, cs], xT[:, kc, 0:T]) for kc in range(16)], [s_ga, xT])
                act(sga[:, 0:T], pg2[:, 0:T], AF.Sigmoid, [pg2], [sga])
                tt("dve", m2[:, obl, 0:T], m2[:, obl, 0:T], sga[:, 0:T], ALU.mult, [m2, sga], [m2])
                tt("dve", mergedT[:, ob, 0:T], m1[:, obl, 0:T], m2[:, obl, 0:T], ALU.add, [m1, m2], [mergedT])

    def layer_norm_rows(hrow, bst_, mv_):
        for c in range(4):
            em.op("dve", lambda e, c=c: e.bn_stats(bst_[:, c, :], hrow[:, c * 512:(c + 1) * 512]), reads=[hbuf], writes=[bst_])
        em.op("dve", lambda e: e.bn_aggr(mv_[:, 0:2], bst_[:, :, :]), reads=[bst_], writes=[mv_])
        act(mv_[:, 3:4], mv_[:, 1:2], AF.Ln, [mv_, epsc], [mv_], bias=epsc[:, 1:2])
        act(mv_[:, 4:5], mv_[:, 3:4], AF.Exp, [mv_], [mv_], scale=-0.5)
        ts("dve", hrow, hrow, mv_[:, 0:1], mv_[:, 4:5], ALU.subtract, ALU.mult, [hbuf, mv_], [hbuf])
        tt("dve", hrow, hrow, lng[:, :], ALU.mult, [hbuf, lng], [hbuf])
        tt("dve", hrow, hrow, lnb[:, :], ALU.add, [hbuf, lnb], [hbuf])

    def wout_ln1(src, t0):
        TB = cfg["TB"]; T = cfg["T"]
        em.dma("sp", lng[:, :], ln1_g.t.ap().partition_broadcast(128), reads=[ln1_g], writes=[lng])
        em.dma("sp", lnb[:, :], ln1_b.t.ap().partition_broadcast(128), reads=[ln1_b], writes=[lnb])
        for cg in range(4):
            s_wo = wslab(w_out, 0, cg * 512, 512)
            for b in range(TB):
                pf = PF[(cg * TB + b) % 4]
                mm_group(pf[:, :], pf, [(mergedT[:, kc, b * 128:(b + 1) * 128], s_wo[:, kc, :]) for kc in range(16)], [mergedT, s_wo])
                cp("act", hbuf[:, b, cg * 512:(cg + 1) * 512], pf[:, :], [pf], [hbuf])
        for b in range(TB):
            em.dma("sp", xres[:, :], src[t0 + b * 128:t0 + (b + 1) * 128, :], reads=[src], writes=[xres])
            hrow = hbuf[:, b, :]
            stt("dve", hrow, xres[:, :], ALPHA, hrow, ALU.mult, ALU.add, [xres, hbuf], [hbuf])
            layer_norm_rows(hrow, bst, mv)
            cp("act", hbf[:, :], hrow, [hbuf], [hbf])
            for hlf in range(2):
                pb = next_pb()
                for k in range(8):
                    kc = hlf * 8 + k
                    tr(pb[:, k * 128:(k + 1) * 128], pb, hbf[:, kc * 128:(kc + 1) * 128], hbf, ident_bf[:, :])
                cp("act" if hlf else "dve", hT[:, hlf * 8:(hlf + 1) * 8, b * 128:(b + 1) * 128],
                   pb[:, :].rearrange("p (k t) -> p k t", k=8), [pb], [hT])

    def mlp_ln2(ydst, t0, nrows):
        TB = cfg["TB"]; T = cfg["T"]
        for hf in range(2):
            for s4 in range(8):
                s_up = wslab(w_up, 0, (hf * 8 + s4) * 512, 512)
                for fl_ in range(4):
                    pf = proj_fm_h(s_up, fl_ * 128)
                    act(rtmp[:, 0:T], pf[:, 0:T], AF.Relu, [pf], [rtmp])
                    tt("dve", uT[:, s4 * 4 + fl_, 0:T], rtmp[:, 0:T], rtmp[:, 0:T], ALU.mult, [rtmp], [uT])
            if hf == 1:
                em.dma("sp", lng[:, :], ln2_g.t.ap().partition_broadcast(128), reads=[ln2_g], writes=[lng])
                em.dma("sp", lnb[:, :], ln2_b.t.ap().partition_broadcast(128), reads=[ln2_b], writes=[lnb])
            for cg in range(4):
                for q2 in range(2):
                    s_d = wslab(w_down, (hf * 32 + q2 * 16) * 128, cg * 512, 512)
                    for b in range(TB):
                        pf = PF[2 + ((q2 * TB + b) % 4)]
                        mm_group(pf[:, :], pf, [(uT[:, q2 * 16 + kc, b * 128:(b + 1) * 128], s_d[:, kc, :]) for kc in range(16)], [uT, s_d])
                        hseg = hbuf[:, b, cg * 512:(cg + 1) * 512]
                        if hf == 0 and q2 == 0:
                            stt("dve", hseg, hseg, ALPHA, pf[:, :], ALU.mult, ALU.add, [hbuf, pf], [hbuf])
                        else:
                            tt("dve", hseg, hseg, pf[:, :], ALU.add, [hbuf, pf], [hbuf])
        for b in range(TB):
            hrow = hbuf[:, b, :]
            layer_norm_rows(hrow, bst2, mv2)
            r = min(128, nrows - b * 128)
            if r > 0:
                em.dma("sp", ydst[t0 + b * 128:t0 + b * 128 + r, :], hbuf[0:r, b, :], reads=[hbuf], writes=[ydst])

    def proj_fm_h(slab, col0):
        TB = cfg["TB"]; T = cfg["T"]
        pf = PF[pf_rr[0] % 2]
        pf_rr[0] += 1
        mm_group(pf[:, 0:T], pf, [(slab[:, kc, col0:col0 + 128], hT[:, kc, 0:T]) for kc in range(16)], [slab, hT])
        return pf

    def out_state(dst):
        for c in range(8):
            pf = PF[c % 4]
            for i in range(4):
                blk = c * 4 + i
                tr(pf[:, i * 128:(i + 1) * 128], pf, S[:, blk * 128:(blk + 1) * 128], S, ident_f[:, :])
            st_ = kvst[c % 2]
            cp("act" if c % 2 else "dve", st_[:, :], pf[:, :], [pf], [st_])
            em.dma("sp", dst[c * 512:(c + 1) * 512, :].rearrange("(b p) n -> p b n", p=128), st_[:, :].rearrange("p (b n) -> p b n", b=4),
                   reads=[st_], writes=[dst])

    def out_conv(dst):
        for t in range(3):
            pf = PF[t]
            tr(pf[0:48, 0:128], pf, ctail[:, t, :], ctail, ident_f[:, :])
            st_ = kvst[t % 2]
            cp("dve", st_[0:48, 0:128], pf[0:48, 0:128], [pf], [st_])
            em.dma("sp", dst[t, :].rearrange("(b p) -> b p", p=128), st_[0:48, 0:128], reads=[st_], writes=[dst])

    def finish():
        em.barrier(engines=("pe", "act", "dve", "pool", "sp"))
        em.emit()
        return nc, em

    if USE_SCRATCH:
        convert_weight("w_in", 8)
    setup()
    if STOP == 1:
        return finish()
    mset("dve", ctail[:, :, :], 0.0, ctail)
    mset("dve", S[:, :], 0.0, S)
    mset("pool", KTh[:, :, :], 0.0, KTh)
    mset("pool", Vh[:, :, :], 0.0, Vh)
    em.barrier()

    npre = NPRE // T
    for ti in range(npre):
        load_xT(xpre, ti * T)
        if STOP == 2:
            return finish()
        dt_stage(128)
        if STOP == 3:
            return finish()
        nc_ = (ti == npre - 1)
        ssd_projconv(0, "pre", nc_)
        ssd_transposes(0)
        for g in range(NG):
            if g + 1 < NG:
                ssd_projconv(g + 1, "pre", nc_)
            ssd_bloop(g, "pre", 128)
            if g + 1 < NG:
                ssd_transposes(g + 1)
        if STOP == 5:
            return finish()
        if (npre - ti) * T <= 512:
            em.barrier()
            for hg in range(4):
                attn_kv_proj(hg, False)
                hist_update(hg)
        if ti == 0 and USE_SCRATCH:
            convert_weight("w_up", 4)
            convert_weight("w_down", 4)
            convert_weight("w_ssd_out", 2)
            convert_weight("w_att_out", 1)
            convert_weight("w_out", 1)
        em.barrier()
    if STOP == 6:
        return finish()
    ts("dve", S[:, :], S[:, :], flag[:, 0:1], None, ALU.mult, None, [S, flag], [S])

    nmain = NMAIN // T
    for ti in range(nmain):
        t0 = ti * T
        load_xT(xmain, t0)
        if STOP == 61:
            return finish()
        build_EB()
        if STOP == 62:
            return finish()
        want = (NMAIN - t0) <= 512
        if attention(hist_flagged=t0, kout=k_main if want else None, vout=v_main if want else None,
                     orow=(t0 - (NMAIN - 512)) if want else 0, stop=STOP):
            return finish()
        if STOP == 7:
            return finish()
        dump("yatt", y_attT, ti)
        dump("xT", xT, ti)
        em.barrier()
        dt_stage(128)
        for g in range(NG):
            ssd_group(g, "main", 128)
        if STOP == 8:
            return finish()
        dump("yssd", y_ssdT, ti)
        em.barrier()
        gates_outproj()
        if STOP == 9:
            return finish()
        dump("merged", mergedT, ti)
        em.barrier()
        wout_ln1(xmain, t0)
        if STOP == 10:
            return finish()
        dump("h", hbuf, ti)
        em.barrier()
        mlp_ln2(y_main, t0, T)
        em.barrier()
        if STOP == 11:
            return finish()
    out_state(ssd_main)
    out_conv(conv_main)
    em.barrier()
    if STOP == 12:
        return finish()

    cfg["TB"] = 1
    cfg["T"] = 128
    cfg["LT"] = 64
    stage2, _ = view("stage2", A3, [128, 8, 128], F32)
    for t in range(3):
        em.dma("sp", stage2[0:48, t, :], state_conv[t, :].rearrange("(b p) -> b p", p=128), reads=[state_conv], writes=[stage2])
    for t in range(3):
        tr(PF[0][:, t * 48:(t + 1) * 48], PF[0], stage2[0:48, t, :], stage2, ident_f[0:48, 0:48])
    cp("dve", ctail[:, :, :], PF[0][:, 0:144].rearrange("p (a b) -> p a b", a=3), [PF[0]], [ctail])
    for c in range(8):
        st_ = kvst[c % 2]
        em.dma("sp", st_[:, :].rearrange("p (b n) -> p b n", b=4), state_ssd[c * 512:(c + 1) * 512, :].rearrange("(b p) n -> p b n", p=128),
               reads=[state_ssd], writes=[st_])
        pf = PF[c % 4]
        for i in range(4):
            tr(pf[:, i * 128:(i + 1) * 128], pf, st_[:, i * 128:(i + 1) * 128], st_, ident_f[:, :])
        cp("act" if c % 2 else "dve", S[:, c * 512:(c + 1) * 512], pf[:, :], [pf], [S])
    for j in range(4):
        em.dma("pool", Vh[:, j, :], cache_v[j * 128:(j + 1) * 128, :], reads=[cache_v], writes=[Vh])
    for j in range(4):
        em.dma("pool", xtok[:, :], cache_k[j * 128:(j + 1) * 128, :], reads=[cache_k], writes=[xtok])
        for hlf in range(2):
            pb = next_pb()
            for k in range(8):
                hh = hlf * 8 + k
                tr(pb[:, k * 128:(k + 1) * 128], pb, xtok[:, hh * 128:(hh + 1) * 128], xtok, ident_bf[:, :])
            cp("act" if hlf else "dve", KTh[:, hlf * 8:(hlf + 1) * 8, j * 128:(j + 1) * 128],
               pb[:, :].rearrange("p (k t) -> p k t", k=8), [pb], [KTh])
    em.barrier()
    load_xT(xsamp, 0)
    em.barrier()
    build_EB()
    em.barrier()
    attention(hist_flagged=None, kout=k_samp, vout=v_samp, orow=0, slide=False)
    em.barrier()
    dt_stage(64)
    for g in range(NG):
        ssd_group(g, "main", 64)
    em.barrier()
    gates_outproj()
    em.barrier()
    wout_ln1(xsamp, 0)
    em.barrier()
    mlp_ln2(y_samp, 0, 128)
    em.barrier()
    out_state(ssd_samp)
    out_conv(conv_samp)
    outs = (y_main, y_samp, k_main, v_main, conv_main, ssd_main, k_samp, v_samp, conv_samp, ssd_samp)
    em.barrier(engines=("pe", "act", "dve", "pool", "sp"))
    em.emit()
    return nc, em


TB_DEFAULT = 4
_PROG = {}


def _get_prog(npre, nmain, tb):
    key = (npre, nmain, tb)
    if key not in _PROG:
        _PROG[key] = build_program(npre, nmain, tb)
    return _PROG[key]


def _core_inputs(c, half_len, x_prompt, x_sample, cache_k, cache_v, state_conv, state_ssd, weights):
    seq, half = c // 2, c % 2
    f32 = np.float32
    xm = np.ascontiguousarray(x_prompt[seq, half * half_len:(half + 1) * half_len], dtype=f32)
    xp = np.ascontiguousarray(x_prompt[seq, 0:half_len], dtype=f32) if half else np.zeros((half_len, D), f32)
    xs = np.zeros((128, D), f32)
    xs[:DEC_SEQ] = x_sample[c]
    m = {
        "xpre": xp, "xmain": xm, "xsamp": xs,
        "flag": np.full((128, 1), float(half), f32),
        "cache_k": np.ascontiguousarray(cache_k[0, c].reshape(512, D), dtype=f32),
        "cache_v": np.ascontiguousarray(cache_v[0, c].reshape(512, D), dtype=f32),
        "state_conv": np.ascontiguousarray(state_conv[0, c], dtype=f32),
        "state_ssd": np.ascontiguousarray(state_ssd[0, c].reshape(D_INNER, NST), dtype=f32),
    }
    m.update(weights)
    return m


def _weights(w_in, conv_w, conv_b, dt_bias, a_log, d_skip, ssd_norm_w, rel_table, w_ssd_out, w_att_out,
             w_out, ln1_g, ln1_b, w_up, w_down, ln2_g, ln2_b):
    f = lambda a: np.ascontiguousarray(np.asarray(a), dtype=np.float32)
    return {
        "w_in": f(w_in[0]), "conv_w": f(conv_w[0]), "conv_b": f(conv_b[0]).reshape(1, CONV_DIM),
        "dt_bias": f(dt_bias[0]), "a_log": f(a_log[0]), "d_skip": f(d_skip[0]),
        "ssd_norm_w": f(ssd_norm_w[0]).reshape(32, 128), "rel_table": f(rel_table[0]),
        "w_ssd_out": f(w_ssd_out[0]), "w_att_out": f(w_att_out[0]), "w_out": f(w_out[0]),
        "ln1_g": f(ln1_g[0]), "ln1_b": f(ln1_b[0]), "w_up": f(w_up[0]), "w_down": f(w_down[0]),
        "ln2_g": f(ln2_g[0]), "ln2_b": f(ln2_b[0]),
    }


def kernel(x_prompt, x_sample, cache_k, cache_v, state_conv, state_ssd, w_in, conv_w, conv_b,
           dt_bias, a_log, d_skip, ssd_norm_w, rel_table, w_ssd_out, w_att_out, w_out,
           ln1_g, ln1_b, w_up, w_down, ln2_g, ln2_b):
    x_prompt = np.asarray(x_prompt)
    x_sample = np.asarray(x_sample)
    cache_k = np.asarray(cache_k)
    cache_v = np.asarray(cache_v)
    state_conv = np.asarray(state_conv)
    state_ssd = np.asarray(state_ssd)
    half_len = SEQ // 2
    nc, _ = _get_prog(half_len, half_len, TB_DEFAULT)
    wts = _weights(w_in, conv_w, conv_b, dt_bias, a_log, d_skip, ssd_norm_w, rel_table, w_ssd_out, w_att_out,
                   w_out, ln1_g, ln1_b, w_up, w_down, ln2_g, ln2_b)
    in_maps = [_core_inputs(c, half_len, x_prompt, x_sample, cache_k, cache_v, state_conv, state_ssd, wts)
               for c in range(8)]
    res = run_bass_kernel_spmd(nc, in_maps, core_ids=list(range(8))).results
    f32 = np.float32
    y_prompt = np.empty((BATCH, SEQ, D), f32)
    y_sample = np.empty((DEC_BATCH, DEC_SEQ, D), f32)
    nkp = np.empty((1, BATCH, 512, AH, 128), f32)
    nvp = np.empty((1, BATCH, 512, AH, 128), f32)
    ncp = np.empty((1, BATCH, 3, CONV_DIM), f32)
    nsp = np.empty((1, BATCH, NH, 64, NST), f32)
    nks = np.empty((1, DEC_BATCH, DEC_SEQ, AH, 128), f32)
    nvs = np.empty((1, DEC_BATCH, DEC_SEQ, AH, 128), f32)
    ncs = np.empty((1, DEC_BATCH, 3, CONV_DIM), f32)
    nss = np.empty((1, DEC_BATCH, NH, 64, NST), f32)
    for c in range(8):
        seq, half = c // 2, c % 2
        r = res[c]
        y_prompt[seq, half * half_len:(half + 1) * half_len] = r["y_main"]
        y_sample[c] = r["y_samp"][:DEC_SEQ]
        if half:
            nkp[0, seq] = r["k_main"].reshape(512, AH, 128)
            nvp[0, seq] = r["v_main"].reshape(512, AH, 128)
            ncp[0, seq] = r["conv_main"]
            nsp[0, seq] = r["ssd_main"].reshape(NH, 64, NST)
        nks[0, c] = r["k_samp"][:DEC_SEQ].reshape(DEC_SEQ, AH, 128)
        nvs[0, c] = r["v_samp"][:DEC_SEQ].reshape(DEC_SEQ, AH, 128)
        ncs[0, c] = r["conv_samp"]
        nss[0, c] = r["ssd_samp"].reshape(NH, 64, NST)
    return (y_prompt, y_sample, nkp, nvp, ncp, nsp, nks, nvs, ncs, nss)
```
